# Optimizing a Trainium2 kernel written in Bass

```python
import math
import jax
import jax.numpy as jnp
from jax import lax
import numpy as np

D_MODEL = 1024
BATCH = 2
SEQ = 16384
DEPTH = 2

GRID_W = 64
CTX_LEN = 256
EPS = 1e-6
F32 = jnp.float32

DN_HEADS = 4
DN_DIM = 128
DN_WIDTH = DN_HEADS * DN_DIM
DN_CONV = 3
DN_CHUNK = 64
DN_COLS = 4 * DN_WIDTH + 4 * DN_HEADS
SWA_HEADS = 4
SWA_KV_HEADS = 2
SWA_DIM = 128
SWA_WINDOW = 128
SWA_BLOCK = 128
SWA_COLS = (SWA_HEADS + 2 * SWA_KV_HEADS) * SWA_DIM
ROPE_THETA = 10000.0
AB_COLS = DN_COLS + SWA_COLS
RET_HEADS = 4
RET_DIM = 128
RET_WIDTH = RET_HEADS * RET_DIM
RET_CHUNK = 128
RET_COLS = 4 * RET_WIDTH
HY_CH = 512
HY_ORDER = 2
HY_CONV = 3
HY_BANDS = 8
HY_EMB = 1 + 2 * HY_BANDS
HY_HID = 64
HY_SIN_FREQ = 1.0
HY_TARGET = 1e-2
HY_DECAY_SHORT = 0.3
HY_DECAY_LONG = 1.5
HY_COLS = (HY_ORDER + 1) * HY_CH
CD_COLS = RET_COLS + HY_COLS
MIX_WIDTH = DN_WIDTH + SWA_HEADS * SWA_DIM
FFN_HIDDEN = -(-8 * D_MODEL // (3 * 256)) * 256
N_EVEN = (DEPTH + 1) // 2
N_ODD = DEPTH // 2

kernel_name = 'hybrid_deltanet_swa_retention_hyena_dit'


def rmsnorm(x, w):
    xf = x.astype(F32)
    y = xf * lax.rsqrt(jnp.mean(xf * xf, axis=-1, keepdims=True) + EPS)
    return (y * w.astype(F32)).astype(x.dtype)


def l2norm(x):
    return x * lax.rsqrt(jnp.sum(x * x, axis=-1, keepdims=True) + EPS)


def modulate(h, shift, scale):
    return h * (1 + scale) + shift


def swiglu(h, w_in, w_out):
    gate, up = jnp.split(h @ w_in, 2, axis=-1)
    return (jax.nn.silu(gate) * up) @ w_out


def rotate_half(x, ang):
    cos = jnp.cos(ang)[:, None, :]
    sin = jnp.sin(ang)[:, None, :]
    x1, x2 = jnp.split(x, 2, axis=-1)
    return jnp.concatenate([x1 * cos - x2 * sin, x2 * cos + x1 * sin], axis=-1).astype(x.dtype)


def axial_angles(rows):
    m = SWA_DIM // 4
    inv = ROPE_THETA ** (-jnp.arange(m, dtype=F32) / m)
    row = jnp.repeat(jnp.arange(rows, dtype=F32), GRID_W)
    col = jnp.broadcast_to(jnp.arange(GRID_W, dtype=F32), (rows, GRID_W)).reshape(-1)
    return row[:, None] * inv, col[:, None] * inv


def axial_rope(x, ang_row, ang_col):
    h = x.shape[-1] // 2
    return jnp.concatenate([rotate_half(x[..., :h], ang_row), rotate_half(x[..., h:], ang_col)], axis=-1)


def short_conv(x, w):
    K, C = w.shape
    return lax.conv_general_dilated(x, w[:, None, :].astype(x.dtype), window_strides=(1,),
                                    padding=[(K // 2, K // 2)],
                                    dimension_numbers=('NWC', 'WIO', 'NWC'),
                                    feature_group_count=C)


def flip_time(t):
    return jnp.flip(t, axis=2)


def keep_time(t):
    return t


def delta_chunk_scan(q, k, v, g, beta, s0):
    B, H, L, dk = q.shape
    dv = v.shape[-1]
    C = DN_CHUNK
    N = L // C
    q = q.reshape(B, H, N, C, dk)
    k = k.reshape(B, H, N, C, dk)
    v = v.reshape(B, H, N, C, dv)
    g = g.reshape(B, H, N, C)
    beta = beta.reshape(B, H, N, C)
    gam = jnp.cumsum(g, axis=-1)
    idx = jnp.arange(C)
    incl = idx[:, None] >= idx[None, :]
    strict = idx[:, None] > idx[None, :]
    e = jnp.exp(jnp.where(incl, gam[..., :, None] - gam[..., None, :], 0.0))
    dec_incl = jnp.where(incl, e, 0.0)
    dec_strict = jnp.where(strict, e, 0.0)
    kb = k * beta[..., None]
    m = jnp.einsum('bhnid,bhnjd->bhnij', kb, k) * dec_strict
    a = m + jnp.eye(C, dtype=F32)
    rhs = jnp.concatenate([v * beta[..., None], kb * jnp.exp(gam)[..., None]], axis=-1)
    sol = lax.linalg.triangular_solve(a, rhs, left_side=True, lower=True, unit_diagonal=True)
    u, w = sol[..., :dv], sol[..., dv:]
    attn = jnp.einsum('bhnid,bhnjd->bhnij', q, k) * dec_incl
    q_dec = q * jnp.exp(gam)[..., None]
    k_dec = k * jnp.exp(gam[..., -1:] - gam)[..., None]
    chunk_dec = jnp.exp(gam[..., -1])

    def step(S, xs):
        u_n, w_n, a_n, qd_n, kd_n, cd_n = xs
        v_new = u_n - jnp.einsum('bhck,bhkv->bhcv', w_n, S)
        o = jnp.einsum('bhck,bhkv->bhcv', qd_n, S) + jnp.einsum('bhij,bhjv->bhiv', a_n, v_new)
        S = S * cd_n[..., None, None] + jnp.einsum('bhck,bhcv->bhkv', kd_n, v_new)
        return S, o

    xs = tuple(jnp.moveaxis(t, 2, 0) for t in (u, w, attn, q_dec, k_dec, chunk_dec))
    S, o = lax.scan(step, s0, xs)
    return jnp.moveaxis(o, 0, 2).reshape(B, H, L, dv), S


def dn_features(p, conv_w, a_log, dt_bias):
    B, L, _ = p.shape
    qkv = jax.nn.silu(short_conv(p[..., :3 * DN_WIDTH], conv_w)).astype(F32)
    heads = lambda t: t.reshape(B, L, DN_HEADS, DN_DIM).transpose(0, 2, 1, 3)
    q, k, v = (heads(t) for t in jnp.split(qkv, 3, axis=-1))
    q = l2norm(q) * DN_DIM ** -0.5
    k = l2norm(k)
    gate = p[..., 3 * DN_WIDTH:4 * DN_WIDTH]
    ab = p[..., 4 * DN_WIDTH:].astype(F32).reshape(B, L, 2, 2, DN_HEADS)
    g = -jnp.exp(a_log.astype(F32)) * jax.nn.softplus(ab[:, :, 0] + dt_bias.astype(F32))
    beta = jax.nn.sigmoid(ab[:, :, 1])
    return q, k, v, g.transpose(2, 0, 3, 1), beta.transpose(2, 0, 3, 1), gate


def head_norm_gate(o, gate, w):
    B, H, L, dv = o.shape
    y = rmsnorm(o.transpose(0, 2, 1, 3), w)
    return (y * jax.nn.silu(gate.astype(F32).reshape(B, L, H, dv))).reshape(B, L, H * dv)


def gated_deltanet(p_lat, p_ctx, conv_w, a_log, dt_bias, norm_w, need_ctx):
    lat = dn_features(p_lat, conv_w, a_log, dt_bias)
    cx = dn_features(p_ctx, conv_w, a_log, dt_bias)
    B = p_lat.shape[0]
    o_lat, o_ctx = [], []
    for d in range(2):
        fl = flip_time if d == 1 else keep_time
        s0 = jnp.zeros((B, DN_HEADS, DN_DIM, DN_DIM), F32)
        oc, s_ctx = delta_chunk_scan(fl(cx[0]), fl(cx[1]), fl(cx[2]), fl(cx[3][d]), fl(cx[4][d]), s0)
        ol, _ = delta_chunk_scan(fl(lat[0]), fl(lat[1]), fl(lat[2]), fl(lat[3][d]), fl(lat[4][d]), s_ctx)
        o_lat.append(fl(ol))
        o_ctx.append(fl(oc))
    out_lat = head_norm_gate(o_lat[0] + o_lat[1], lat[5], norm_w)
    out_ctx = head_norm_gate(o_ctx[0] + o_ctx[1], cx[5], norm_w) if need_ctx else None
    return out_lat, out_ctx


def gqa_qkv(p, q_norm_w, k_norm_w):
    B, L, _ = p.shape
    nq, nk = SWA_HEADS * SWA_DIM, SWA_KV_HEADS * SWA_DIM
    q = rmsnorm(p[..., :nq].reshape(B, L, SWA_HEADS, SWA_DIM), q_norm_w)
    k = rmsnorm(p[..., nq:nq + nk].reshape(B, L, SWA_KV_HEADS, SWA_DIM), k_norm_w)
    v = p[..., nq + nk:].reshape(B, L, SWA_KV_HEADS, SWA_DIM)
    return q, k, v


def banded_attention(q, k, v, kc, vc, sink):
    B, L, Hq, d = q.shape
    Hkv = k.shape[2]
    G = Hq // Hkv
    W = SWA_BLOCK
    NB = L // W
    qb = q.reshape(B, NB, W, Hkv, G, d)

    def band(t):
        tp = jnp.pad(t.reshape(B, NB, W, Hkv, d), ((0, 0), (1, 1), (0, 0), (0, 0), (0, 0)))
        return jnp.concatenate([tp[:, :-2], tp[:, 1:-1], tp[:, 2:]], axis=2)

    kb, vb = band(k), band(v)
    scale = d ** -0.5
    s_loc = jnp.einsum('bnqhgd,bnkhd->bnhgqk', qb, kb).astype(F32) * scale
    s_ctx = jnp.einsum('bnqhgd,bchd->bnhgqc', qb, kc).astype(F32) * scale
    rel = (jnp.arange(3 * W) - W)[None, :] - jnp.arange(W)[:, None]
    kblk = jnp.arange(NB)[:, None] + jnp.arange(3 * W)[None, :] // W - 1
    valid = (jnp.abs(rel) <= SWA_WINDOW)[None] & ((kblk >= 0) & (kblk < NB))[:, None, :]
    s_loc = jnp.where(valid[None, :, None, None], s_loc, -jnp.inf)
    s_sink = jnp.broadcast_to(sink.astype(F32).reshape(Hkv, G, 1, 1), s_loc.shape[:-1] + (1,))
    prob = jax.nn.softmax(jnp.concatenate([s_loc, s_ctx, s_sink], axis=-1), axis=-1).astype(v.dtype)
    n_ctx = kc.shape[1]
    o = (jnp.einsum('bnhgqk,bnkhd->bnqhgd', prob[..., :3 * W], vb)
         + jnp.einsum('bnhgqc,bchd->bnqhgd', prob[..., 3 * W:3 * W + n_ctx], vc))
    return o.reshape(B, L, Hq * d)


def context_attention(qc, kc, vc, sink):
    B, Cn, Hq, d = qc.shape
    Hkv = kc.shape[2]
    G = Hq // Hkv
    q = qc.reshape(B, Cn, Hkv, G, d)
    s = jnp.einsum('bqhgd,bkhd->bhgqk', q, kc).astype(F32) * d ** -0.5
    s_sink = jnp.broadcast_to(sink.astype(F32).reshape(Hkv, G, 1, 1), s.shape[:-1] + (1,))
    prob = jax.nn.softmax(jnp.concatenate([s, s_sink], axis=-1), axis=-1)[..., :-1].astype(vc.dtype)
    return jnp.einsum('bhgqk,bkhd->bqhgd', prob, vc).reshape(B, Cn, Hq * d)


def window_gqa(p_lat, p_ctx, ang_row, ang_col, q_norm_w, k_norm_w, sink, need_ctx):
    ql, kl, vl = gqa_qkv(p_lat, q_norm_w, k_norm_w)
    ql = axial_rope(ql, ang_row, ang_col)
    kl = axial_rope(kl, ang_row, ang_col)
    qc, kc, vc = gqa_qkv(p_ctx, q_norm_w, k_norm_w)
    o_lat = banded_attention(ql, kl, vl, kc, vc, sink)
    o_ctx = context_attention(qc, kc, vc, sink) if need_ctx else None
    return o_lat, o_ctx


def ab_mixer(p_lat, p_ctx, ang_row, ang_col, dn_conv_w, dn_a_log, dn_dt_bias, dn_norm_w,
             q_norm_w, k_norm_w, sink, need_ctx):
    a_lat, a_ctx = gated_deltanet(p_lat[..., :DN_COLS], p_ctx[..., :DN_COLS], dn_conv_w, dn_a_log,
                                  dn_dt_bias, dn_norm_w, need_ctx)
    b_lat, b_ctx = window_gqa(p_lat[..., DN_COLS:], p_ctx[..., DN_COLS:], ang_row, ang_col,
                              q_norm_w, k_norm_w, sink, need_ctx)
    o_lat = jnp.concatenate([a_lat, b_lat.astype(F32)], axis=-1)
    o_ctx = jnp.concatenate([a_ctx, b_ctx.astype(F32)], axis=-1) if need_ctx else None
    return o_lat, o_ctx


def retention_chunk_scan(q, k, v, log_gamma, s0):
    B, H, L, dk = q.shape
    dv = v.shape[-1]
    C = RET_CHUNK
    N = L // C
    q = q.reshape(B, H, N, C, dk)
    k = k.reshape(B, H, N, C, dk)
    v = v.reshape(B, H, N, C, dv)
    pos = jnp.arange(C, dtype=F32)
    rel = pos[:, None] - pos[None, :]
    incl = rel >= 0
    lg = log_gamma[:, None, None]
    dmat = jnp.where(incl, jnp.exp(lg * jnp.where(incl, rel, 0.0)), 0.0)
    scores = jnp.einsum('bhnid,bhnjd->bhnij', q, k) * dmat[:, None]
    o_in = jnp.einsum('bhnij,bhnje->bhnie', scores, v)
    q_dec = q * jnp.exp(log_gamma[:, None] * (pos + 1.0))[:, None, :, None]
    k_dec = k * jnp.exp(log_gamma[:, None] * (C - 1.0 - pos))[:, None, :, None]
    chunk_dec = jnp.exp(log_gamma * C)[None, :, None, None]

    def step(S, xs):
        qd, kd, vn = xs
        o = jnp.einsum('bhcd,bhde->bhce', qd, S)
        S = S * chunk_dec + jnp.einsum('bhcd,bhce->bhde', kd, vn)
        return S, o

    xs = tuple(jnp.moveaxis(t, 2, 0) for t in (q_dec, k_dec, v))
    S, o_x = lax.scan(step, s0, xs)
    o = o_in + jnp.moveaxis(o_x, 0, 2)
    return o.reshape(B, H, L, dv), S


def ret_features(p):
    B, L, _ = p.shape
    q, k, v, g = jnp.split(p, 4, axis=-1)
    inv = ROPE_THETA ** (-jnp.linspace(0.0, 1.0, RET_DIM // 2, dtype=F32))
    ang = jnp.arange(L, dtype=F32)[:, None] * inv
    heads = lambda t: t.reshape(B, L, RET_HEADS, RET_DIM).astype(F32)
    q = rotate_half(heads(q), ang)
    k = rotate_half(heads(k), ang) * RET_DIM ** -0.5
    tr = lambda t: t.transpose(0, 2, 1, 3)
    return tr(q), tr(k), tr(heads(v)), g


def head_groupnorm_gate(o, gate, w):
    B, H, L, dv = o.shape
    o = o.transpose(0, 2, 1, 3)
    mu = jnp.mean(o, axis=-1, keepdims=True)
    var = jnp.mean(jnp.square(o - mu), axis=-1, keepdims=True)
    y = (o - mu) * lax.rsqrt(var + EPS) * w.astype(F32).reshape(H, dv)
    return (y * jax.nn.silu(gate.astype(F32).reshape(B, L, H, dv))).reshape(B, L, H * dv)


def retention(p_lat, p_ctx, decay_logit, gn_w, need_ctx):
    lat, cx = ret_features(p_lat), ret_features(p_ctx)
    log_gamma = jax.nn.log_sigmoid(decay_logit.astype(F32))
    B = p_lat.shape[0]
    o_lat, o_ctx = [], []
    for d in range(2):
        fl = flip_time if d == 1 else keep_time
        s0 = jnp.zeros((B, RET_HEADS, RET_DIM, RET_DIM), F32)
        oc, s_ctx = retention_chunk_scan(fl(cx[0]), fl(cx[1]), fl(cx[2]), log_gamma[d], s0)
        ol, _ = retention_chunk_scan(fl(lat[0]), fl(lat[1]), fl(lat[2]), log_gamma[d], s_ctx)
        o_lat.append(fl(ol))
        o_ctx.append(fl(oc))
    out_lat = head_groupnorm_gate(o_lat[0] + o_lat[1], lat[3], gn_w)
    out_ctx = head_groupnorm_gate(o_ctx[0] + o_ctx[1], cx[3], gn_w) if need_ctx else None
    return out_lat, out_ctx


def hyena_filters(L, w1, b1, w2, b2, w3):
    pos = jnp.arange(L, dtype=F32)
    t = pos / max(L - 1, 1)
    bands = jnp.linspace(1e-4, HY_BANDS - 1, HY_BANDS, dtype=F32)
    phase = (2.0 * math.pi / L) * pos[:, None] * bands[None, :]
    z = jnp.concatenate([t[:, None], jnp.cos(phase), -jnp.sin(phase)], axis=-1)
    h = jnp.sin(HY_SIN_FREQ * (z @ w1.astype(F32) + b1.astype(F32)))
    h = jnp.sin(HY_SIN_FREQ * (h @ w2.astype(F32) + b2.astype(F32)))
    h = (h @ w3.astype(F32)).reshape(L, HY_ORDER, 2, HY_CH)
    rates = jnp.abs(jnp.linspace(math.log(HY_TARGET) / HY_DECAY_LONG, math.log(HY_TARGET) / HY_DECAY_SHORT,
                                 HY_CH, dtype=F32))
    h = h * jnp.exp(-t[:, None] * rates[None, :])[:, None, None, :]
    h = h / (jnp.sum(jnp.abs(h), axis=(0, 2), keepdims=True) + EPS)
    return h.transpose(1, 2, 0, 3)


def bidir_long_conv(u, h_fwd, h_bwd, skip):
    B, L, C = u.shape
    taps = jnp.concatenate([h_fwd, jnp.zeros((1, C), F32), h_bwd[:0:-1]], axis=0)
    y = jnp.fft.irfft(jnp.fft.rfft(u, n=2 * L, axis=1) * jnp.fft.rfft(taps, axis=0)[None],
                      n=2 * L, axis=1)[:, :L]
    return y + u * skip


def hyena(p, conv_w, w1, b1, w2, b2, w3, skip):
    L = p.shape[1]
    filt = hyena_filters(L, w1, b1, w2, b2, w3)
    u = short_conv(p, conv_w).astype(F32)
    parts = jnp.split(u, HY_ORDER + 1, axis=-1)
    z = parts[0]
    for n in range(HY_ORDER):
        z = parts[n + 1] * bidir_long_conv(z, filt[n, 0], filt[n, 1], skip[n].astype(F32))
    return z


def cd_mixer(p_lat, p_ctx, ret_decay_logit, ret_gn_w, hy_conv_w, hy_f_w1, hy_f_b1, hy_f_w2, hy_f_b2,
             hy_f_w3, hy_bias, need_ctx):
    c_lat, c_ctx = retention(p_lat[..., :RET_COLS], p_ctx[..., :RET_COLS], ret_decay_logit, ret_gn_w, need_ctx)
    hy = lambda p: hyena(p, hy_conv_w, hy_f_w1, hy_f_b1, hy_f_w2, hy_f_b2, hy_f_w3, hy_bias)
    o_lat = jnp.concatenate([c_lat, hy(p_lat[..., RET_COLS:])], axis=-1)
    o_ctx = jnp.concatenate([c_ctx, hy(p_ctx[..., RET_COLS:])], axis=-1) if need_ctx else None
    return o_lat, o_ctx


def setup_inputs(seed: int = 0) -> dict:
    key = jax.random.key(seed)
    keys = iter(jax.random.split(key, 40))

    def nrm(shape, scale):
        return jax.random.normal(next(keys), shape, F32) * scale

    def unif(shape, lo, hi):
        return jax.random.uniform(next(keys), shape, F32, lo, hi)

    dt = jnp.exp(unif((N_EVEN, 2, DN_HEADS), math.log(1e-3), math.log(1e-1)))
    ret_logit0 = jnp.log(2.0 ** (5.0 + jnp.arange(RET_HEADS, dtype=F32)) - 1.0)
    return {
        'x': nrm((BATCH, SEQ, D_MODEL), 1.0),
        'c': nrm((BATCH, D_MODEL), 1.0),
        'ctx': nrm((BATCH, CTX_LEN, D_MODEL), 1.0),
        'c_ctx': nrm((D_MODEL,), 1.0),
        'mod_w': nrm((DEPTH, D_MODEL, 6 * D_MODEL), 0.5 * D_MODEL ** -0.5),
        'mod_b': nrm((DEPTH, 6 * D_MODEL), 0.02),
        'norm_mix_w': 1.0 + nrm((DEPTH, D_MODEL), 0.02),
        'norm_ffn_w': 1.0 + nrm((DEPTH, D_MODEL), 0.02),
        'ffn_w_in': nrm((DEPTH, D_MODEL, 2 * FFN_HIDDEN), D_MODEL ** -0.5),
        'ffn_w_out': nrm((DEPTH, FFN_HIDDEN, D_MODEL), FFN_HIDDEN ** -0.5),
        'ab_w_in': nrm((N_EVEN, D_MODEL, AB_COLS), D_MODEL ** -0.5),
        'ab_w_out': nrm((N_EVEN, MIX_WIDTH, D_MODEL), MIX_WIDTH ** -0.5),
        'dn_conv_w': nrm((N_EVEN, DN_CONV, 3 * DN_WIDTH), DN_CONV ** -0.5),
        'dn_a_log': jnp.log(unif((N_EVEN, 2, DN_HEADS), 1.0, 16.0)),
        'dn_dt_bias': dt + jnp.log(-jnp.expm1(-dt)),
        'dn_norm_w': 1.0 + nrm((N_EVEN, DN_DIM), 0.02),
        'swa_q_norm_w': 1.0 + nrm((N_EVEN, SWA_DIM), 0.02),
        'swa_k_norm_w': 1.0 + nrm((N_EVEN, SWA_DIM), 0.02),
        'swa_sink': nrm((N_EVEN, SWA_HEADS), 0.5),
        'cd_w_in': nrm((N_ODD, D_MODEL, CD_COLS), D_MODEL ** -0.5),
        'cd_w_out': nrm((N_ODD, MIX_WIDTH, D_MODEL), MIX_WIDTH ** -0.5),
        'ret_decay_logit': ret_logit0 + nrm((N_ODD, 2, RET_HEADS), 0.05),
        'ret_gn_w': 1.0 + nrm((N_ODD, RET_WIDTH), 0.02),
        'hy_conv_w': nrm((N_ODD, HY_CONV, HY_COLS), HY_CONV ** -0.5),
        'hy_f_w1': nrm((N_ODD, HY_EMB, HY_HID), HY_EMB ** -0.5),
        'hy_f_b1': nrm((N_ODD, HY_HID), 0.1),
        'hy_f_w2': nrm((N_ODD, HY_HID, HY_HID), HY_HID ** -0.5),
        'hy_f_b2': nrm((N_ODD, HY_HID), 0.1),
        'hy_f_w3': nrm((N_ODD, HY_HID, HY_ORDER * 2 * HY_CH), HY_HID ** -0.5),
        'hy_bias': nrm((N_ODD, HY_ORDER, HY_CH), 0.5),
    }


def reference(x, c, ctx, c_ctx, mod_w, mod_b, norm_mix_w, norm_ffn_w, ffn_w_in, ffn_w_out,
              ab_w_in, ab_w_out, dn_conv_w, dn_a_log, dn_dt_bias, dn_norm_w, swa_q_norm_w, swa_k_norm_w,
              swa_sink, cd_w_in, cd_w_out, ret_decay_logit, ret_gn_w, hy_conv_w, hy_f_w1, hy_f_b1,
              hy_f_w2, hy_f_b2, hy_f_w3, hy_bias):
    L = x.shape[1]
    rows = L // GRID_W
    ang_row, ang_col = axial_angles(rows)
    c_act = jax.nn.silu(c)
    cc_act = jax.nn.silu(c_ctx)[None]
    h_ctx = ctx
    for layer in range(DEPTH):
        need_ctx = layer != DEPTH - 1
        i = layer // 2
        mod = jnp.split((c_act @ mod_w[layer] + mod_b[layer])[:, None], 6, axis=-1)
        mod_c = jnp.split((cc_act @ mod_w[layer] + mod_b[layer])[:, None], 6, axis=-1)
        hx = modulate(rmsnorm(x, norm_mix_w[layer]), mod[0], mod[1])
        hc = modulate(rmsnorm(h_ctx, norm_mix_w[layer]), mod_c[0], mod_c[1])
        if layer % 2 == 0:
            o_lat, o_ctx = ab_mixer(hx @ ab_w_in[i], hc @ ab_w_in[i], ang_row, ang_col, dn_conv_w[i],
                                    dn_a_log[i], dn_dt_bias[i], dn_norm_w[i], swa_q_norm_w[i],
                                    swa_k_norm_w[i], swa_sink[i], need_ctx)
            w_out = ab_w_out[i]
        else:
            o_lat, o_ctx = cd_mixer(hx @ cd_w_in[i], hc @ cd_w_in[i], ret_decay_logit[i], ret_gn_w[i],
                                    hy_conv_w[i], hy_f_w1[i], hy_f_b1[i], hy_f_w2[i], hy_f_b2[i],
                                    hy_f_w3[i], hy_bias[i], need_ctx)
            w_out = cd_w_out[i]
        x = x + mod[2] * (o_lat.astype(x.dtype) @ w_out)
        x = x + mod[5] * swiglu(modulate(rmsnorm(x, norm_ffn_w[layer]), mod[3], mod[4]),
                                ffn_w_in[layer], ffn_w_out[layer])
        if need_ctx:
            h_ctx = h_ctx + mod_c[2] * (o_ctx.astype(h_ctx.dtype) @ w_out)
            h_ctx = h_ctx + mod_c[5] * swiglu(modulate(rmsnorm(h_ctx, norm_ffn_w[layer]), mod_c[3], mod_c[4]),
                                              ffn_w_in[layer], ffn_w_out[layer])
    return x
```

```python
import numpy as np
import concourse.bass as bass
import concourse.mybir as mybir
from concourse.bass_utils import run_bass_kernel_spmd

F32 = mybir.dt.float32
BF16 = mybir.dt.bfloat16
AF = mybir.ActivationFunctionType
ALU = mybir.AluOpType
AX = mybir.AxisListType


class T:
    __slots__ = ("h", "name", "lw", "rd")

    def __init__(self, h, name):
        self.h = h
        self.name = name
        self.lw = None
        self.rd = {}

    def __getitem__(self, idx):
        return self.h[idx]


class TV:
    def __init__(self, t, fn):
        self.t = t
        self.fn = fn

    def __getitem__(self, idx):
        return self.fn(self.t)[idx]
    lw = property(lambda s: s.t.lw, lambda s, v: setattr(s.t, "lw", v))
    rd = property(lambda s: s.t.rd, lambda s, v: setattr(s.t, "rd", v))
    name = property(lambda s: s.t.name)


class KB:
    NDMA = 6

    def __init__(self):
        self.nc = bass.Bass("TRN2", target_bir_lowering=False)
        nc = self.nc
        self._ctx = []
        self.eng = {"pe": nc.tensor, "dve": nc.vector, "act": nc.scalar, "pool": nc.gpsimd, "sp": nc.sync}
        self.sems = {}
        self.cnt = {}
        for e in ("pe", "dve", "act", "pool"):
            self.sems[e] = self._enter(nc.semaphore("s_" + e))
            self.cnt[e] = 0
        self.dma_rr = {"sp": 0, "pool": 0, "act": 0}
        for q in ("sp", "pool", "act"):
            for i in range(self.NDMA):
                k = "d%s%d" % (q, i)
                self.sems[k] = self._enter(nc.semaphore("s_" + k))
                self.cnt[k] = 0
        self.seen = {e: {} for e in self.eng}
        self.n_ins = 0
        self.n_wait = 0

    def _enter(self, cm):
        v = cm.__enter__()
        self._ctx.append(cm)
        return v

    def close(self):
        for cm in reversed(self._ctx):
            cm.__exit__(None, None, None)
        self._ctx = []

    def sb(self, name, shape, dt=F32):
        self.uid = getattr(self, "uid", 0) + 1
        name = "%s_%d" % (name, self.uid)
        return T(self._enter(self.nc.sbuf_tensor(name, list(shape), dt)), name)

    def ps(self, name, shape, dt=F32):
        self.uid = getattr(self, "uid", 0) + 1
        name = "%s_%d" % (name, self.uid)
        return T(self._enter(self.nc.psum_tensor(name, list(shape), dt)), name)

    def dram(self, name, shape, dt=F32, kind="ExternalInput"):
        return self.nc.dram_tensor(name, list(shape), dt, kind=kind).ap()

    def _wait(self, e, deps):
        need = {}
        for d in deps:
            if d is None:
                continue
            k, c = d
            if c > need.get(k, 0):
                need[k] = c
        seen = self.seen[e]
        for k, c in need.items():
            if seen.get(k, 0) >= c:
                continue
            self.eng[e].wait_ge(self.sems[k], c)
            self.n_wait += 1
            seen[k] = c

    def op(self, e, fn, reads=(), writes=(), pe_acc=False):
        deps = []
        for t in reads:
            deps.append(t.lw)
        for t in writes:
            if not (pe_acc and t.lw is not None and t.lw[0] == "pe"):
                deps.append(t.lw)
            deps.extend(t.rd.items())
        if e == "pe":
            deps = [d for d in deps if d is not None and d[0] != "pe"]
        self._wait(e, deps)
        ins = fn()
        self.cnt[e] += 1
        ins.then_inc(self.sems[e], 1)
        me = (e, self.cnt[e])
        for t in reads:
            t.rd[me[0]] = me[1]
        for t in writes:
            t.lw = me
            t.rd = {}
        self.n_ins += 1
        return ins

    def dma(self, q, out, in_, reads=(), writes=(), **kw):
        i = self.dma_rr[q]
        self.dma_rr[q] = (i + 1) % self.NDMA
        k = "d%s%d" % (q, i)
        deps = [(k, self.cnt[k])] if self.cnt[k] else []
        for t in reads:
            deps.append(t.lw)
        for t in writes:
            deps.append(t.lw)
            deps.extend(t.rd.items())
        self._wait(q, deps)
        ins = self.eng[q].dma_start(out=out, in_=in_, **kw)
        self.cnt[k] += 16
        ins.then_inc(self.sems[k], 16)
        me = (k, self.cnt[k])
        for t in reads:
            t.rd[me[0]] = me[1]
        for t in writes:
            t.lw = me
            t.rd = {}
        self.n_ins += 1
        return me

    def mark(self):
        return len(self._ctx)

    def release(self, mark):
        while len(self._ctx) > mark:
            self._ctx.pop().__exit__(None, None, None)

    def all_counts(self):
        return [(k, c) for k, c in self.cnt.items() if c]

    def barrier(self):
        deps = self.all_counts()
        for e in self.eng:
            self._wait(e, deps)

    def dma_counts(self):
        return [(k, c) for k, c in self.cnt.items() if c and k.startswith("d")]

    def allgather(self, in_ap, out_ap, deps, groups):
        if "cc" not in self.sems:
            self.sems["cc"] = self._enter(self.nc.semaphore("s_cc"))
            self.cnt["cc"] = 0
        self._wait("pool", deps)
        ins = self.nc.gpsimd.collective_compute("AllGather", ALU.bypass, replica_groups=groups, ins=[in_ap], outs=[out_ap])
        self.cnt["cc"] += 1
        ins.then_inc(self.sems["cc"])
        return ("cc", self.cnt["cc"])

    def finish(self, e="sp"):
        deps = [(k, c) for k, c in self.cnt.items() if c]
        self._wait(e, deps)


def interleave(gen_iter, width):
    active = []
    it = iter(gen_iter)
    more = True
    while True:
        while more and len(active) < width:
            try:
                active.append(next(it))
            except StopIteration:
                more = False
        if not active:
            break
        for g in list(active):
            try:
                next(g)
            except StopIteration:
                active.remove(g)


D = 1024
KC = 8
FH = 2816
NTOK = 4160
BLOCKS = [(0, 64, 1)] + [(64 + i * 512, 512, 0) for i in range(8)]
EPS = 1e-6


class Ring:
    def __init__(self, ts):
        self.ts = ts
        self.i = 0

    def next(self):
        t = self.ts[self.i]
        self.i = (self.i + 1) % len(self.ts)
        return t


def tile_w(W, kc):
    K, NCOL = W.shape
    nch = (NCOL + 127) // 128
    Wp = np.zeros((K, nch * 128), np.float32)
    Wp[:, :NCOL] = W
    return np.ascontiguousarray(Wp.reshape(kc, 128, nch, 128).transpose(2, 1, 0, 3).reshape(nch, 128, kc * 128))


def col8(v):
    return np.ascontiguousarray(v.reshape(-1, 128).T)


def build_dense(has_out, has_ffn, has_in, ncols_in, final):
    kb = KB()
    nc = kb.nc
    nin = (ncols_in + 127) // 128 if has_in else 0
    xT = kb.dram("xT", [D, NTOK])
    csT = kb.dram("csT", [128, 16])
    ins = ["xT", "csT"]
    if has_out or has_ffn:
        modw_a = kb.dram("modw_a", [48, 128, KC * 128]); modb_a = kb.dram("modb_a", [128, 48])
        ins += ["modw_a", "modb_a"]
    if has_in:
        modw_b = kb.dram("modw_b", [48, 128, KC * 128]); modb_b = kb.dram("modb_b", [128, 48])
        nmix = kb.dram("nmix", [128, 8])
        w_in = kb.dram("w_in", [nin, 128, KC * 128])
        pT = kb.dram("pT", [nin * 128, NTOK], kind="ExternalOutput")
        ins += ["modw_b", "modb_b", "nmix", "w_in"]
    if has_out:
        oT = kb.dram("oT", [D, NTOK]); w_out = kb.dram("w_out", [8, 128, KC * 128])
        ins += ["oT", "w_out"]
    if has_ffn:
        nffn = kb.dram("nffn", [128, 8])
        f_in = kb.dram("f_in", [44, 128, KC * 128]); f_out = kb.dram("f_out", [8, 128, 22 * 128])
        ins += ["nffn", "f_in", "f_out"]
    xo = kb.dram("xo", [D, NTOK], kind="ExternalOutput")

    wring = Ring([kb.sb("w%d" % i, [128, 22 * 128]) for i in range(4)])
    psr = Ring([kb.ps("ps%d" % i, [128, 512]) for i in range(7)])
    psm = kb.ps("psm", [128, 512])
    xr = Ring([kb.sb("x%d" % i, [128, KC, 512]) for i in range(2)])
    ht = kb.sb("ht", [128, KC, 512])
    sq = kb.sb("sq", [128, KC, 512])
    rr = kb.sb("rr", [128, 512])
    ones = kb.sb("ones", [128, 128])
    cs = kb.sb("cs", [128, 16])
    tmpr = Ring([kb.sb("tmp%d" % i, [128, 512]) for i in range(3)])
    if has_out:
        ot = kb.sb("ot", [128, KC, 512])
    if has_ffn:
        actT = kb.sb("actT", [128, 22, 512])
    wq = ["sp", "pool"]
    wqi = [0]

    def wload(src_ap, width):
        w = wring.next()
        q = wq[wqi[0] % 2]; wqi[0] += 1
        kb.dma(q, w[:, :width], src_ap, writes=[w])
        return w

    kb.op("dve", lambda: nc.vector.memset(ones[:], 1.0 / D), writes=[ones])
    kb.dma("sp", cs[:], csT[:, :], writes=[cs])
    kb.op("act", lambda: nc.scalar.activation(out=cs[:], in_=cs[:], func=AF.Silu), reads=[cs], writes=[cs])

    def mod_compute(modw, modb, name):
        mb = kb.sb(name + "_b", [128, 48])
        mo = kb.sb(name, [128, 48, 2])
        kb.dma("sp", mb[:], modb[:, :], writes=[mb])
        for n in range(48):
            w = wload(modw[n], KC * 128)
            for k in range(KC):
                kb.op("pe", lambda: nc.tensor.matmul(psm[:, n * 2:n * 2 + 2], w[:, k * 128:(k + 1) * 128], cs[:, k * 2:k * 2 + 2],
                                                     start=(k == 0), stop=(k == KC - 1)),
                      reads=[w, cs], writes=[psm], pe_acc=True)
        for s in range(2):
            kb.op("dve", lambda: nc.vector.tensor_tensor(mo[:, :, s], psm[:, s:96:2], mb[:, :], ALU.add),
                  reads=[psm, mb], writes=[mo])
        return mo

    def gs(mo, nw_dram, name, sh, sc):
        nw = kb.sb(name + "_nw", [128, 8])
        G = kb.sb(name + "_G", [128, 8, 2])
        kb.dma("sp", nw[:], nw_dram[:, :], writes=[nw])
        for s in range(2):
            kb.op("dve", lambda: nc.vector.scalar_tensor_tensor(G[:, :, s], mo[:, sc * 8:sc * 8 + 8, s], 1.0, nw[:, :], ALU.add, ALU.mult),
                  reads=[mo, nw], writes=[G])
        return G

    if has_out or has_ffn:
        moA = mod_compute(modw_a, modb_a, "moA")
    if has_ffn:
        G2 = gs(moA, nffn, "g2", 3, 4)
    if has_in:
        moB = mod_compute(modw_b, modb_b, "moB")
        G1 = gs(moB, nmix, "g1", 0, 1)

    def norm_mod(xt, N, G, mo, shift_idx, s):
        for k in range(KC):
            kb.op("act", lambda: nc.scalar.activation(out=sq[:, k, :N], in_=xt[:, k, :N], func=AF.Square), reads=[xt], writes=[sq])
        ps = psr.next()
        for k in range(KC):
            kb.op("pe", lambda: nc.tensor.matmul(ps[:, :N], ones[:, :], sq[:, k, :N], start=(k == 0), stop=(k == KC - 1)),
                  reads=[ones, sq], writes=[ps], pe_acc=True)
        kb.op("dve", lambda: nc.vector.tensor_scalar(rr[:, :N], ps[:, :N], EPS, None, ALU.add), reads=[ps], writes=[rr])
        kb.op("act", lambda: nc.scalar.activation(out=rr[:, :N], in_=rr[:, :N], func=AF.Sqrt), reads=[rr], writes=[rr])
        kb.op("dve", lambda: nc.vector.reciprocal(rr[:, :N], rr[:, :N]), reads=[rr], writes=[rr])
        for k in range(KC):
            kb.op("dve", lambda: nc.vector.scalar_tensor_tensor(ht[:, k, :N], xt[:, k, :N], G[:, k, s:s + 1], rr[:, :N], ALU.mult, ALU.mult),
                  reads=[xt, G, rr], writes=[ht])
            kb.op("act", lambda: nc.scalar.activation(out=ht[:, k, :N], in_=ht[:, k, :N], func=AF.Identity,
                                                      bias=mo[:, shift_idx * 8 + k, s:s + 1], scale=1.0),
                  reads=[ht, mo], writes=[ht])

    xTv = xT.rearrange("(k p) t -> p k t", p=128)
    xov = xo.rearrange("(k p) t -> p k t", p=128)
    if has_out:
        oTv = oT.rearrange("(k p) t -> p k t", p=128)

    for (t0, N, s) in BLOCKS:
        xt = xr.next()
        kb.dma("sp", xt[:, :, :N], xTv[:, :, t0:t0 + N], writes=[xt])
        if has_out:
            kb.dma("pool", ot[:, :, :N], oTv[:, :, t0:t0 + N], writes=[ot])
            for m in range(8):
                w = wload(w_out[m], KC * 128)
                ps = psr.next()
                for k in range(KC):
                    kb.op("pe", lambda: nc.tensor.matmul(ps[:, :N], w[:, k * 128:(k + 1) * 128], ot[:, k, :N], start=(k == 0), stop=(k == KC - 1)),
                          reads=[w, ot], writes=[ps], pe_acc=True)
                kb.op("dve", lambda: nc.vector.scalar_tensor_tensor(xt[:, m, :N], ps[:, :N], moA[:, 16 + m, s:s + 1], xt[:, m, :N], ALU.mult, ALU.add),
                      reads=[ps, moA, xt], writes=[xt])
        if has_ffn:
            norm_mod(xt, N, G2, moA, 3, s)
            for j in range(22):
                wg = wload(f_in[j], KC * 128)
                wu = wload(f_in[22 + j], KC * 128)
                pg = psr.next(); pu = psr.next()
                for k in range(KC):
                    kb.op("pe", lambda: nc.tensor.matmul(pg[:, :N], wg[:, k * 128:(k + 1) * 128], ht[:, k, :N], start=(k == 0), stop=(k == KC - 1)),
                          reads=[wg, ht], writes=[pg], pe_acc=True)
                for k in range(KC):
                    kb.op("pe", lambda: nc.tensor.matmul(pu[:, :N], wu[:, k * 128:(k + 1) * 128], ht[:, k, :N], start=(k == 0), stop=(k == KC - 1)),
                          reads=[wu, ht], writes=[pu], pe_acc=True)
                tg = tmpr.next()
                kb.op("act", lambda: nc.scalar.activation(out=tg[:, :N], in_=pg[:, :N], func=AF.Silu), reads=[pg], writes=[tg])
                kb.op("dve", lambda: nc.vector.tensor_tensor(actT[:, j, :N], tg[:, :N], pu[:, :N], ALU.mult), reads=[tg, pu], writes=[actT])
            for m in range(8):
                w = wload(f_out[m], 22 * 128)
                ps = psr.next()
                for j in range(22):
                    kb.op("pe", lambda: nc.tensor.matmul(ps[:, :N], w[:, j * 128:(j + 1) * 128], actT[:, j, :N], start=(j == 0), stop=(j == 21)),
                          reads=[w, actT], writes=[ps], pe_acc=True)
                kb.op("dve", lambda: nc.vector.scalar_tensor_tensor(xt[:, m, :N], ps[:, :N], moA[:, 40 + m, s:s + 1], xt[:, m, :N], ALU.mult, ALU.add),
                      reads=[ps, moA, xt], writes=[xt])
        kb.dma("sp", xov[:, :, t0:t0 + N], xt[:, :, :N], reads=[xt])
        if has_in:
            norm_mod(xt, N, G1, moB, 0, s)
            for n in range(nin):
                w = wload(w_in[n], KC * 128)
                ps = psr.next()
                for k in range(KC):
                    kb.op("pe", lambda: nc.tensor.matmul(ps[:, :N], w[:, k * 128:(k + 1) * 128], ht[:, k, :N], start=(k == 0), stop=(k == KC - 1)),
                          reads=[w, ht], writes=[ps], pe_acc=True)
                tp = tmpr.next()
                kb.op("act", lambda: nc.scalar.copy(out=tp[:, :N], in_=ps[:, :N]), reads=[ps], writes=[tp])
                kb.dma("sp", pT[n * 128:(n + 1) * 128, t0:t0 + N], tp[:, :N], reads=[tp])
    kb.finish("sp")
    kb.close()
    return kb, ins


def dense_host_common(inputs, layer_a, layer_b):
    m = {}
    if layer_a is not None:
        m["modw_a"] = tile_w(inputs["mod_w"][layer_a], KC)
        m["modb_a"] = col8(inputs["mod_b"][layer_a])
        m["nffn"] = col8(inputs["norm_ffn_w"][layer_a])
        m["f_in"] = tile_w(inputs["ffn_w_in"][layer_a], KC)
        m["f_out"] = tile_w(inputs["ffn_w_out"][layer_a], 22)
        m["w_out"] = tile_w((inputs["ab_w_out"] if layer_a == 0 else inputs["cd_w_out"])[0], KC)
    if layer_b is not None:
        m["modw_b"] = tile_w(inputs["mod_w"][layer_b], KC)
        m["modb_b"] = col8(inputs["mod_b"][layer_b])
        m["nmix"] = col8(inputs["norm_mix_w"][layer_b])
        m["w_in"] = tile_w((inputs["ab_w_in"] if layer_b == 0 else inputs["cd_w_in"])[0], KC)
    return m


def cs_core(inputs, b):
    a = np.stack([inputs["c"][b], inputs["c_ctx"]], axis=-1)
    return np.ascontiguousarray(a.reshape(8, 128, 2).transpose(1, 0, 2).reshape(128, 16))


def shard_tokT(lat, ctx, c):
    b, q = c // 4, c % 4
    return np.ascontiguousarray(np.concatenate([ctx[b, q * 64:(q + 1) * 64], lat[b, q * 4096:(q + 1) * 4096]], axis=0).T)


def unshard_tokT(arrs, F):
    lat = np.zeros((2, 16384, F), np.float32); ctx = np.zeros((2, 256, F), np.float32)
    for c, a in enumerate(arrs):
        b, q = c // 4, c % 4
        ctx[b, q * 64:(q + 1) * 64] = a[:F, :64].T
        lat[b, q * 4096:(q + 1) * 4096] = a[:F, 64:].T
    return lat, ctx


L = 16384
CTX = 256
EPS = 1e-6
NCK = 260
DN_BATCHES = [(0, 4)] + [(4 + 8 * i, 8) for i in range(32)]


def build_ab(do_dn=True, do_swa=True, kb=None, pre=None, out_cb=None):
    own = kb is None
    if own:
        kb = KB()
    nc = kb.nc
    ins = []
    pre = pre or {}

    def din(name, shape):
        if name in pre:
            return pre[name]
        ins.append(name)
        t_ = kb.dram(name, shape)
        if not own:
            pre[name] = t_
        return t_

    ident_d = din("c_ident", [128, 128])
    ident = kb.sb("ident", [128, 128])
    kb.dma("sp", ident[:], ident_d[:, :], writes=[ident])
    ones = kb.sb("ones", [128, 128])
    kb.op("dve", lambda: nc.vector.memset(ones[:], 1.0), writes=[ones])
    PS = [kb.ps("P%d" % i, [128, 512]) for i in range(8)]

    if do_dn:
        qpad = din("d_qpad", [128, CTX + 2 + L + 2])
        kpad = din("d_kpad", [128, CTX + 2 + L + 2])
        vpad = din("d_vpad", [CTX + 2 + L + 2, 128])
        gate = din("d_gate", [CTX + L, 128])
        abx = din("d_ab", [64, NCK, 4])
        cwqk = din("d_cwqk", [128, 6])
        cwv = din("d_cwv", [64, 3, 128])
        alog = din("d_alog", [64, 2]); dtb = din("d_dtb", [64, 2])
        nw = din("d_nw", [128, 128])
        ctri = din("c_tri", [64, 2, 64])
        cmaskS = din("c_maskS", [64, 2, 64])
        cmaskI = din("c_maskI", [64, 2, 64])
        o_dn = kb.dram("o_dn", [CTX + L, 128], kind="ExternalOutput") if out_cb is None else None
        oscr = kb.dram("d_oscr", [2, CTX + L, 128], kind="Internal")

        tri = kb.sb("tri", [64, 2, 64]); maskS = kb.sb("maskS", [64, 2, 64]); maskI = kb.sb("maskI", [64, 2, 64])
        cw = kb.sb("cw", [128, 6]); cwvs = kb.sb("cwvs", [64, 3, 128]); al = kb.sb("al", [64, 2]); db = kb.sb("db", [64, 2])
        nws = kb.sb("nws", [128, 128])
        for t_, d_ in ((tri, ctri), (maskS, cmaskS), (maskI, cmaskI), (cwvs, cwv)):
            kb.dma("sp", t_[:], d_[:, :, :], writes=[t_])
        for t_, d_ in ((cw, cwqk), (al, alog), (db, dtb), (nws, nw)):
            kb.dma("sp", t_[:], d_[:, :], writes=[t_])
        gb = kb.sb("gb", [64, NCK, 4]); tA = kb.sb("tA", [64, NCK, 2]); tB = kb.sb("tB", [64, NCK, 2])
        kb.dma("sp", gb[:], abx[:, :, :], writes=[gb])
        kb.op("act", lambda: nc.scalar.activation(out=al[:], in_=al[:], func=AF.Exp), reads=[al], writes=[al])
        kb.op("dve", lambda: nc.vector.tensor_scalar(al[:], al[:], -1.0, None, ALU.mult), reads=[al], writes=[al])
        for d in range(2):
            kb.op("dve", lambda: nc.vector.tensor_scalar(tA[:, :, d], gb[:, :, d], db[:, d:d + 1], None, ALU.add), reads=[gb, db], writes=[tA])
        kb.op("act", lambda: nc.scalar.activation(out=tB[:], in_=tA[:], func=AF.Abs), reads=[tA], writes=[tB])
        kb.op("act", lambda: nc.scalar.activation(out=tB[:], in_=tB[:], func=AF.Exp, scale=-1.0), reads=[tB], writes=[tB])
        kb.op("dve", lambda: nc.vector.tensor_scalar(tB[:], tB[:], 1.0, None, ALU.add), reads=[tB], writes=[tB])
        kb.op("act", lambda: nc.scalar.activation(out=tB[:], in_=tB[:], func=AF.Ln), reads=[tB], writes=[tB])
        kb.op("dve", lambda: nc.vector.tensor_scalar(tA[:], tA[:], 0.0, None, ALU.max), reads=[tA], writes=[tA])
        kb.op("dve", lambda: nc.vector.tensor_tensor(tA[:], tA[:], tB[:], ALU.add), reads=[tA, tB], writes=[tA])
        for d in range(2):
            kb.op("dve", lambda: nc.vector.tensor_scalar(gb[:, :, d], tA[:, :, d], al[:, d:d + 1], None, ALU.mult), reads=[tA, al], writes=[gb])
        kb.op("act", lambda: nc.scalar.activation(out=gb[:, :, 2:4], in_=gb[:, :, 2:4], func=AF.Sigmoid), reads=[gb], writes=[gb])

        mk_main = kb.mark()
        rawq = Ring([kb.sb("rawq%d" % i, [128, 514]) for i in range(2)])
        rawk = Ring([kb.sb("rawk%d" % i, [128, 514]) for i in range(2)])
        qTr = Ring([kb.sb("qT%d" % i, [128, 512]) for i in range(2)])
        kTr = Ring([kb.sb("kT%d" % i, [128, 512]) for i in range(2)])
        sqr = Ring([kb.sb("sq%d" % i, [128, 512]) for i in range(2)])
        rsr = Ring([kb.sb("rs%d" % i, [128, 512]) for i in range(2)])
        vrw = Ring([kb.sb("vrw%d" % i, [64, 3, 8, 128]) for i in range(2)])
        vtr = Ring([kb.sb("vt%d" % i, [64, 8, 128]) for i in range(2)])
        vt2 = kb.sb("vt2", [64, 8, 128])
        gLr = Ring([kb.sb("gL%d" % i, [64, 8, 64]) for i in range(2)])
        gcol = kb.sb("gcol", [64, 2, NCK])
        egr = Ring([kb.sb("eg%d" % i, [128, 8, 64]) for i in range(2)])
        dfr = Ring([kb.sb("df%d" % i, [64, 8, 64]) for i in range(2)])
        dSr = Ring([kb.sb("dS%d" % i, [64, 8, 64]) for i in range(2)])
        Ar = Ring([kb.sb("A%d" % i, [64, 8, 64]) for i in range(2)])
        Br = Ring([kb.sb("B%d" % i, [64, 8, 64]) for i in range(2)])
        Ur = Ring([kb.sb("U%d" % i, [64, 8, 64]) for i in range(2)])
        ktm = Ring([kb.sb("ktm%d" % i, [64, 8, 128]) for i in range(2)])
        ecol = kb.sb("ecol", [64, 2, NCK]); ecol2 = kb.sb("ecol2", [64, 2, NCK])
        o_u = Ring([kb.sb("ou%d" % i, [64, 8, 128]) for i in range(2)])
        o_wT = Ring([kb.sb("owT%d" % i, [128, 8, 64]) for i in range(2)])
        o_qd = Ring([kb.sb("oqd%d" % i, [128, 8, 64]) for i in range(2)])
        o_at = Ring([kb.sb("oat%d" % i, [64, 8, 64]) for i in range(2)])
        o_kd = Ring([kb.sb("okd%d" % i, [64, 8, 128]) for i in range(2)])
        o_cd = Ring([kb.sb("ocd%d" % i, [128, 8]) for i in range(2)])
        wtmr = Ring([kb.sb("wtm%d" % i, [64, 8, 128]) for i in range(2)])
        o_MT = Ring([kb.sb("oMT%d" % i, [128, 8, 128]) for i in range(2)])
        o_NN = Ring([kb.sb("oNN%d" % i, [128, 8, 128]) for i in range(2)])
        o_QT = Ring([kb.sb("oQT%d" % i, [128, 8, 64]) for i in range(2)])
        vnr = Ring([kb.sb("vn%d" % i, [64, 128]) for i in range(2)])
        oor = Ring([kb.sb("oo%d" % i, [64, 128]) for i in range(3)])
        Sr = Ring([kb.sb("S%d" % i, [128, 128]) for i in range(2)])

        def tok0(c0):
            return c0 * 64

        def padoff(c0):
            return c0 * 64 if c0 < 4 else c0 * 64 + 2

        for d in range(2):
            kb.op("pe", lambda: nc.tensor.matmul(PS[0][0:64, 0:NCK], tri[:, d, :], gb[:, :, d], start=True, stop=True), reads=[tri, gb], writes=[PS[0]])
            kb.op("dve", lambda: nc.vector.tensor_copy(gcol[:, d, :], PS[0][0:64, 0:NCK]), reads=[PS[0]], writes=[gcol])
        kb.op("act", lambda: nc.scalar.activation(out=ecol[:], in_=gcol[:], func=AF.Exp), reads=[gcol], writes=[ecol])

        def feat_pre(c0, n):
            N = n * 64
            po = padoff(c0)
            outs = []
            for which, (pad_d, ring_raw, ring_o) in enumerate(((qpad, rawq, qTr), (kpad, rawk, kTr))):
                raw = ring_raw.next(); o = ring_o.next(); sq = sqr.next(); rs = rsr.next()
                kb.dma("sp" if which == 0 else "pool", raw[:, :N + 2], pad_d[:, po:po + N + 2], writes=[raw])
                kb.op("dve", lambda: nc.vector.tensor_scalar(o[:, :N], raw[:, 0:N], cw[:, which * 3:which * 3 + 1], None, ALU.mult), reads=[raw, cw], writes=[o])
                kb.op("dve", lambda: nc.vector.scalar_tensor_tensor(o[:, :N], raw[:, 1:N + 1], cw[:, which * 3 + 1:which * 3 + 2], o[:, :N], ALU.mult, ALU.add), reads=[raw, cw, o], writes=[o])
                kb.op("dve", lambda: nc.vector.scalar_tensor_tensor(o[:, :N], raw[:, 2:N + 2], cw[:, which * 3 + 2:which * 3 + 3], o[:, :N], ALU.mult, ALU.add), reads=[raw, cw, o], writes=[o])
                kb.op("act", lambda: nc.scalar.activation(out=o[:, :N], in_=o[:, :N], func=AF.Silu), reads=[o], writes=[o])
                kb.op("act", lambda: nc.scalar.activation(out=sq[:, :N], in_=o[:, :N], func=AF.Square), reads=[o], writes=[sq])
                ps = PS[1]
                kb.op("pe", lambda: nc.tensor.matmul(ps[:, :N], ones[:], sq[:, :N], start=True, stop=True), reads=[ones, sq], writes=[ps])
                kb.op("dve", lambda: nc.vector.tensor_scalar(rs[:, :N], ps[:, :N], EPS, None, ALU.add), reads=[ps], writes=[rs])
                kb.op("act", lambda: nc.scalar.activation(out=rs[:, :N], in_=rs[:, :N], func=AF.Sqrt), reads=[rs], writes=[rs])
                kb.op("dve", lambda: nc.vector.reciprocal(rs[:, :N], rs[:, :N]), reads=[rs], writes=[rs])
                if which == 0:
                    kb.op("dve", lambda: nc.vector.scalar_tensor_tensor(o[:, :N], o[:, :N], 128.0 ** -0.5, rs[:, :N], ALU.mult, ALU.mult), reads=[o, rs], writes=[o])
                else:
                    kb.op("dve", lambda: nc.vector.tensor_tensor(o[:, :N], o[:, :N], rs[:, :N], ALU.mult), reads=[o, rs], writes=[o])
                outs.append(o)
            return outs

        def v_pre(c0, n):
            vr = vrw.next(); vt = vtr.next()
            base = padoff(c0)
            for s in range(3):
                kb.dma("pool", vr[:, s, :n, :], vpad[base + s: base + s + n * 64, :].rearrange("(c j) e -> j c e", j=64), writes=[vr])

            def wv(s):
                return cwvs[:, s, :].rearrange("p (o e) -> p o e", o=1).to_broadcast([64, n, 128])
            kb.op("pool", lambda: nc.gpsimd.tensor_tensor(vt[:, :n, :], vr[:, 0, :n, :], wv(0), ALU.mult), reads=[vr, cwvs], writes=[vt])
            kb.op("pool", lambda: nc.gpsimd.tensor_tensor(vt2[:, :n, :], vr[:, 1, :n, :], wv(1), ALU.mult), reads=[vr, cwvs], writes=[vt2])
            kb.op("pool", lambda: nc.gpsimd.tensor_tensor(vt[:, :n, :], vt[:, :n, :], vt2[:, :n, :], ALU.add), reads=[vt, vt2], writes=[vt])
            kb.op("pool", lambda: nc.gpsimd.tensor_tensor(vt2[:, :n, :], vr[:, 2, :n, :], wv(2), ALU.mult), reads=[vr, cwvs], writes=[vt2])
            kb.op("pool", lambda: nc.gpsimd.tensor_tensor(vt[:, :n, :], vt[:, :n, :], vt2[:, :n, :], ALU.add), reads=[vt, vt2], writes=[vt])
            kb.op("act", lambda: nc.scalar.activation(out=vt[:, :n, :], in_=vt[:, :n, :], func=AF.Silu), reads=[vt], writes=[vt])
            return vt

        def chunk_pre(d, c0, n, qT, kT, vt, res):
            last = 63 if d == 0 else 0
            cs = slice(c0, c0 + n)
            gL = gLr.next()
            kb.op("dve", lambda: nc.vector.tensor_tensor(gL[:, :n, :], tri[:, d, :].rearrange("p (o i) -> p o i", o=1).to_broadcast([64, n, 64]),
                                                         gb[:, cs, d].rearrange("p (c o) -> p c o", o=1).to_broadcast([64, n, 64]), ALU.mult),
                  reads=[tri, gb], writes=[gL])
            pg = PS[2]
            kb.op("pe", lambda: nc.tensor.matmul(pg[:, :n * 64], ones[0:64, :], gL[:, :n, :].rearrange("p c i -> p (c i)"), start=True, stop=True),
                  reads=[ones, gL], writes=[pg])
            pgv = pg[:, :n * 64].rearrange("p (c i) -> p c i", i=64)
            eg = egr.next()
            kb.op("act", lambda: nc.scalar.activation(out=eg[:, :n, :], in_=pgv, func=AF.Exp), reads=[pg], writes=[eg])
            yield
            df = dfr.next(); dS = dSr.next()
            gc_b = gcol[:, d, cs].rearrange("p (c o) -> p c o", o=1).to_broadcast([64, n, 64])
            kb.op("dve", lambda: nc.vector.tensor_tensor(df[:, :n, :], pg[0:64, :n * 64].rearrange("p (c i) -> p c i", i=64), gc_b, ALU.subtract), reads=[pg, gcol], writes=[df])
            kb.op("dve", lambda: nc.vector.tensor_scalar(df[:, :n, :], df[:, :n, :], 0.0, None, ALU.min), reads=[df], writes=[df])
            kb.op("act", lambda: nc.scalar.activation(out=df[:, :n, :], in_=df[:, :n, :], func=AF.Exp), reads=[df], writes=[df])
            mS = maskS[:, d, :].rearrange("p (o i) -> p o i", o=1).to_broadcast([64, n, 64])
            mI = maskI[:, d, :].rearrange("p (o i) -> p o i", o=1).to_broadcast([64, n, 64])
            bc = gb[:, cs, 2 + d].rearrange("p (c o) -> p c o", o=1).to_broadcast([64, n, 64])
            kb.op("dve", lambda: nc.vector.tensor_tensor(dS[:, :n, :], df[:, :n, :], mS, ALU.mult), reads=[df, maskS], writes=[dS])
            kb.op("dve", lambda: nc.vector.tensor_tensor(dS[:, :n, :], dS[:, :n, :], bc, ALU.mult), reads=[dS, gb], writes=[dS])
            kb.op("pool", lambda: nc.gpsimd.tensor_tensor(df[:, :n, :], df[:, :n, :], mI, ALU.mult), reads=[df, maskI], writes=[df])
            yield
            pG = PS[3]; pQ = PS[4]
            for c in range(n):
                kb.op("pe", lambda: nc.tensor.matmul(pG[0:64, c * 64:(c + 1) * 64], kT[:, c * 64:(c + 1) * 64], kT[:, c * 64:(c + 1) * 64], start=True, stop=True),
                      reads=[kT], writes=[pG])
                kb.op("pe", lambda: nc.tensor.matmul(pQ[0:64, c * 64:(c + 1) * 64], kT[:, c * 64:(c + 1) * 64], qT[:, c * 64:(c + 1) * 64], start=True, stop=True),
                      reads=[kT, qT], writes=[pQ])
            B = Br.next(); at = o_at.next()
            kb.op("dve", lambda: nc.vector.tensor_tensor(B[:, :n, :], pG[0:64, :n * 64].rearrange("p (c i) -> p c i", i=64), dS[:, :n, :], ALU.mult), reads=[pG, dS], writes=[B])
            kb.op("dve", lambda: nc.vector.tensor_tensor(at[:, :n, :], pQ[0:64, :n * 64].rearrange("p (c i) -> p c i", i=64), df[:, :n, :], ALU.mult), reads=[pQ, df], writes=[at])
            yield
            pA = PS[5]
            for c in range(n):
                kb.op("pe", lambda: nc.tensor.matmul(pA[0:64, c * 64:(c + 1) * 64], B[:, c, :], ident[0:64, 0:64], start=True, stop=True), reads=[B, ident], writes=[pA])
            A = Ar.next()
            kb.op("act", lambda: nc.scalar.copy(out=A[:, :n, :], in_=pA[0:64, :n * 64].rearrange("p (c i) -> p c i", i=64)), reads=[pA], writes=[A])
            U = Ur.next()
            idb = ident[0:64, 0:64].rearrange("p (o i) -> p o i", o=1).to_broadcast([64, n, 64])
            kb.op("pool", lambda: nc.gpsimd.tensor_tensor(U[:, :n, :], B[:, :n, :], idb, ALU.add), reads=[B, ident], writes=[U])
            yield
            for lvl in range(1, 6):
                pA2 = PS[5]; pB2 = PS[3]; pU = PS[4]
                for c in range(n):
                    kb.op("pe", lambda: nc.tensor.matmul(pA2[0:64, c * 64:(c + 1) * 64], B[:, c, :], A[:, c, :], start=True, stop=True), reads=[A, B], writes=[pA2])
                if lvl < 5:
                    for c in range(n):
                        kb.op("pe", lambda: nc.tensor.matmul(pB2[0:64, c * 64:(c + 1) * 64], A[:, c, :], B[:, c, :], start=True, stop=True), reads=[A, B], writes=[pB2])
                A2 = Ar.next()
                kb.op("act", lambda: nc.scalar.copy(out=A2[:, :n, :], in_=pA2[0:64, :n * 64].rearrange("p (c i) -> p c i", i=64)), reads=[pA2], writes=[A2])
                if lvl < 5:
                    B2 = Br.next()
                    kb.op("dve", lambda: nc.vector.tensor_copy(B2[:, :n, :], pB2[0:64, :n * 64].rearrange("p (c i) -> p c i", i=64)), reads=[pB2], writes=[B2])
                    B = B2
                A = A2
                yield
                for c in range(n):
                    kb.op("pe", lambda: nc.tensor.matmul(pU[0:64, c * 64:(c + 1) * 64], A[:, c, :], U[:, c, :], start=True, stop=True), reads=[A, U], writes=[pU])
                U2 = Ur.next()
                kb.op("dve", lambda: nc.vector.tensor_tensor(U2[:, :n, :], U[:, :n, :], pU[0:64, :n * 64].rearrange("p (c i) -> p c i", i=64), ALU.add), reads=[U, pU], writes=[U2])
                U = U2
                yield
            bd = gLr.next()
            kb.op("dve", lambda: nc.vector.tensor_tensor(bd[:, :n, :], idb, bc, ALU.mult), reads=[ident, gb], writes=[bd])
            pb = PS[2]
            kb.op("pe", lambda: nc.tensor.matmul(pb[0:64, :n * 64], ones[0:64, 0:64], bd[:, :n, :].rearrange("p c i -> p (c i)"), start=True, stop=True), reads=[ones, bd], writes=[pb])
            kb.op("dve", lambda: nc.vector.tensor_tensor(U[:, :n, :], U[:, :n, :], pb[0:64, :n * 64].rearrange("p (c i) -> p c i", i=64), ALU.mult), reads=[U, pb], writes=[U])
            yield
            kd = o_kd.next(); kw = ktm.next()
            for half in range(0, n, 4):
                pk = PS[6]
                m = min(4, n - half)
                for c in range(half, half + m):
                    kb.op("pe", lambda: nc.tensor.matmul(pk[0:64, (c - half) * 128:(c - half + 1) * 128], kT[:, c * 64:(c + 1) * 64], ident[:], start=True, stop=True),
                          reads=[kT, ident], writes=[pk])
                pkv = pk[0:64, :m * 128].rearrange("p (c e) -> p c e", e=128)
                e1 = ecol[:, d, c0 + half:c0 + half + m].rearrange("p (c o) -> p c o", o=1).to_broadcast([64, m, 128])
                e2 = ecol2[:, d, c0 + half:c0 + half + m].rearrange("p (c o) -> p c o", o=1).to_broadcast([64, m, 128])
                kb.op("dve", lambda: nc.vector.tensor_tensor(kw[:, half:half + m, :], pkv, e1, ALU.mult), reads=[pk, ecol], writes=[kw])
                kb.op("dve", lambda: nc.vector.tensor_tensor(kd[:, half:half + m, :], pkv, e2, ALU.mult), reads=[pk, ecol2], writes=[kd])
                yield
            u = o_u.next()
            for half in range(0, n, 4):
                pu = PS[7]
                m = min(4, n - half)
                for c in range(half, half + m):
                    kb.op("pe", lambda: nc.tensor.matmul(pu[0:64, (c - half) * 128:(c - half + 1) * 128], U[:, c, :], vt[:, c, :], start=True, stop=True), reads=[U, vt], writes=[pu])
                kb.op("act", lambda: nc.scalar.copy(out=u[:, half:half + m, :], in_=pu[0:64, :m * 128].rearrange("p (c e) -> p c e", e=128)), reads=[pu], writes=[u])
                yield
            qd = o_qd.next(); cd = o_cd.next()
            kb.op("pool", lambda: nc.gpsimd.tensor_tensor(qd[:, :n, :], qT[:, :n * 64].rearrange("p (c i) -> p c i", i=64), eg[:, :n, :], ALU.mult), reads=[qT, eg], writes=[qd])
            kb.op("act", lambda: nc.scalar.copy(out=cd[:, :n], in_=eg[:, :n, last]), reads=[eg], writes=[cd])
            wt_ = wtmr.next()
            for half in range(0, n, 4):
                pw = PS[6]
                m = min(4, n - half)
                for c in range(half, half + m):
                    kb.op("pe", lambda: nc.tensor.matmul(pw[0:64, (c - half) * 128:(c - half + 1) * 128], U[:, c, :], kw[:, c, :], start=True, stop=True), reads=[U, kw], writes=[pw])
                kb.op("act", lambda: nc.scalar.copy(out=wt_[:, half:half + m, :], in_=pw[0:64, :m * 128].rearrange("p (c e) -> p c e", e=128)), reads=[pw], writes=[wt_])
                yield
            MT = o_MT.next(); NN = o_NN.next(); QT = o_QT.next()
            for half in range(0, n, 4):
                pm = PS[3]; pn = PS[5]
                m = min(4, n - half)
                for c in range(half, half + m):
                    kb.op("pe", lambda: nc.tensor.matmul(pm[:, (c - half) * 128:(c - half + 1) * 128], wt_[:, c, :], kd[:, c, :], start=True, stop=True), reads=[wt_, kd], writes=[pm])
                    kb.op("pe", lambda: nc.tensor.matmul(pn[:, (c - half) * 128:(c - half + 1) * 128], kd[:, c, :], u[:, c, :], start=True, stop=True), reads=[kd, u], writes=[pn])
                for c in range(half, half + m):
                    kb.op("dve", lambda: nc.vector.scalar_tensor_tensor(MT[:, c, :], ident[:], cd[:, c:c + 1], pm[:, (c - half) * 128:(c - half + 1) * 128], ALU.mult, ALU.subtract),
                          reads=[ident, cd, pm], writes=[MT])
                kb.op("act", lambda: nc.scalar.copy(out=NN[:, half:half + m, :], in_=pn[:, :m * 128].rearrange("p (c e) -> p c e", e=128)), reads=[pn], writes=[NN])
                yield
            pq = PS[2]
            for c in range(n):
                kb.op("pe", lambda: nc.tensor.matmul(pq[:, c * 64:(c + 1) * 64], wt_[:, c, :], at[:, c, :], start=True, stop=True), reads=[wt_, at], writes=[pq])
            kb.op("dve", lambda: nc.vector.tensor_tensor(QT[:, :n, :], qd[:, :n, :], pq[:, :n * 64].rearrange("p (c i) -> p c i", i=64), ALU.subtract), reads=[qd, pq], writes=[QT])
            res["v"] = (u, MT, NN, QT, at)
            yield

        def chunk_seq(d, c0, n, pre, S, res):
            u, MT, NN, QT, at = pre
            order = range(n) if d == 0 else range(n - 1, -1, -1)
            for c in order:
                p2 = PS[1]
                kb.op("pe", lambda: nc.tensor.matmul(p2[0:64, 0:128], QT[:, c, :], S[:], start=True, stop=False), reads=[QT, S], writes=[p2])
                kb.op("pe", lambda: nc.tensor.matmul(p2[0:64, 0:128], at[:, c, :], u[:, c, :], start=False, stop=True), reads=[at, u], writes=[p2], pe_acc=True)
                p1 = PS[0]
                kb.op("pe", lambda: nc.tensor.matmul(p1[:, 0:128], MT[:, c, :], S[:], start=True, stop=True), reads=[MT, S], writes=[p1])
                S2 = Sr.next()
                kb.op("dve", lambda: nc.vector.tensor_tensor(S2[:], p1[:, 0:128], NN[:, c, :], ALU.add), reads=[p1, NN], writes=[S2])
                oo = oor.next()
                kb.op("act", lambda: nc.scalar.copy(out=oo[:], in_=p2[0:64, 0:128]), reads=[p2], writes=[oo])
                t0 = (c0 + c) * 64
                kb.dma("sp", oscr[d, t0:t0 + 64, :], oo[:], reads=[oo])
                S = S2
                res["S"] = S
                yield
            res["S"] = S

        sel = kb.sb("sel", [64, 2, NCK])
        oh = kb.sb("oh", [64, 2])
        ohd = din("c_onehot", [64, 2])
        kb.dma("sp", oh[:], ohd[:, :], writes=[oh])
        for d in range(2):
            kb.op("dve", lambda: nc.vector.tensor_scalar(sel[:, d, :], gcol[:, d, :], oh[:, d:d + 1], None, ALU.mult), reads=[gcol, oh], writes=[sel])
            kb.op("pe", lambda: nc.tensor.matmul(PS[0][0:64, 0:NCK], ones[0:64, 0:64], sel[:, d, :], start=True, stop=True), reads=[ones, sel], writes=[PS[0]])
            kb.op("dve", lambda: nc.vector.tensor_tensor(ecol2[:, d, :], PS[0][0:64, 0:NCK], gcol[:, d, :], ALU.subtract), reads=[PS[0], gcol], writes=[ecol2])
        kb.op("act", lambda: nc.scalar.activation(out=ecol2[:], in_=ecol2[:], func=AF.Exp), reads=[ecol2], writes=[ecol2])

        for d in range(2):
            batches = DN_BATCHES if d == 0 else [DN_BATCHES[0]] + DN_BATCHES[:0:-1]
            S = Sr.next()
            kb.op("dve", lambda: nc.vector.memset(S[:], 0.0), writes=[S])
            prev = None
            for item in batches + [None]:
                cur = None
                g1 = g2 = None
                r1 = {}; r2 = {"S": S}
                if item is not None:
                    c0, n = item
                    qT, kT = feat_pre(c0, n)
                    vt = v_pre(c0, n)
                    g1 = chunk_pre(d, c0, n, qT, kT, vt, r1)
                if prev is not None:
                    g2 = chunk_seq(d, prev[0], prev[1], prev[2], S, r2)
                tick = 0
                while g1 is not None or g2 is not None:
                    if g1 is not None:
                        try:
                            next(g1)
                        except StopIteration:
                            g1 = None
                    tick += 1
                    if g2 is not None:
                        try:
                            next(g2)
                        except StopIteration:
                            g2 = None
                S = r2["S"]
                if item is not None:
                    cur = (c0, n, r1["v"])
                prev = cur
        dn_done = [(k, c) for k, c in kb.cnt.items() if k.startswith("dsp") and c]
        kb.barrier()
        kb.release(mk_main)
        W_ = 8
        f1r = Ring([kb.sb("f1%d" % i, [128, 2, 128]) for i in range(W_ + 1)])
        f2r = Ring([kb.sb("f2%d" % i, [128, 128]) for i in range(1)])
        fgr = Ring([kb.sb("fg%d" % i, [128, 128]) for i in range(W_ + 1)])
        fjr = Ring([kb.sb("fj%d" % i, [128, 128]) for i in range(W_ + 1)])
        fsr = Ring([kb.sb("fs%d" % i, [128, 2]) for i in range(W_ + 1)])
        kb._wait("pool", dn_done)

        def fin_tile(ti):
            ab_ = f1r.next(); g = fgr.next(); jk = fjr.next(); st = fsr.next()
            rows = slice(ti * 128, (ti + 1) * 128)
            kb.dma("sp", ab_[:], oscr[:, rows, :].rearrange("d p e -> p d e"), writes=[ab_])
            kb.dma("sp", g[:], gate[rows, :], writes=[g])

            a = TV(ab_, lambda t: t[:, 0, :])
            kb.op("dve", lambda: nc.vector.tensor_tensor(ab_[:, 0, :], ab_[:, 0, :], ab_[:, 1, :], ALU.add), reads=[ab_], writes=[ab_])
            yield
            kb.op("act", lambda: nc.scalar.activation(out=jk[:], in_=a[:], func=AF.Square, accum_out=st[:, 0:1]), reads=[a], writes=[jk, st])
            yield
            kb.op("dve", lambda: nc.vector.tensor_scalar(st[:, 1:2], st[:, 0:1], 1.0 / 128, EPS, ALU.mult, ALU.add), reads=[st], writes=[st])
            yield
            kb.op("act", lambda: nc.scalar.activation(out=st[:, 1:2], in_=st[:, 1:2], func=AF.Ln), reads=[st], writes=[st])
            kb.op("act", lambda: nc.scalar.activation(out=st[:, 1:2], in_=st[:, 1:2], func=AF.Exp, scale=-0.5), reads=[st], writes=[st])
            kb.op("act", lambda: nc.scalar.activation(out=jk[:], in_=g[:], func=AF.Exp, scale=-1.0), reads=[g], writes=[jk])
            yield
            kb.op("dve", lambda: nc.vector.tensor_scalar(jk[:], jk[:], 1.0, None, ALU.add), reads=[jk], writes=[jk])
            kb.op("dve", lambda: nc.vector.reciprocal(jk[:], jk[:]), reads=[jk], writes=[jk])
            kb.op("dve", lambda: nc.vector.tensor_tensor(g[:], g[:], jk[:], ALU.mult), reads=[g, jk], writes=[g])
            kb.op("dve", lambda: nc.vector.scalar_tensor_tensor(a[:], a[:], st[:, 1:2], nws[:], ALU.mult, ALU.mult), reads=[a, st, nws], writes=[a])
            kb.op("dve", lambda: nc.vector.tensor_tensor(a[:], a[:], g[:], ALU.mult), reads=[a, g], writes=[a])
            yield
            if out_cb is None:
                kb.dma("sp", o_dn[rows, :], a[:], reads=[a])
            else:
                out_cb(kb, ti * 128, a, ident, PS[4 + ti % 4], PS[4 + ti % 4][:, 0:128])
        interleave((fin_tile(ti) for ti in range((CTX + L) // 128)), W_)

    if own:
        kb.finish("sp")
        kb.close()
    return kb, ins


def ab_consts():
    m = {"c_ident": np.eye(128, dtype=np.float32)}
    i = np.arange(64)
    tri = np.zeros((64, 2, 64), np.float32)
    tri[:, 0, :] = (i[:, None] <= i[None, :]); tri[:, 1, :] = (i[:, None] >= i[None, :])
    m["c_tri"] = tri
    mS = np.zeros((64, 2, 64), np.float32); mI = np.zeros((64, 2, 64), np.float32)
    mS[:, 0, :] = -1.0 * (i[None, :] > i[:, None]); mS[:, 1, :] = -1.0 * (i[None, :] < i[:, None])
    mI[:, 0, :] = (i[None, :] >= i[:, None]); mI[:, 1, :] = (i[None, :] <= i[:, None])
    m["c_maskS"] = mS; m["c_maskI"] = mI
    oh = np.zeros((64, 2), np.float32); oh[63, 0] = 1; oh[0, 1] = 1
    m["c_onehot"] = oh
    return m


def pad_seq(a):
    return np.concatenate([np.zeros((1, a.shape[1]), np.float32), a, np.zeros((1, a.shape[1]), np.float32)], axis=0)


def ab_core_inputs(inputs, pl, pc, b, h, consts, ins):
    m = dict(consts)
    sl = lambda a, o: a[b][:, o + h * 128:o + (h + 1) * 128]
    if "d_qpad" in ins or "d_cwqk" in ins:
        for nm, off in ((("d_qpad", 0), ("d_kpad", 512)) if pl is not None else ()):
            m[nm] = np.concatenate([pad_seq(sl(pc, off)), pad_seq(sl(pl, off))], axis=0).T
        if pl is not None:
            m["d_vpad"] = np.concatenate([pad_seq(sl(pc, 1024)), pad_seq(sl(pl, 1024))], axis=0)
            m["d_gate"] = np.concatenate([sl(pc, 1536), sl(pl, 1536)], axis=0)
            cols = [2048 + kind * 8 + d * 4 + h for kind in range(2) for d in range(2)]
            ab = np.concatenate([pc[b][:, cols], pl[b][:, cols]], axis=0)
            m["d_ab"] = ab.reshape(NCK, 64, 4).transpose(1, 0, 2)
        cwf = inputs["dn_conv_w"][0]
        m["d_cwqk"] = np.concatenate([cwf[:, h * 128:(h + 1) * 128].T, cwf[:, 512 + h * 128:512 + (h + 1) * 128].T], axis=1)
        m["d_cwv"] = np.broadcast_to(cwf[:, 1024 + h * 128:1024 + (h + 1) * 128], (64, 3, 128))
        m["d_alog"] = np.broadcast_to(inputs["dn_a_log"][0][:, h], (64, 2))
        m["d_dtb"] = np.broadcast_to(inputs["dn_dt_bias"][0][:, h], (64, 2))
        m["d_nw"] = np.broadcast_to(inputs["dn_norm_w"][0], (128, 128))
    return {k: np.ascontiguousarray(m[k], dtype=np.float32) for k in ins if k in m}


SCALE = 128.0 ** -0.5


def build_swa(kb=None, pre=None, out_cb=None):
    own = kb is None
    if own:
        kb = KB()
    nc = kb.nc
    ins = []
    pre = pre or {}

    def din(name, shape):
        if name in pre:
            return pre[name]
        ins.append(name)
        t_ = kb.dram(name, shape)
        if not own:
            pre[name] = t_
        return t_

    RF = mybir.dt.float32r

    def RR(ap):
        return ap.bitcast(RF)
    qTd = din("s_qT", [128, CTX + L]); kTd = din("s_kT", [128, CTX + L]); vd = din("s_v", [CTX + L, 128])
    wqk = din("s_wqk", [128, 2]); sinkd = din("s_sink", [128, 1])
    Cd = din("s_C", [128, L]); Sd = din("s_S", [128, L])
    permd = din("s_perm", [128, 128]); identd = din("c_ident", [128, 128]); maskd = din("s_mask", [128, 384])
    o_sw = kb.dram("o_sw", [CTX + L, 128], kind="ExternalOutput") if out_cb is None else None

    BIG1 = kb.sb("BIG1", [128, L]); BIG2 = kb.sb("BIG2", [128, L])
    vv = BIG2[:, :].rearrange("p (b e) -> p b e", e=128)
    PS = [kb.ps("P%d" % i, [128, 512]) for i in range(8)]
    ident = kb.sb("ident", [128, 128]); perm = kb.sb("perm", [128, 128]); mask = kb.sb("mask", [128, 384])
    wq = kb.sb("wq", [128, 2]); sink = kb.sb("sink", [128, 1]); ones = kb.sb("ones", [128, 128])
    kc = kb.sb("kc", [128, 256]); qc = kb.sb("qc", [128, 256]); vc = kb.sb("vc", [128, 2, 128])
    for t_, d_ in ((ident, identd), (perm, permd), (mask, maskd), (wq, wqk), (sink, sinkd)):
        kb.dma("sp", t_[:], d_[:, :], writes=[t_])
    kb.op("dve", lambda: nc.vector.memset(ones[:], 1.0 / 128), writes=[ones])
    vstg = Ring([kb.sb("vstg%d" % i, [128, 8, 128]) for i in range(2)])
    ident_r = kb.sb("ident_r", [128, 128], dt=RF)
    kb.op("act", lambda: nc.scalar.copy(out=ident_r[:], in_=ident[:]), reads=[ident], writes=[ident_r])
    vs_ = vstg.next()
    kb.dma("pool", vs_[:, 0:2, :], vd[0:CTX, :].rearrange("(b p) e -> p b e", p=128), writes=[vs_])
    kb.op("pool", lambda: nc.gpsimd.tensor_copy(RR(vc[:]), vs_[:, 0:2, :]), reads=[vs_], writes=[vc])
    for q16 in range(16):
        vs_ = vstg.next()
        kb.dma("pool", vs_[:], vd[CTX + q16 * 1024: CTX + (q16 + 1) * 1024, :].rearrange("(b p) e -> p b e", p=128), writes=[vs_])
        if q16 % 2:
            kb.op("pool", lambda: nc.gpsimd.tensor_copy(RR(vv[:, q16 * 8:(q16 + 1) * 8, :]), vs_[:]), reads=[vs_], writes=[BIG2])
        else:
            kb.op("act", lambda: nc.scalar.copy(out=RR(vv[:, q16 * 8:(q16 + 1) * 8, :]), in_=vs_[:]), reads=[vs_], writes=[BIG2])

    rawr = Ring([kb.sb("raw%d" % i, [128, 512]) for i in range(2)])
    sqr = Ring([kb.sb("sq%d" % i, [128, 512]) for i in range(2)])
    rsr = Ring([kb.sb("rs%d" % i, [128, 512]) for i in range(2)])
    Cr = Ring([kb.sb("C%d" % i, [128, 512]) for i in range(2)])
    Sr_ = Ring([kb.sb("Sg%d" % i, [128, 512]) for i in range(2)])
    t2r = Ring([kb.sb("t2%d" % i, [128, 512]) for i in range(2)])
    qrr = Ring([kb.sb("qr%d" % i, [128, 512]) for i in range(2)])

    def prep(src_d, col0, N, wcol, out_t, out_ap, rope_t0):
        raw = rawr.next(); sq = sqr.next(); rs = rsr.next()
        kb.dma("sp", raw[:, :N], src_d[:, col0:col0 + N], writes=[raw])
        kb.op("act", lambda: nc.scalar.activation(out=sq[:, :N], in_=raw[:, :N], func=AF.Square), reads=[raw], writes=[sq])
        ps = PS[6]
        kb.op("pe", lambda: nc.tensor.matmul(ps[:, :N], ones[:], sq[:, :N], start=True, stop=True), reads=[ones, sq], writes=[ps])
        kb.op("dve", lambda: nc.vector.tensor_scalar(rs[:, :N], ps[:, :N], EPS, None, ALU.add), reads=[ps], writes=[rs])
        kb.op("act", lambda: nc.scalar.activation(out=rs[:, :N], in_=rs[:, :N], func=AF.Sqrt), reads=[rs], writes=[rs])
        kb.op("dve", lambda: nc.vector.reciprocal(rs[:, :N], rs[:, :N]), reads=[rs], writes=[rs])
        if rope_t0 is None:
            kb.op("dve", lambda: nc.vector.scalar_tensor_tensor(RR(out_ap), raw[:, :N], wq[:, wcol:wcol + 1], rs[:, :N], ALU.mult, ALU.mult), reads=[raw, wq, rs], writes=[out_t])
            return
        kb.op("dve", lambda: nc.vector.scalar_tensor_tensor(raw[:, :N], raw[:, :N], wq[:, wcol:wcol + 1], rs[:, :N], ALU.mult, ALU.mult), reads=[raw, wq, rs], writes=[raw])
        C = Cr.next(); S = Sr_.next(); t2 = t2r.next()
        kb.dma("pool", C[:, :N], Cd[:, rope_t0:rope_t0 + N], writes=[C])
        kb.dma("pool", S[:, :N], Sd[:, rope_t0:rope_t0 + N], writes=[S])
        pp = PS[7]
        kb.op("pe", lambda: nc.tensor.matmul(pp[:, :N], perm[:], raw[:, :N], start=True, stop=True), reads=[perm, raw], writes=[pp])
        kb.op("dve", lambda: nc.vector.tensor_tensor(t2[:, :N], pp[:, :N], S[:, :N], ALU.mult), reads=[pp, S], writes=[t2])
        kb.op("pool", lambda: nc.gpsimd.tensor_tensor(C[:, :N], raw[:, :N], C[:, :N], ALU.mult), reads=[raw, C], writes=[C])
        kb.op("dve", lambda: nc.vector.tensor_tensor(RR(out_ap), C[:, :N], t2[:, :N], ALU.add), reads=[C, t2], writes=[out_t])

    prep(kTd, 0, 256, 1, kc, kc[:, :], None)
    for blk in range(32):
        prep(kTd, CTX + blk * 512, 512, 1, BIG1, BIG1[:, blk * 512:(blk + 1) * 512], blk * 512)
    prep(qTd, 0, 256, 0, qc, qc[:, :], None)

    smr = Ring([kb.sb("sm%d" % i, [128, 640]) for i in range(4)])
    Pr = Ring([kb.sb("Pp%d" % i, [128, 640]) for i in range(4)])
    PTr = Ring([kb.sb("PT%d" % i, [128, 5, 128]) for i in range(4)])
    str_ = Ring([kb.sb("st%d" % i, [128, 8]) for i in range(4)])
    oor = Ring([kb.sb("oo%d" % i, [128, 128]) for i in range(4)])
    par = [0]

    def attend(q_t, q_ap, kloc, out_row0):
        p = par[0]; par[0] ^= 1
        PSl = PS[0 + p]; PSc = PS[2 + p]; PST = PS[4 + p]
        W = 0
        sm = smr.next(); P = Pr.next(); PT = PTr.next(); st = str_.next(); oo = oor.next()
        if kloc is not None:
            k0, k1, jlo, vblocks = kloc
            W = k1 - k0
            kb.op("pe", lambda: nc.tensor.matmul(PSl[:, 0:W], RR(q_ap), RR(BIG1[:, k0:k1]), start=True, stop=True), reads=[q_t, BIG1], writes=[PSl])
            kb.op("dve", lambda: nc.vector.tensor_tensor(sm[:, 0:W], PSl[:, 0:W], mask[:, jlo:jlo + W], ALU.add), reads=[PSl, mask], writes=[sm])
        else:
            vblocks = []
        kb.op("pe", lambda: nc.tensor.matmul(PSc[:, 0:256], RR(q_ap), RR(kc[:, :]), start=True, stop=True), reads=[q_t, kc], writes=[PSc])
        kb.op("act", lambda: nc.scalar.copy(out=sm[:, W:W + 256], in_=PSc[:, 0:256]), reads=[PSc], writes=[sm])
        yield
        WT = W + 256
        kb.op("dve", lambda: nc.vector.reduce_max(st[:, 0:1], sm[:, 0:WT], AX.X), reads=[sm], writes=[st])
        kb.op("dve", lambda: nc.vector.tensor_scalar(st[:, 1:2], st[:, 0:1], SCALE, sink[:, 0:1], ALU.mult, ALU.max), reads=[st, sink], writes=[st])
        kb.op("dve", lambda: nc.vector.tensor_scalar(st[:, 2:3], st[:, 1:2], -1.0, None, ALU.mult), reads=[st], writes=[st])
        yield
        kb.op("act", lambda: nc.scalar.activation(out=RR(P[:, 0:WT]), in_=sm[:, 0:WT], func=AF.Exp, scale=SCALE, bias=st[:, 2:3], accum_out=st[:, 3:4]),
              reads=[sm, st], writes=[P, st])
        kb.op("act", lambda: nc.scalar.activation(out=st[:, 4:5], in_=sink[:, 0:1], func=AF.Exp, scale=1.0, bias=st[:, 2:3]), reads=[sink, st], writes=[st])
        yield
        kb.op("dve", lambda: nc.vector.tensor_tensor(st[:, 5:6], st[:, 3:4], st[:, 4:5], ALU.add), reads=[st], writes=[st])
        kb.op("dve", lambda: nc.vector.reciprocal(st[:, 6:7], st[:, 5:6]), reads=[st], writes=[st])
        nblk = WT // 128
        for i in range(nblk):
            dst = PST[:, i * 128:(i + 1) * 128] if i < 4 else PSc[:, 384:512]
            dst_t = PST if i < 4 else PSc
            kb.op("pe", lambda: nc.tensor.matmul(dst, RR(P[:, i * 128:(i + 1) * 128]), ident_r[:], start=True, stop=True), reads=[P, ident_r], writes=[dst_t])
        n4 = min(4, nblk)
        yield
        kb.op("act", lambda: nc.scalar.copy(out=RR(PT[:, 0:n4, :]), in_=PST[:, 0:n4 * 128].rearrange("p (b q) -> p b q", q=128)), reads=[PST], writes=[PT])
        if nblk > 4:
            kb.op("dve", lambda: nc.vector.tensor_copy(RR(PT[:, 4, :]), PSc[:, 384:512]), reads=[PSc], writes=[PT])
        yield
        vsrc = [(BIG2, vv[:, vb, :]) for vb in vblocks] + [(vc, vc[:, 0, :]), (vc, vc[:, 1, :])]
        PO = PSc
        for i, (vt_, vap) in enumerate(vsrc):
            kb.op("pe", lambda: nc.tensor.matmul(PO[:, 256:384], RR(PT[:, i, :]), RR(vap), start=(i == 0), stop=(i == nblk - 1)), reads=[PT, vt_], writes=[PO], pe_acc=(i > 0))
        yield
        kb.op("dve", lambda: nc.vector.tensor_scalar(oo[:], PO[:, 256:384], st[:, 6:7], None, ALU.mult), reads=[PO, st], writes=[oo])
        if out_cb is None:
            kb.dma("sp", o_sw[out_row0:out_row0 + 128, :], oo[:], reads=[oo])
        else:
            out_cb(kb, out_row0, oo, ident, PSl, PSl[:, 384:512])

    NB = L // 128

    def gens():
        for cb in range(2):
            yield attend(qc, qc[:, cb * 128:(cb + 1) * 128], None, cb * 128)
        for sb_ in range(32):
            qr = qrr.next()

            def pg(qr=qr, sb_=sb_):
                prep(qTd, CTX + sb_ * 512, 512, 0, qr, qr[:, :], sb_ * 512)
                return
                yield
            yield pg()
            for qb in range(4):
                n = sb_ * 4 + qb
                b0 = max(n - 1, 0); b1 = min(n + 1, NB - 1)
                jlo = 0 if n > 0 else 128
                yield attend(qr, qr[:, qb * 128:(qb + 1) * 128], (b0 * 128, (b1 + 1) * 128, jlo, list(range(b0, b1 + 1))), CTX + n * 128)
    interleave(gens(), 2)
    if own:
        kb.finish("sp")
        kb.close()
    return kb, ins


def swa_consts():
    m = {"c_ident": np.eye(128, dtype=np.float32)}
    d = np.arange(128)
    partner = np.where(d % 64 < 32, d + 32, d - 32)
    perm = np.zeros((128, 128), np.float32); perm[partner, d] = 1.0
    m["s_perm"] = perm
    inv = (np.float32(10000.0) ** (-np.arange(32, dtype=np.float32) / np.float32(32))).astype(np.float32)
    t = np.arange(L)
    row = (t // 64).astype(np.float32); col = (t % 64).astype(np.float32)
    ar = (row[:, None] * inv[None, :]).astype(np.float32).astype(np.float64)
    ac = (col[:, None] * inv[None, :]).astype(np.float32).astype(np.float64)
    C = np.concatenate([np.cos(ar), np.cos(ar), np.cos(ac), np.cos(ac)], axis=1).T
    S = np.concatenate([-np.sin(ar), np.sin(ar), -np.sin(ac), np.sin(ac)], axis=1).T
    m["s_C"] = C.astype(np.float32); m["s_S"] = S.astype(np.float32)
    i = np.arange(128)[:, None]; jj = np.arange(384)[None, :]
    valid = (jj >= i) & (jj <= i + 256)
    m["s_mask"] = np.where(valid, 0.0, -1e30).astype(np.float32)
    return m


def swa_core_inputs(inputs, pl, pc, b, h, consts, ins):
    m = dict(consts)
    g = h // 2
    if pl is not None:
        cat = lambda o: np.concatenate([pc[b][:, o:o + 128], pl[b][:, o:o + 128]], axis=0)
        m["s_qT"] = cat(2064 + h * 128).T
        m["s_kT"] = cat(2064 + 512 + g * 128).T
        m["s_v"] = cat(2064 + 768 + g * 128)
    m["s_wqk"] = np.stack([inputs["swa_q_norm_w"][0], inputs["swa_k_norm_w"][0]], axis=1)
    m["s_sink"] = np.full((128, 1), inputs["swa_sink"][0][h], np.float32)
    return {k: np.ascontiguousarray(m[k], dtype=np.float32) for k in ins if k in m}


L = 16384
NCH = 130
EPS = 1e-6


def dft_consts():
    n = np.arange(128)
    k1 = np.arange(256)
    c = {}
    a1 = 2 * np.pi * np.outer(n, k1) / 256.0
    c["F1cat"] = np.concatenate([np.cos(a1), -np.sin(a1)], axis=1)
    at = 2 * np.pi * np.outer(n, k1) / 32768.0
    twr, twi = np.cos(at), -np.sin(at)
    c["TwRR"] = np.concatenate([twr, twr], axis=1)
    c["TwII"] = np.concatenate([twi, twi], axis=1)
    a2 = 2 * np.pi * np.outer(n, n) / 128.0
    f2r, f2i = np.cos(a2), -np.sin(a2)
    c["F2"] = np.concatenate([f2r, f2i, -f2i], axis=1)
    g2r, g2i = np.cos(a2), np.sin(a2)
    c["G2a"] = np.concatenate([g2r, g2i], axis=1)
    c["G2b"] = np.concatenate([-g2i, g2r], axis=1)
    atc = 2 * np.pi * np.outer(k1, n) / 32768.0
    tcr = np.cos(atc).reshape(2, 128, 128).transpose(1, 0, 2)
    tci = np.sin(atc).reshape(2, 128, 128).transpose(1, 0, 2)
    c["TcR"] = tcr.reshape(128, 256)
    c["TcI"] = tci.reshape(128, 256)
    ag = 2 * np.pi * np.outer(k1, n) / 256.0
    g1r = (np.cos(ag) / 32768.0).reshape(2, 128, 128).transpose(1, 0, 2)
    g1i = (-np.sin(ag) / 32768.0).reshape(2, 128, 128).transpose(1, 0, 2)
    c["G1"] = np.concatenate([g1r.reshape(128, 256), g1i.reshape(128, 256)], axis=1)
    return {k: np.ascontiguousarray(v, dtype=np.float32) for k, v in c.items()}


def build_cd(do_ret=True, do_hy=True, kb=None, pre=None, out_cb=None, hy_out_cb=None, pad_src=None):
    own = kb is None
    if own:
        kb = KB()
    nc = kb.nc
    ins = []
    pre = pre or {}

    def din(name, shape):
        if name in pre:
            return pre[name]
        ins.append(name)
        t_ = kb.dram(name, shape)
        if not own:
            pre[name] = t_
        return t_

    BIG1 = kb.sb("BIG1", [128, 16384])
    BIG2 = kb.sb("BIG2", [128, 16384])
    if do_hy:
        PA = [kb.ps("PA", [128, 2, 512]) for i in range(2)]
        PB = [kb.ps("PB", [128, 2, 512]) for i in range(2)]
    else:
        PS8 = [kb.ps("RP%d" % i, [128, 512]) for i in range(8)]
    ones = kb.sb("ones", [128, 128])
    kb.op("dve", lambda: nc.vector.memset(ones[:], 1.0), writes=[ones])

    RF = mybir.dt.float32r

    def RR(ap):
        return ap.bitcast(RF)

    if do_ret:
        qkT = din("r_qkT", [128, NCH, 256])
        CT = din("r_CT", [128, NCH, 256])
        ST = din("r_ST", [128, NCH, 256])
        vtm = din("r_v", [NCH * 128, 128])
        gtm = din("r_g", [L, 128])
        gnw = din("r_gnw", [128, 128])
        lgt = din("r_logit", [128, 2])
        cperm = din("c_perm", [128, 128])
        cident = din("c_ident", [128, 128])
        cposrow = din("c_posrow", [128, 2, 256])
        cmaskT = din("c_maskT", [128, 2, 128])
        o_ret = kb.dram("o_ret", [L, 128], kind="ExternalOutput") if out_cb is None else None

        perm = kb.sb("perm", [128, 128]); ident = kb.sb("ident", [128, 128])
        posrow = kb.sb("posrow", [128, 2, 256]); maskT = kb.sb("maskT", [128, 2, 128])
        lg = kb.sb("lg", [128, 2]); nlg = kb.sb("nlg", [128, 2]); gC = kb.sb("gC", [128, 2]); qks = kb.sb("qks", [128, 2, 256])
        gw = kb.sb("gw", [128, 128])
        kb.dma("sp", perm[:], cperm[:, :], writes=[perm]); kb.dma("sp", ident[:], cident[:, :], writes=[ident])
        kb.dma("sp", posrow[:], cposrow[:, :, :], writes=[posrow]); kb.dma("sp", maskT[:], cmaskT[:, :, :], writes=[maskT])
        kb.dma("sp", lg[:], lgt[:, :], writes=[lg]); kb.dma("sp", gw[:], gnw[:, :], writes=[gw])
        perm_r = kb.sb("perm_r", [128, 128], dt=RF); ident_r = kb.sb("ident_r", [128, 128], dt=RF)
        kb.op("act", lambda: nc.scalar.copy(out=perm_r[:], in_=perm[:]), reads=[perm], writes=[perm_r])
        kb.op("act", lambda: nc.scalar.copy(out=ident_r[:], in_=ident[:]), reads=[ident], writes=[ident_r])
        qkrr = Ring([kb.sb("qkr%d" % i, [128, 256], dt=RF) for i in range(4)])
        vrr = Ring([kb.sb("vr%d" % i, [128, 128], dt=RF) for i in range(4)])
        kb.op("act", lambda: nc.scalar.activation(out=lg[:], in_=lg[:], func=AF.Exp, scale=-1.0), reads=[lg], writes=[lg])
        kb.op("dve", lambda: nc.vector.tensor_scalar(lg[:], lg[:], 1.0, None, ALU.add), reads=[lg], writes=[lg])
        kb.op("act", lambda: nc.scalar.activation(out=lg[:], in_=lg[:], func=AF.Ln), reads=[lg], writes=[lg])
        kb.op("dve", lambda: nc.vector.tensor_scalar(lg[:], lg[:], -1.0, None, ALU.mult), reads=[lg], writes=[lg])
        for d in range(2):
            kb.op("act", lambda: nc.scalar.activation(out=qks[:, d, :], in_=posrow[:, d, :], func=AF.Exp, scale=lg[:, d:d + 1]),
                  reads=[posrow, lg], writes=[qks])
        kb.op("act", lambda: nc.scalar.activation(out=gC[:], in_=lg[:], func=AF.Exp, scale=128.0), reads=[lg], writes=[gC])

        oacc = BIG1
        qkr_ = Ring([kb.sb("qk%d" % i, [128, 256]) for i in range(6)])
        ctr_ = Ring([kb.sb("ct%d" % i, [128, 256]) for i in range(6)])
        str_ = Ring([kb.sb("st%d" % i, [128, 256]) for i in range(6)])
        vr_ = Ring([kb.sb("v%d" % i, [128, 128]) for i in range(6)])
        t1r = Ring([kb.sb("t1%d" % i, [128, 256]) for i in range(4)])
        t2r = Ring([kb.sb("t2%d" % i, [128, 256]) for i in range(4)])
        ktr = Ring([kb.sb("kt%d" % i, [128, 128]) for i in range(4)])
        ptr = Ring([kb.sb("pt%d" % i, [128, 128]) for i in range(4)])
        Sr = Ring([kb.sb("S%d" % i, [128, 128]) for i in range(2)])

        def psl(i):
            if not do_hy:
                return (PS8[i], PS8[i][:, :])
            return (PA[i // 2], PA[i // 2][:, i % 2, :]) if i < 4 else (PB[(i - 4) // 2], PB[(i - 4) // 2][:, i % 2, :])
        psi = [0]

        def nps():
            i = psi[0]; psi[0] = (i + 1) % 8
            return psl(i)

        def ret_dir(d):
            order = [0, 1] + list(range(2, NCH)) if d == 0 else [1, 0] + list(range(NCH - 1, 1, -1))
            Sr = Ring([kb.sb("S%d_%d" % (d, i), [128, 128]) for i in range(2)])
            S = Sr.next()
            kb.op("dve", lambda: nc.vector.tensor_scalar(RR(S[:]), ident[:], 0.0, None, ALU.mult), reads=[ident], writes=[S])
            for ci in order:
                qk = qkr_.next(); ct = ctr_.next(); st = str_.next(); v = vr_.next()
                kb.dma("sp", qk[:], qkT[:, ci, :], writes=[qk])
                kb.dma("pool", ct[:], CT[:, ci, :], writes=[ct])
                kb.dma("pool", st[:], ST[:, ci, :], writes=[st])
                kb.dma("sp", v[:], vtm[ci * 128:(ci + 1) * 128, :], writes=[v])
                qk_r = qkrr.next(); v_r = vrr.next()
                kb.op("pool", lambda: nc.gpsimd.tensor_copy(qk_r[:], qk[:]), reads=[qk], writes=[qk_r])
                kb.op("act", lambda: nc.scalar.copy(out=v_r[:], in_=v[:]), reads=[v], writes=[v_r])
                pT, p1 = nps()
                kb.op("pe", lambda: nc.tensor.matmul(p1[:, :256], perm_r[:], qk_r[:], start=True, stop=True), reads=[perm_r, qk_r], writes=[pT])
                t1 = t1r.next(); t2 = t2r.next()
                kb.op("pool", lambda: nc.gpsimd.tensor_tensor(RR(t1[:]), qk[:], ct[:], ALU.mult), reads=[qk, ct], writes=[t1])
                kb.op("dve", lambda: nc.vector.tensor_tensor(t2[:], p1[:, :256], st[:], ALU.mult), reads=[pT, st], writes=[t2])
                kb.op("dve", lambda: nc.vector.tensor_tensor(RR(t1[:]), t1[:], t2[:], ALU.add), reads=[t1, t2], writes=[t1])
                kb.op("dve", lambda: nc.vector.tensor_tensor(RR(t1[:]), t1[:], qks[:, d, :], ALU.mult), reads=[t1, qks], writes=[t1])
                qT = RR(t1[:, 0:128]); kT = RR(t1[:, 128:256])
                pT2, p2 = nps()
                kb.op("pe", lambda: nc.tensor.matmul(p2[:, :128], kT, ident_r[:], start=True, stop=True), reads=[t1, ident_r], writes=[pT2])
                kt = ktr.next()
                kb.op("act", lambda: nc.scalar.activation(out=RR(kt[:]), in_=p2[:, :128], func=AF.Copy, scale=gC[:, d:d + 1]), reads=[pT2, gC], writes=[kt])
                if ci >= 2:
                    pT3, p3 = nps()
                    kb.op("pe", lambda: nc.tensor.matmul(p3[:, :128], kT, qT, start=True, stop=True), reads=[t1], writes=[pT3])
                    pt = ptr.next()
                    kb.op("dve", lambda: nc.vector.tensor_tensor(RR(pt[:]), p3[:, :128], maskT[:, d, :], ALU.mult), reads=[pT3, maskT], writes=[pt])
                    pT4, p4 = nps()
                    kb.op("pe", lambda: nc.tensor.matmul(p4[:, :128], RR(pt[:]), v_r[:], start=True, stop=False), reads=[pt, v_r], writes=[pT4])
                    kb.op("pe", lambda: nc.tensor.matmul(p4[:, :128], qT, RR(S[:]), start=False, stop=True), reads=[t1, S], writes=[pT4], pe_acc=True)
                    li = ci - 2
                    if first_write[li]:
                        first_write[li] = False
                        kb.op("act", lambda: nc.scalar.copy(out=oacc[:, li * 128:(li + 1) * 128], in_=p4[:, :128]), reads=[pT4], writes=[oacc])
                    else:
                        kb.op("dve", lambda: nc.vector.tensor_tensor(oacc[:, li * 128:(li + 1) * 128], oacc[:, li * 128:(li + 1) * 128], p4[:, :128], ALU.add),
                              reads=[pT4, oacc], writes=[oacc])
                pT5, p5 = nps()
                kb.op("pe", lambda: nc.tensor.matmul(p5[:, :128], RR(kt[:]), v_r[:], start=True, stop=True), reads=[kt, v_r], writes=[pT5])
                S2 = Sr.next()
                kb.op("dve", lambda: nc.vector.scalar_tensor_tensor(RR(S2[:]), S[:], gC[:, d:d + 1], p5[:, :128], ALU.mult, ALU.add),
                      reads=[S, gC, pT5], writes=[S2])
                S = S2
                yield
        first_write = [True] * 128
        gens = [ret_dir(0), ret_dir(1)]
        while gens:
            for g in list(gens):
                try:
                    next(g)
                except StopIteration:
                    gens.remove(g)
        W_ = 4
        gr = Ring([kb.sb("g%d" % i, [128, 128]) for i in range(W_ + 1)])
        cr = Ring([kb.sb("c%d" % i, [128, 128]) for i in range(W_ + 1)])
        jr = Ring([kb.sb("j%d" % i, [128, 128]) for i in range(W_ + 1)])
        yr = Ring([kb.sb("y%d" % i, [128, 128]) for i in range(W_ + 1)])
        s1r = Ring([kb.sb("s1%d" % i, [128, 4]) for i in range(W_ + 1)])

        def gn_tile(li):
            g = gr.next(); cen = cr.next(); y = yr.next(); st = s1r.next(); jk = jr.next()
            o = oacc[:, li * 128:(li + 1) * 128]
            kb.dma("pool", g[:], gtm[li * 128:(li + 1) * 128, :], writes=[g])
            kb.op("dve", lambda: nc.vector.reduce_sum(st[:, 0:1], o, AX.X), reads=[oacc], writes=[st])
            kb.op("dve", lambda: nc.vector.tensor_scalar(st[:, 1:2], st[:, 0:1], -1.0 / 128, None, ALU.mult), reads=[st], writes=[st])
            kb.op("dve", lambda: nc.vector.tensor_scalar(cen[:], o, st[:, 1:2], None, ALU.add), reads=[oacc, st], writes=[cen])
            yield
            kb.op("act", lambda: nc.scalar.activation(out=jk[:], in_=cen[:], func=AF.Square, accum_out=st[:, 2:3]), reads=[cen], writes=[jk, st])
            yield
            kb.op("dve", lambda: nc.vector.tensor_scalar(st[:, 3:4], st[:, 2:3], 1.0 / 128, EPS, ALU.mult, ALU.add), reads=[st], writes=[st])
            yield
            kb.op("act", lambda: nc.scalar.activation(out=st[:, 3:4], in_=st[:, 3:4], func=AF.Ln), reads=[st], writes=[st])
            kb.op("act", lambda: nc.scalar.activation(out=st[:, 3:4], in_=st[:, 3:4], func=AF.Exp, scale=-0.5), reads=[st], writes=[st])
            kb.op("act", lambda: nc.scalar.activation(out=jk[:], in_=g[:], func=AF.Exp, scale=-1.0), reads=[g], writes=[jk])
            yield
            kb.op("dve", lambda: nc.vector.tensor_scalar(jk[:], jk[:], 1.0, None, ALU.add), reads=[jk], writes=[jk])
            kb.op("dve", lambda: nc.vector.reciprocal(jk[:], jk[:]), reads=[jk], writes=[jk])
            kb.op("dve", lambda: nc.vector.tensor_tensor(g[:], g[:], jk[:], ALU.mult), reads=[g, jk], writes=[g])
            kb.op("dve", lambda: nc.vector.scalar_tensor_tensor(y[:], cen[:], st[:, 3:4], gw[:], ALU.mult, ALU.mult), reads=[cen, st, gw], writes=[y])
            kb.op("dve", lambda: nc.vector.tensor_tensor(y[:], y[:], g[:], ALU.mult), reads=[y, g], writes=[y])
            yield
            if out_cb is None:
                kb.dma("sp", o_ret[li * 128:(li + 1) * 128, :], y[:], reads=[y])
            elif do_hy:
                out_cb(kb, li * 128, y, ident, PB[1], PB[1][:, 1, 0:128])
            else:
                out_cb(kb, li * 128, y, ident, PS8[li % 4], PS8[li % 4][:, 0:128])
        interleave((gn_tile(li) for li in range(128)), W_)

    if do_hy:
        F32R = mybir.dt.float32r

        def R_(ap):
            return ap.bitcast(F32R)
        GC = 2
        NG = 128 // GC
        pads = din("h_pads", [3, 128, 128, 130]) if pad_src is None else None
        cw = din("h_cw", [128, 3, 3, 128])
        skp = din("h_skip", [128, 2, 128])
        zT = din("h_zT", [17, L])
        w1 = din("h_w1", [17, 64]); b1 = din("h_b1", [64, 1]); w2 = din("h_w2", [64, 64]); b2 = din("h_b2", [64, 1])
        w3 = din("h_w3", [64, 4, 128])
        dec = din("h_dec", [128, 128, 128])
        cn = {}
        cstg = kb.sb("cstg", [128, 512])
        for nm, w in (("F1cat", 512), ("TwRR", 512), ("TwII", 512), ("F2", 384), ("G2a", 256), ("G2b", 256), ("TcR", 256), ("TcI", 256), ("G1", 512)):
            if nm in ("F1cat", "F2", "G2a", "G2b", "G1"):
                cn[nm] = (din("c_" + nm, [128, w]), kb.sb("k_" + nm, [128, w], dt=F32R))
                kb.dma("sp", cstg[:, :w], cn[nm][0][:, :], writes=[cstg])
                kb.op("act", lambda: nc.scalar.copy(out=cn[nm][1][:], in_=cstg[:, :w]), reads=[cstg], writes=[cn[nm][1]])
            else:
                cn[nm] = (din("c_" + nm, [128, w]), kb.sb("k_" + nm, [128, w]))
                kb.dma("sp", cn[nm][1][:], cn[nm][0][:, :], writes=[cn[nm][1]])
        F1cat, TwRR, TwII, F2, G2a, G2b, TcR, TcI, G1 = [cn[k][1] for k in ("F1cat", "TwRR", "TwII", "F2", "G2a", "G2b", "TcR", "TcI", "G1")]
        Hs = kb.dram("h_Hs", [4, NG, 128, GC * 512], kind="Internal")
        o_hy = kb.dram("o_hy", [128, 128, 128], kind="ExternalOutput") if hy_out_cb is None else None

        def bcg(t, w):
            return t[:, :].rearrange("p (o w) -> p o w", o=1).to_broadcast([128, GC, w])

        M1 = [kb.sb("M1", [128, GC, 512]) for i in range(2)]; M2 = [kb.sb("M2", [128, GC, 512]) for i in range(2)]
        Bt = [kb.sb("Bt", [128, GC, 512]) for i in range(2)]
        GH1 = [kb.sb("GH1", [128, GC, 512]) for i in range(2)]; GH2 = [kb.sb("GH2", [128, GC, 512]) for i in range(2)]

        class V:
            def __init__(self, t, pat, **kw):
                self.t = t; self.pat = pat; self.kw = kw
            def __getitem__(self, idx):
                return self.t[:].rearrange(self.pat, **self.kw)[idx]

        def fwd_fft(s, src_t, lhs_of):
            pa, pb, m1, m2, bt = PA[s], PB[s], M1[s], M2[s], Bt[s]
            for c in range(GC):
                kb.op("pe", lambda: nc.tensor.matmul(pa[:, c, :], R_(lhs_of(c)), F1cat[:], start=True, stop=True), reads=[src_t, F1cat], writes=[pa])
            yield
            kb.op("dve", lambda: nc.vector.tensor_tensor(m1[:], pa[:], bcg(TwRR, 512), ALU.mult), reads=[pa, TwRR], writes=[m1])
            kb.op("dve", lambda: nc.vector.tensor_tensor(m2[:], pa[:], bcg(TwII, 512), ALU.mult), reads=[pa, TwII], writes=[m2])
            yield
            kb.op("dve", lambda: nc.vector.tensor_tensor(R_(bt[:, :, 0:256]), m1[:, :, 0:256], m2[:, :, 256:512], ALU.subtract), reads=[m1, m2], writes=[bt])
            kb.op("pool", lambda: nc.gpsimd.tensor_tensor(R_(bt[:, :, 256:512]), m2[:, :, 0:256], m1[:, :, 256:512], ALU.add), reads=[m1, m2], writes=[bt])
            yield
            for c in range(GC):
                kb.op("pe", lambda: nc.tensor.matmul(pb[:, c, 0:256], F2[:, 0:128], R_(bt[:, c, 0:256]), start=True, stop=False), reads=[F2, bt], writes=[pb])
                kb.op("pe", lambda: nc.tensor.matmul(pb[:, c, 0:256], F2[:, 256:384], R_(bt[:, c, 256:512]), start=False, stop=True), reads=[F2, bt], writes=[pb], pe_acc=True)
                kb.op("pe", lambda: nc.tensor.matmul(pb[:, c, 256:512], F2[:, 128:256], R_(bt[:, c, 0:256]), start=True, stop=False), reads=[F2, bt], writes=[pb], pe_acc=True)
                kb.op("pe", lambda: nc.tensor.matmul(pb[:, c, 256:512], F2[:, 0:128], R_(bt[:, c, 256:512]), start=False, stop=True), reads=[F2, bt], writes=[pb], pe_acc=True)
            yield

        h1T = BIG2
        h2T = BIG1
        taps = BIG2
        w1s = kb.sb("w1s", [17, 64]); w2s = kb.sb("w2s", [64, 64]); w3s = kb.sb("w3s", [64, 512])
        b1s = kb.sb("b1s", [64, 1]); b2s = kb.sb("b2s", [64, 1])
        kb.dma("sp", w1s[:], w1[:, :], writes=[w1s]); kb.dma("sp", w2s[:], w2[:, :], writes=[w2s])
        kb.dma("sp", w3s[:], w3.rearrange("k g c -> k (g c)"), writes=[w3s])
        kb.dma("sp", b1s[:], b1[:, :], writes=[b1s]); kb.dma("sp", b2s[:], b2[:, :], writes=[b2s])
        kb.op("dve", lambda: nc.vector.tensor_scalar(b1s[:], b1s[:], 1.0 / 3, None, ALU.mult), reads=[b1s], writes=[b1s])
        kb.op("dve", lambda: nc.vector.tensor_scalar(b2s[:], b2s[:], 1.0 / 3, None, ALU.mult), reads=[b2s], writes=[b2s])
        ztr = Ring([kb.sb("zt%d" % i, [17, 512]) for i in range(2)])
        sr_ = Ring([M1[0], Bt[0], M1[1], Bt[1]])
        s2r_ = Ring([M2[0], GH1[0], M2[1], GH1[1]])

        def sin3(ps_t, ps_ap, bias, out_t, out_ap):
            s_t = sr_.next(); q_t = s2r_.next()
            s = V(s_t, "p c w -> p (c w)")[0:64, 0:512]; q = V(q_t, "p c w -> p (c w)")[0:64, 0:512]
            kb.op("act", lambda: nc.scalar.activation(out=R_(s), in_=ps_ap, func=AF.Sin, scale=1.0 / 3, bias=bias[:, 0:1]), reads=[ps_t, bias], writes=[s_t])
            kb.op("dve", lambda: nc.vector.tensor_tensor(q, s, s, ALU.mult), reads=[s_t], writes=[q_t])
            kb.op("dve", lambda: nc.vector.tensor_scalar(q, q, -4.0, 3.0, ALU.mult, ALU.add), reads=[q_t], writes=[q_t])
            kb.op("dve", lambda: nc.vector.tensor_tensor(R_(out_ap), q, s, ALU.mult), reads=[q_t, s_t], writes=[out_t])

        for blk in range(32):
            z = ztr.next()
            kb.dma("sp", z[:], zT[:, blk * 512:(blk + 1) * 512], writes=[z])
            pa = PA[blk % 2]
            kb.op("pe", lambda: nc.tensor.matmul(pa[0:64, 0, :], w1s[:], z[:], start=True, stop=True), reads=[w1s, z], writes=[pa])
            sin3(pa, pa[0:64, 0, :], b1s, h1T, h1T[0:64, blk * 512:(blk + 1) * 512])
        for blk in range(32):
            pa = PB[blk % 2]
            kb.op("pe", lambda: nc.tensor.matmul(pa[0:64, 1, :], w2s[:], h1T[0:64, blk * 512:(blk + 1) * 512], start=True, stop=True), reads=[w2s, h1T], writes=[pa])
            sin3(pa, pa[0:64, 1, :], b2s, h2T, h2T[0:64, blk * 512:(blk + 1) * 512])

        rsum = kb.sb("rsum", [128, 128]); rtmp = kb.sb("rtmp", [128, 128])
        rn = kb.sb("rn", [128, 2, 128])
        tapsv = taps[:, :].rearrange("p (n c) -> p n c", c=128)
        for o in range(2):
            for d in range(2):
                gi = o * 2 + d
                for nb in range(16):
                    s_ = nb % 2
                    dc_t = GH1[s_]; dc = V(dc_t, "p c (a n) -> p (c a) n", n=128)
                    kb.dma("pool", dc[:], dec[:, nb * 8:(nb + 1) * 8, :], writes=[dc_t])
                    pa = PA[s_]
                    for q2 in range(2):
                        for j in range(4):
                            n2 = nb * 8 + q2 * 4 + j
                            kb.op("pe", lambda: nc.tensor.matmul(pa[:, q2, j * 128:(j + 1) * 128], h2T[0:64, n2:L:128], w3s[:, gi * 128:(gi + 1) * 128],
                                                                 start=True, stop=True), reads=[h2T, w3s], writes=[pa])
                    kb.op("dve", lambda: nc.vector.tensor_tensor(R_(tapsv[:, nb * 8:(nb + 1) * 8, :]), pa[:].rearrange("p a (j c) -> p (a j) c", c=128), dc[:], ALU.mult),
                          reads=[pa, dc_t], writes=[taps])
                    ab_t = M1[s_]; ab = V(ab_t, "p c (a n) -> p (c a) n", n=128)
                    kb.op("act", lambda: nc.scalar.activation(out=ab[:], in_=tapsv[:, nb * 8:(nb + 1) * 8, :], func=AF.Abs), reads=[taps], writes=[ab_t])
                    if nb == 0:
                        kb.op("dve", lambda: nc.vector.reduce_sum(rsum[:], ab[:].rearrange("p n c -> p c n"), AX.X), reads=[ab_t], writes=[rsum])
                    else:
                        kb.op("dve", lambda: nc.vector.reduce_sum(rtmp[:], ab[:].rearrange("p n c -> p c n"), AX.X), reads=[ab_t], writes=[rtmp])
                        kb.op("dve", lambda: nc.vector.tensor_tensor(rsum[:], rsum[:], rtmp[:], ALU.add), reads=[rsum, rtmp], writes=[rsum])
                pt_ = PB[0]
                kb.op("pe", lambda: nc.tensor.matmul(pt_[:, 0, 0:128], ones[:], rsum[:], start=True, stop=True), reads=[ones, rsum], writes=[pt_])
                if d == 0:
                    kb.op("dve", lambda: nc.vector.tensor_copy(rn[:, o, :], pt_[:, 0, 0:128]), reads=[pt_], writes=[rn])
                else:
                    kb.op("dve", lambda: nc.vector.tensor_tensor(rn[:, o, :], rn[:, o, :], pt_[:, 0, 0:128], ALU.add), reads=[pt_, rn], writes=[rn])
                    kb.op("dve", lambda: nc.vector.tensor_scalar(rn[:, o, :], rn[:, o, :], EPS, None, ALU.add), reads=[rn], writes=[rn])
                    kb.op("dve", lambda: nc.vector.reciprocal(rn[:, o, :], rn[:, o, :]), reads=[rn], writes=[rn])
                    kb.op("dve", lambda: nc.vector.tensor_scalar(R_(tapsv[0:1, 0:1, :]), tapsv[0:1, 0:1, :], 0.0, None, ALU.mult), reads=[taps], writes=[taps])

                def filt_group(gi, cg):
                    s_ = cg % 2
                    yield from fwd_fft(s_, taps, lambda c: tapsv[:, :, cg * GC + c])
                    hs = GH2[s_]
                    kb.op("act", lambda: nc.scalar.copy(out=R_(hs[:]), in_=PB[s_][:]), reads=[PB[s_]], writes=[hs])
                    yield
                    kb.dma("sp", Hs[gi, cg].rearrange("p (c w) -> p c w", w=512), hs[:], reads=[hs])
                interleave((filt_group(gi, cg) for cg in range(NG)), 2)
        hs_done = [(k, c) for k, c in kb.cnt.items() if k.startswith("dsp") and c]

        u = BIG1[:, :].rearrange("p (c n) -> p c n", n=128)
        zz = BIG2[:, :].rearrange("p (c n) -> p c n", n=128)
        cws = kb.sb("cws", [128, 9, 128]); sks = kb.sb("sks", [128, 2, 128])
        kb.dma("sp", cws[:], cw.rearrange("p a k c -> p (a k) c"), writes=[cws]); kb.dma("sp", sks[:], skp[:, :, :], writes=[sks])
        pad_ = [kb.sb("pad", [128, GC, 130]) for i in range(2)]
        cv_ = [kb.sb("cv", [128, GC, 128]) for i in range(2)]
        cv2_ = [kb.sb("cv2", [128, GC, 128]) for i in range(2)]
        tt__ = [kb.sb("tt", [128, GC, 128]) for i in range(2)]

        def wb(part, k, cg):
            return cws[:, part * 3 + k, cg * GC:(cg + 1) * GC].rearrange("p (c o) -> p c o", o=1).to_broadcast([128, GC, 128])

        def sconv(s_, part, cg, out_t, out_ap):
            pd = pad_[s_]; cv2 = cv2_[s_]
            if pad_src is None:
                kb.dma("pool", pd[:], pads[part, :, cg * GC:(cg + 1) * GC, :], writes=[pd])
            else:
                for c in range(GC):
                    kb.dma("pool", pd[:, c, :], pad_src(part, cg * GC + c), writes=[pd])
            kb.op("pool", lambda: nc.gpsimd.tensor_tensor(cv2[:], pd[:, :, 0:128], wb(part, 0, cg), ALU.mult), reads=[pd, cws], writes=[cv2])
            kb.op("dve", lambda: nc.vector.tensor_tensor(R_(out_ap), pd[:, :, 1:129], wb(part, 1, cg), ALU.mult), reads=[pd, cws], writes=[out_t])
            kb.op("dve", lambda: nc.vector.tensor_tensor(R_(out_ap), out_ap, cv2[:], ALU.add), reads=[out_t, cv2], writes=[out_t])
            kb.op("pool", lambda: nc.gpsimd.tensor_tensor(cv2[:], pd[:, :, 2:130], wb(part, 2, cg), ALU.mult), reads=[pd, cws], writes=[cv2])
            kb.op("dve", lambda: nc.vector.tensor_tensor(R_(out_ap), out_ap, cv2[:], ALU.add), reads=[out_t, cv2], writes=[out_t])

        for cg in range(NG):
            sconv(cg % 2, 0, cg, BIG1, u[:, cg * GC:(cg + 1) * GC, :])
        srcs = [(BIG1, u), (BIG2, zz)]

        def data_group(o, cg, src_t, src, dst_t, dst):
            s_ = cg % 2
            pa, pb, m1, m2, bt = PA[s_], PB[s_], M1[s_], M2[s_], Bt[s_]
            hf = GH1[s_]; hb = m2; Hc = hf; Yt = bt; Zp_t = GH2[s_]
            kb.dma("pool", hf[:], Hs[o * 2 + 0, cg].rearrange("p (c w) -> p c w", w=512), writes=[hf])
            kb.dma("pool", hb[:], Hs[o * 2 + 1, cg].rearrange("p (c w) -> p c w", w=512), writes=[hb])
            rb = rn[:, o, cg * GC:(cg + 1) * GC].rearrange("p (c o) -> p c o", o=1).to_broadcast([128, GC, 256])
            kb.op("pool", lambda: nc.gpsimd.tensor_tensor(Hc[:, :, 0:256], hf[:, :, 0:256], hb[:, :, 0:256], ALU.add), reads=[hf, hb], writes=[Hc])
            kb.op("pool", lambda: nc.gpsimd.tensor_tensor(Hc[:, :, 256:512], hf[:, :, 256:512], hb[:, :, 256:512], ALU.subtract), reads=[hf, hb], writes=[Hc])
            yield
            kb.op("pool", lambda: nc.gpsimd.tensor_tensor(Hc[:, :, 0:256], Hc[:, :, 0:256], rb, ALU.mult), reads=[Hc, rn], writes=[Hc])
            kb.op("pool", lambda: nc.gpsimd.tensor_tensor(Hc[:, :, 256:512], Hc[:, :, 256:512], rb, ALU.mult), reads=[Hc, rn], writes=[Hc])
            yield from fwd_fft(s_, src_t, lambda c: src[:, cg * GC + c, :])
            HRR = Hc[:, :, 0:256].rearrange("p c (o w) -> p c o w", o=1).to_broadcast([128, GC, 2, 256])
            HII = Hc[:, :, 256:512].rearrange("p c (o w) -> p c o w", o=1).to_broadcast([128, GC, 2, 256])
            PBv = pb[:].rearrange("p c (o w) -> p c o w", o=2)
            kb.op("dve", lambda: nc.vector.tensor_tensor(m1[:].rearrange("p c (o w) -> p c o w", o=2), PBv, HRR, ALU.mult), reads=[pb, Hc], writes=[m1])
            kb.op("dve", lambda: nc.vector.tensor_tensor(m2[:].rearrange("p c (o w) -> p c o w", o=2), PBv, HII, ALU.mult), reads=[pb, Hc], writes=[m2])
            yield
            kb.op("dve", lambda: nc.vector.tensor_tensor(R_(Yt[:, :, 0:256]), m1[:, :, 0:256], m2[:, :, 256:512], ALU.subtract), reads=[m1, m2], writes=[Yt])
            kb.op("pool", lambda: nc.gpsimd.tensor_tensor(R_(Yt[:, :, 256:512]), m2[:, :, 0:256], m1[:, :, 256:512], ALU.add), reads=[m1, m2], writes=[Yt])
            yield
            for c in range(GC):
                for hf_ in range(2):
                    kb.op("pe", lambda: nc.tensor.matmul(pa[:, c, hf_ * 256:(hf_ + 1) * 256], R_(Yt[:, c, hf_ * 128:(hf_ + 1) * 128]), G2a[:], start=True, stop=False),
                          reads=[Yt, G2a], writes=[pa], pe_acc=(c + hf_ > 0))
                    kb.op("pe", lambda: nc.tensor.matmul(pa[:, c, hf_ * 256:(hf_ + 1) * 256], R_(Yt[:, c, 256 + hf_ * 128:256 + (hf_ + 1) * 128]), G2b[:], start=False, stop=True),
                          reads=[Yt, G2b], writes=[pa], pe_acc=True)
            yield
            ZpV = Zp_t[:].rearrange("p a w -> p (a w)").rearrange("p (h r c n) -> p h r c n", h=2, r=2, c=GC)
            PAv = pa[:].rearrange("p c (h r n) -> p h r c n", h=2, r=2)
            TR = TcR[:, :].rearrange("p (h o n) -> p h o n", h=2, o=1).to_broadcast([128, 2, GC, 128])
            TI = TcI[:, :].rearrange("p (h o n) -> p h o n", h=2, o=1).to_broadcast([128, 2, GC, 128])
            M1v = m1[:].rearrange("p c (h r n) -> p h r c n", h=2, r=2)
            M2v = m2[:].rearrange("p c (h r n) -> p h r c n", h=2, r=2)
            for r in range(2):
                kb.op("dve", lambda: nc.vector.tensor_tensor(M1v[:, :, r], PAv[:, :, r], TR, ALU.mult), reads=[pa, TcR], writes=[m1])
                kb.op("dve", lambda: nc.vector.tensor_tensor(M2v[:, :, r], PAv[:, :, r], TI, ALU.mult), reads=[pa, TcI], writes=[m2])
            yield
            kb.op("dve", lambda: nc.vector.tensor_tensor(R_(ZpV[:, :, 0]), M1v[:, :, 0], M2v[:, :, 1], ALU.subtract), reads=[m1, m2], writes=[Zp_t])
            kb.op("pool", lambda: nc.gpsimd.tensor_tensor(R_(ZpV[:, :, 1]), M2v[:, :, 0], M1v[:, :, 1], ALU.add), reads=[m1, m2], writes=[Zp_t])
            yield
            i = 0
            for r in range(2):
                for hf_ in range(2):
                    kb.op("pe", lambda: nc.tensor.matmul(pb[:, 0, 0:GC * 128], G1[:, r * 256 + hf_ * 128: r * 256 + (hf_ + 1) * 128],
                                                         R_(ZpV[:, hf_, r].rearrange("p c n -> p (c n)")), start=(i == 0), stop=(i == 3)),
                          reads=[G1, Zp_t], writes=[pb], pe_acc=(i > 0))
                    i += 1
            yield
            yv = pb[:, 0, 0:GC * 128].rearrange("p (c n) -> p c n", n=128)
            tt_ = tt__[s_]; cv = cv_[s_]
            sb_ = sks[:, o, cg * GC:(cg + 1) * GC].rearrange("p (c o) -> p c o", o=1).to_broadcast([128, GC, 128])
            kb.op("pool", lambda: nc.gpsimd.tensor_tensor(tt_[:], src[:, cg * GC:(cg + 1) * GC, :], sb_, ALU.mult), reads=[src_t, sks], writes=[tt_])
            kb.op("dve", lambda: nc.vector.tensor_tensor(tt_[:], tt_[:], yv, ALU.add), reads=[tt_, pb], writes=[tt_])
            yield
            sconv(s_, o + 1, cg, cv, cv[:])
            kb.op("dve", lambda: nc.vector.tensor_tensor(R_(dst[:, cg * GC:(cg + 1) * GC, :]), tt_[:], cv[:], ALU.mult), reads=[tt_, cv], writes=[dst_t])
            yield

        kb._wait("pool", hs_done)
        for o in range(2):
            src_t, src = srcs[o % 2]
            dst_t, dst = srcs[(o + 1) % 2]
            interleave((data_group(o, cg, src_t, src, dst_t, dst) for cg in range(NG)), 2)
        fin_t, fin = srcs[0]
        if hy_out_cb is None:
            for q in range(4):
                kb.dma("sp", o_hy[:, q * 32:(q + 1) * 32, :], fin[:, q * 32:(q + 1) * 32, :], reads=[fin_t])
        else:
            hy_out_cb(kb, fin_t, fin)
    if own:
        kb.finish("sp")
        kb.close()
    return kb, ins


def cd_consts():
    c = dft_consts()
    m = {"c_" + k: v for k, v in c.items()}
    idx = np.arange(128)
    perm = np.zeros((128, 128), np.float32); perm[(idx + 64) % 128, idx] = 1.0
    m["c_perm"] = perm
    m["c_ident"] = np.eye(128, dtype=np.float32)
    pr = np.zeros((128, 2, 256), np.float32)
    pr[:, 0, :128] = idx + 1; pr[:, 0, 128:] = -(idx + 1.0)
    pr[:, 1, :128] = 128 - idx; pr[:, 1, 128:] = -(128.0 - idx)
    m["c_posrow"] = pr
    mk = np.zeros((128, 2, 128), np.float32)
    mk[:, 0, :] = (idx[None, :] >= idx[:, None])
    mk[:, 1, :] = (idx[None, :] <= idx[:, None])
    m["c_maskT"] = mk
    inv = (np.float32(10000.0) ** (-np.linspace(0.0, 1.0, 64, dtype=np.float32))).astype(np.float32)
    pos = np.concatenate([np.arange(256, dtype=np.float32), np.arange(L, dtype=np.float32)])
    ang = (pos[:, None] * inv[None, :]).astype(np.float32).astype(np.float64)
    cos = np.cos(ang).T; sin = np.sin(ang).T
    Cf = np.concatenate([cos, cos], axis=0); Sf = np.concatenate([-sin, sin], axis=0)
    ks = 128.0 ** -0.5
    CT = np.stack([Cf.reshape(128, NCH, 128), Cf.reshape(128, NCH, 128) * ks], axis=2).reshape(128, NCH, 256)
    ST = np.stack([Sf.reshape(128, NCH, 128), Sf.reshape(128, NCH, 128) * ks], axis=2).reshape(128, NCH, 256)
    m["r_CT"] = CT.astype(np.float32); m["r_ST"] = ST.astype(np.float32)
    p = np.arange(L, dtype=np.float32)
    t = (p / np.float32(L - 1)).astype(np.float32)
    bands = np.linspace(1e-4, 7, 8, dtype=np.float32)
    phase = (np.float32(2.0 * np.pi / L) * p[:, None] * bands[None, :]).astype(np.float32).astype(np.float64)
    z = np.concatenate([t[:, None].astype(np.float64), np.cos(phase), -np.sin(phase)], axis=-1)
    m["h_zT"] = np.ascontiguousarray(z.T, dtype=np.float32)
    return m


def cd_dec(h):
    p = np.arange(L, dtype=np.float32)
    t = (p / np.float32(L - 1)).astype(np.float32)
    rates = np.abs(np.linspace(np.log(1e-2) / 1.5, np.log(1e-2) / 0.3, 512, dtype=np.float32))[h * 128:(h + 1) * 128]
    d = np.exp(-(t[:, None] * rates[None, :]).astype(np.float32).astype(np.float64))
    return np.ascontiguousarray(d.reshape(128, 128, 128), dtype=np.float32)


def cd_core_inputs(inputs, pl, pc, b, h, consts, ins):
    m = dict(consts)
    sl = lambda a, o: a[b][:, o + h * 128:o + (h + 1) * 128]
    if pl is not None:
        q = np.concatenate([sl(pc, 0), sl(pl, 0)], axis=0)
        k = np.concatenate([sl(pc, 512), sl(pl, 512)], axis=0)
        qk = np.stack([q.T.reshape(128, NCH, 128), k.T.reshape(128, NCH, 128)], axis=2).reshape(128, NCH, 256)
        m["r_qkT"] = np.ascontiguousarray(qk)
        m["r_v"] = np.ascontiguousarray(np.concatenate([sl(pc, 1024), sl(pl, 1024)], axis=0))
        m["r_g"] = np.ascontiguousarray(sl(pl, 1536))
    m["r_gnw"] = np.ascontiguousarray(np.broadcast_to(inputs["ret_gn_w"][0][h * 128:(h + 1) * 128], (128, 128)))
    m["r_logit"] = np.ascontiguousarray(np.broadcast_to(inputs["ret_decay_logit"][0][:, h], (128, 2)))
    pads = np.zeros((3, 128, 128, 130), np.float32)
    for part in (range(3) if pl is not None else ()):
        a = pl[b][:, 2048 + part * 512 + h * 128: 2048 + part * 512 + (h + 1) * 128]
        ap = np.zeros((L + 2, 128), np.float32); ap[1:L + 1] = a
        i0 = (np.arange(128)[:, None] * 128 + np.arange(130)[None, :])
        pads[part] = ap[i0].transpose(0, 2, 1)
    if pl is not None:
        m["h_pads"] = pads
    cwf = inputs["hy_conv_w"][0]
    cw = np.stack([cwf[:, part * 512 + h * 128: part * 512 + (h + 1) * 128] for part in range(3)], axis=0)
    m["h_cw"] = np.ascontiguousarray(np.broadcast_to(cw, (128, 3, 3, 128)))
    m["h_skip"] = np.ascontiguousarray(np.broadcast_to(inputs["hy_bias"][0][:, h * 128:(h + 1) * 128], (128, 2, 128)))
    m["h_w1"] = inputs["hy_f_w1"][0]; m["h_b1"] = inputs["hy_f_b1"][0].reshape(64, 1)
    m["h_w2"] = inputs["hy_f_w2"][0]; m["h_b2"] = inputs["hy_f_b2"][0].reshape(64, 1)
    w3 = inputs["hy_f_w3"][0].reshape(64, 2, 2, 512)[:, :, :, h * 128:(h + 1) * 128].reshape(64, 4, 128)
    m["h_w3"] = np.ascontiguousarray(w3)
    m["h_dec"] = cd_dec(h)
    return {k: np.ascontiguousarray(m[k], dtype=np.float32) for k in ins if k in m}


def cd_assemble(results, do_ret=True, do_hy=True):
    o = np.zeros((2, L, 1024), np.float32)
    for c, r in enumerate(results):
        b, h = c // 4, c % 4
        if do_ret:
            o[b, :, h * 128:(h + 1) * 128] = r["o_ret"]
        if do_hy:
            o[b, :, 512 + h * 128: 512 + (h + 1) * 128] = r["o_hy"].transpose(0, 2, 1).reshape(L, 128)
    return o


GROUPS = [[0, 1, 2, 3], [4, 5, 6, 7]]
F32R = mybir.dt.float32r


def R_(ap):
    return ap.bitcast(F32R)

HCH = [(0, 64)] + [(64 + 256 * i, 256) for i in range(16)]
OCH = [(0, 256)] + [(256 + 1024 * i, 1024) for i in range(16)]
BLK_ALL = [(0, 64, 1)] + [(64 + i * 512, 512, 0) for i in range(8)]
BLK_LAT = [(64 + i * 512, 512, 0) for i in range(8)]
SEQ = CTX + L


def idram(kb, name, shape):
    return kb.nc.dram_tensor(name, list(shape), F32, kind="Internal").ap()


def dense_phase(kb, P, inputs_list, *, x_src, x_dst, blocks, layer_a, layer_b, og, hx, hg, mos=None):
    nc = kb.nc
    has_a = layer_a is not None
    has_b = layer_b is not None

    def din(name, shape):
        nm = P + name
        inputs_list.append(nm)
        return kb.dram(nm, shape)

    csT = din("csT", [128, 16])
    if has_a:
        w_out = din("w_out", [8, 128, KC * 128])
        nffn = din("nffn", [128, 8])
        f_in = din("f_in", [44, 128, KC * 128]); f_out = din("f_out", [8, 128, 22 * 128])
        selv = din("selv", [128, 4])
    if has_b:
        modw_b = din("modw_b", [48, 128, KC * 128]); modb_b = din("modb_b", [128, 48])
        nmix = din("nmix", [128, 8])

    WT = 11 * 128
    wring = Ring([kb.sb("w", [128, WT]) for i in range(6)])
    wrr = Ring([kb.sb("wr", [128, WT], dt=F32R) for i in range(6)])
    psr = Ring([kb.ps("ps", [128, 512]) for i in range(7)])
    psm = kb.ps("psm", [128, 512])
    xr = Ring([kb.sb("x", [128, KC, 512]) for i in range(1 if has_a else 2)])
    ht = kb.sb("ht", [128, KC, 512])
    sq = kb.sb("sq", [128, KC, 512]) if not has_a else None
    rr = kb.sb("rr", [128, 512])
    ones = kb.sb("ones", [128, 128])
    cs = kb.sb("cs", [128, 16])
    tmpr = Ring([kb.sb("tmp", [128, 512]) for i in range(3)])
    if has_a:
        candr = Ring([kb.sb("cand", [128, KC, 512]) for i in range(2)])
        sq = candr.ts[0]
        candr.i = 1
        actT = kb.sb("actT", [128, 22, 512])

        ot_v = kb.sb("ot", [128, KC, 512])
        sel = kb.sb("sel", [128, 4])
        kb.dma("sp", sel[:], selv[:, :], writes=[sel])
    wqi = [0]

    def wload(src_ap, width, rounded=True):
        w = wring.next()
        q = ("sp", "pool")[wqi[0] % 2]; wqi[0] += 1
        kb.dma(q, w[:, :width], src_ap, writes=[w])
        if not rounded:
            return w
        wr = wrr.next()
        if wqi[0] % 2:
            kb.op("act", lambda: nc.scalar.copy(out=wr[:, :width], in_=w[:, :width]), reads=[w], writes=[wr])
        else:
            kb.op("pool", lambda: nc.gpsimd.tensor_copy(wr[:, :width], w[:, :width]), reads=[w], writes=[wr])
        return wr

    kb.op("dve", lambda: nc.vector.memset(ones[:], 1.0 / D), writes=[ones])
    kb.dma("sp", cs[:], csT[:, :], writes=[cs])
    kb.op("act", lambda: nc.scalar.activation(out=cs[:], in_=cs[:], func=AF.Silu), reads=[cs], writes=[cs])

    def mod_load(layer, name):
        mo = kb.sb(name, [128, 48, 2])
        kb.dma("sp", mo[:], mos[layer][:, :].rearrange("p (n s) -> p n s", s=2), writes=[mo])
        return mo

    def mod_compute(modw, modb, name, layer):
        mb = kb.sb(name + "_b", [128, 48])
        mo = kb.sb(name, [128, 48, 2])
        kb.dma("sp", mb[:], modb[:, :], writes=[mb])
        for n in range(48):
            w = wload(modw[n], KC * 128, rounded=False)
            for k in range(KC):
                kb.op("pe", lambda: nc.tensor.matmul(psm[:, n * 2:n * 2 + 2], w[:, k * 128:(k + 1) * 128], cs[:, k * 2:k * 2 + 2],
                                                     start=(k == 0), stop=(k == KC - 1)),
                      reads=[w, cs], writes=[psm], pe_acc=True)
        for s in range(2):
            kb.op("dve", lambda: nc.vector.tensor_tensor(mo[:, :, s], psm[:, s:96:2], mb[:, :], ALU.add),
                  reads=[psm, mb], writes=[mo])
        kb.dma("sp", mos[layer][:, :].rearrange("p (n s) -> p n s", s=2), mo[:], reads=[mo])
        return mo

    def gs(mo, nw_dram, name, sc):
        nw = kb.sb(name + "_nw", [128, 8])
        G = kb.sb(name + "_G", [128, 8, 2])
        kb.dma("sp", nw[:], nw_dram[:, :], writes=[nw])
        for s in range(2):
            kb.op("dve", lambda: nc.vector.scalar_tensor_tensor(G[:, :, s], mo[:, sc * 8:sc * 8 + 8, s], 1.0, nw[:, :], ALU.add, ALU.mult),
                  reads=[mo, nw], writes=[G])
        return G

    if has_a:
        moA = mod_load(layer_a, "moA")
        G2 = gs(moA, nffn, "g2", 4)
    if has_b:
        moB = mod_compute(modw_b, modb_b, "moB", layer_b)
        G1 = gs(moB, nmix, "g1", 1)

    def norm_mod(xt, N, G, mo, shift_idx, s):
        for k in range(KC):
            kb.op("act", lambda: nc.scalar.activation(out=sq[:, k, :N], in_=xt[:, k, :N], func=AF.Square), reads=[xt], writes=[sq])
        ps = psr.next()
        for k in range(KC):
            kb.op("pe", lambda: nc.tensor.matmul(ps[:, :N], ones[:, :], sq[:, k, :N], start=(k == 0), stop=(k == KC - 1)),
                  reads=[ones, sq], writes=[ps], pe_acc=True)
        kb.op("dve", lambda: nc.vector.tensor_scalar(rr[:, :N], ps[:, :N], EPS, None, ALU.add), reads=[ps], writes=[rr])
        kb.op("act", lambda: nc.scalar.activation(out=rr[:, :N], in_=rr[:, :N], func=AF.Sqrt), reads=[rr], writes=[rr])
        kb.op("dve", lambda: nc.vector.reciprocal(rr[:, :N], rr[:, :N]), reads=[rr], writes=[rr])
        for k in range(KC):
            kb.op("dve", lambda: nc.vector.scalar_tensor_tensor(sq[:, k, :N], xt[:, k, :N], G[:, k, s:s + 1], rr[:, :N], ALU.mult, ALU.mult),
                  reads=[xt, G, rr], writes=[sq])
            kb.op("act", lambda: nc.scalar.activation(out=R_(ht[:, k, :N]), in_=sq[:, k, :N], func=AF.Identity,
                                                      bias=mo[:, shift_idx * 8 + k, s:s + 1], scale=1.0),
                  reads=[sq, mo], writes=[ht])

    xsv = x_src.rearrange("(k p) t -> p k t", p=128)
    xdv = x_dst.rearrange("(k p) t -> p k t", p=128)
    dcol0 = blocks[0][0] if x_dst.shape[1] != NTOK else 0

    for bi, (t0, N, s) in enumerate(blocks):
        xt = xr.next()
        kb.dma("sp", xt[:, :, :N], xsv[:, :, t0:t0 + N], writes=[xt])
        if has_a:
            for rq in range(4):
                cd = candr.next()
                for m_ in range(2):
                    if s == 1:
                        src = og[m_][0][:, rq * 64:(rq + 1) * 64]
                    else:
                        i = (t0 - 64) // 512
                        j = 1 + rq * 4 + i // 2
                        c0 = (i % 2) * 512
                        src = og[m_][j][:, c0:c0 + N]
                    kb.dma("pool" if m_ else "sp", cd[:, m_:KC:2, :N], src.rearrange("(k p) t -> p k t", p=128), writes=[cd])
                if rq == 0:
                    kb.op("dve", lambda: nc.vector.tensor_scalar(R_(ot_v[:, :, :N]), cd[:, :, :N], sel[:, 0:1], None, ALU.mult), reads=[cd, sel], writes=[ot_v])
                else:
                    kb.op("dve", lambda: nc.vector.scalar_tensor_tensor(R_(ot_v[:, :, :N]), cd[:, :, :N], sel[:, rq:rq + 1], ot_v[:, :, :N], ALU.mult, ALU.add),
                          reads=[cd, sel, ot_v], writes=[ot_v])
            for m in range(8):
                w = wload(w_out[m], KC * 128)
                ps = psr.next()
                for k in range(KC):
                    kb.op("pe", lambda: nc.tensor.matmul(ps[:, :N], w[:, k * 128:(k + 1) * 128], R_(ot_v[:, k, :N]), start=(k == 0), stop=(k == KC - 1)),
                          reads=[w, ot_v], writes=[ps], pe_acc=True)
                kb.op("dve", lambda: nc.vector.scalar_tensor_tensor(xt[:, m, :N], ps[:, :N], moA[:, 16 + m, s:s + 1], xt[:, m, :N], ALU.mult, ALU.add),
                      reads=[ps, moA, xt], writes=[xt])
            norm_mod(xt, N, G2, moA, 3, s)
            for j in range(22):
                wg = wload(f_in[j], KC * 128)
                wu = wload(f_in[22 + j], KC * 128)
                pg = psr.next(); pu = psr.next()
                for k in range(KC):
                    kb.op("pe", lambda: nc.tensor.matmul(pg[:, :N], wg[:, k * 128:(k + 1) * 128], R_(ht[:, k, :N]), start=(k == 0), stop=(k == KC - 1)),
                          reads=[wg, ht], writes=[pg], pe_acc=True)
                for k in range(KC):
                    kb.op("pe", lambda: nc.tensor.matmul(pu[:, :N], wu[:, k * 128:(k + 1) * 128], R_(ht[:, k, :N]), start=(k == 0), stop=(k == KC - 1)),
                          reads=[wu, ht], writes=[pu], pe_acc=True)
                tg = tmpr.next()
                kb.op("act", lambda: nc.scalar.activation(out=tg[:, :N], in_=pg[:, :N], func=AF.Silu), reads=[pg], writes=[tg])
                kb.op("dve", lambda: nc.vector.tensor_tensor(R_(actT[:, j, :N]), tg[:, :N], pu[:, :N], ALU.mult), reads=[tg, pu], writes=[actT])
            for m in range(8):
                wa = wload(f_out[m][:, 0:WT], WT)
                wb = wload(f_out[m][:, WT:2 * WT], WT)
                ps = psr.next()
                for j in range(22):
                    w = wa if j < 11 else wb
                    jj = j % 11
                    kb.op("pe", lambda: nc.tensor.matmul(ps[:, :N], w[:, jj * 128:(jj + 1) * 128], R_(actT[:, j, :N]), start=(j == 0), stop=(j == 21)),
                          reads=[w, actT], writes=[ps], pe_acc=True)
                kb.op("dve", lambda: nc.vector.scalar_tensor_tensor(xt[:, m, :N], ps[:, :N], moA[:, 40 + m, s:s + 1], xt[:, m, :N], ALU.mult, ALU.add),
                      reads=[ps, moA, xt], writes=[xt])
            kb.dma("sp", xdv[:, :, t0 - dcol0:t0 - dcol0 + N], xt[:, :, :N], reads=[xt])
        if has_b:
            norm_mod(xt, N, G1, moB, 0, s)
            for ji, (c0, ncol) in enumerate(HCH):
                if c0 < t0 or c0 >= t0 + N:
                    continue
                d = kb.dma("sp", hx[ji].rearrange("p (k t) -> p k t", k=KC), ht[:, :, c0 - t0:c0 - t0 + ncol], reads=[ht])
                kb.allgather(hx[ji], hg[ji], [d], GROUPS)


def fe_phase(kb, P, inputs_list, *, hg, nfm, ntm, fm_dst, tm_dst):
    nc = kb.nc
    nmW = P + "wfm"; nmT = P + "wtm"
    inputs_list += [nmW, nmT]
    wfm_d = kb.dram(nmW, [128, KC, nfm * 128]); wtm_d = kb.dram(nmT, [128, KC, ntm])
    wfm = kb.sb("wfm", [128, KC, nfm * 128], dt=F32R); wtm = kb.sb("wtm", [128, KC, ntm], dt=F32R)
    wtm32 = kb.sb("wtm32", [128, KC, ntm])
    for k in range(KC):
        stg = kb.sb("wstg", [128, nfm * 128]) if k == 0 else stg
        kb.dma("sp", stg[:], wfm_d[:, k, :], writes=[stg])
        kb.op("act", lambda: nc.scalar.copy(out=wfm[:, k, :], in_=stg[:]), reads=[stg], writes=[wfm])
    kb.dma("pool", wtm32[:], wtm_d[:, :, :], writes=[wtm32])
    kb.op("dve", lambda: nc.vector.tensor_copy(wtm[:], wtm32[:]), reads=[wtm32], writes=[wtm])
    WD = 3
    hur = Ring([kb.sb("hu", [128, KC, 256], dt=F32R) for i in range(WD + 1)])
    hu32r = Ring([kb.sb("hu32", [128, KC, 256]) for i in range(WD + 1)])
    psr = Ring([kb.ps("ps", [128, 512]) for i in range(8)])
    fmr = Ring([kb.sb("fmo", [128, 256]) for i in range(6)])
    tmr = Ring([kb.sb("tmo", [128, ntm]) for i in range(4)])

    def unit(ji, c0, ncol, rq):
        kb._wait("sp", [("cc", kb.cc_base + ji + 1)]); kb._wait("pool", [("cc", kb.cc_base + ji + 1)])
        ts0 = rq * 64 if ji == 0 else CTX + rq * 4096 + (c0 - 64)
        hu = hur.next(); h32 = hu32r.next()
        kb.dma("sp" if rq % 2 == 0 else "pool", h32[:, :, :ncol], hg[ji][rq * 128:(rq + 1) * 128, :].rearrange("p (k t) -> p k t", k=KC), writes=[h32])
        if rq % 2 == 0:
            kb.op("pool", lambda: nc.gpsimd.tensor_copy(hu[:, :, :ncol], h32[:, :, :ncol]), reads=[h32], writes=[hu])
        else:
            kb.op("dve", lambda: nc.vector.tensor_copy(hu[:, :, :ncol], h32[:, :, :ncol]), reads=[h32], writes=[hu])
        yield
        for f in range(nfm):
            ps = psr.next()
            for k in range(KC):
                kb.op("pe", lambda: nc.tensor.matmul(ps[:, :ncol], wfm[:, k, f * 128:(f + 1) * 128], hu[:, k, :ncol], start=(k == 0), stop=(k == KC - 1)),
                      reads=[wfm, hu], writes=[ps], pe_acc=True)
            yield
            fo = fmr.next()
            kb.op("act" if f % 2 == 0 else "dve",
                  (lambda: nc.scalar.copy(out=fo[:, :ncol], in_=ps[:, :ncol])) if f % 2 == 0 else (lambda: nc.vector.tensor_copy(fo[:, :ncol], ps[:, :ncol])),
                  reads=[ps], writes=[fo])
            fm_dst(f, ts0, ncol, fo, fo[:, :ncol])
        for sub in range(0, ncol, 128):
            m = min(128, ncol - sub)
            ps = psr.next()
            for k in range(KC):
                kb.op("pe", lambda: nc.tensor.matmul(ps[:m, :ntm], hu[:, k, sub:sub + m] if m == 128 else h32[:, k, sub:sub + m], wtm[:, k, :] if m == 128 else wtm32[:, k, :], start=(k == 0), stop=(k == KC - 1)),
                      reads=[wtm, hu, wtm32, h32], writes=[ps], pe_acc=True)
            yield
            to = tmr.next()
            kb.op("act", lambda: nc.scalar.copy(out=to[:m, :], in_=ps[:m, :ntm]), reads=[ps], writes=[to])
            tm_dst(ts0 + sub, m, to)
    interleave((unit(ji, c0, ncol, rq) for ji, (c0, ncol) in enumerate(HCH) for rq in range(4)), WD)


def build_fused():
    kb = KB()
    nc = kb.nc
    ins = []
    pre = {}
    kb.cc_base = 0
    xT = kb.dram("xT", [D, NTOK]); ins.append("xT")
    xo = kb.dram("xo", [D, 4096], kind="ExternalOutput")
    x1s = idram(kb, "x1s", [D, NTOK])
    hx = [[idram(kb, "hx%d_%d" % (l, j), [128, KC * n]) for j, (c0, n) in enumerate(HCH)] for l in range(2)]
    hg = [[idram(kb, "hg%d_%d" % (l, j), [4 * 128, KC * n]) for j, (c0, n) in enumerate(HCH)] for l in range(2)]
    ox = [[[idram(kb, "ox%d_%d_%d" % (l, m, j), [128, n]) for j, (c0, n) in enumerate(OCH)] for m in range(2)] for l in range(2)]
    og = [[[idram(kb, "og%d_%d_%d" % (l, m, j), [4 * 128, n]) for j, (c0, n) in enumerate(OCH)] for m in range(2)] for l in range(2)]
    zt = None
    mos = [idram(kb, "mos%d" % l, [128, 96]) for l in range(2)]

    def phase_end(mark):
        kb.barrier()
        kb.release(mark)

    def och_of(ts):
        if ts < CTX:
            return 0, ts
        tl = ts - CTX
        return 1 + tl // 1024, tl % 1024

    def make_out_cb(layer, mrow):
        pend = {}

        def cb(kb_, row0, tile, ident, ps_t, ps_ap):
            j, col = och_of(row0)
            kb_.op("pe", lambda: nc.tensor.matmul(ps_ap, tile[:], ident[:], start=True, stop=True), reads=[tile, ident], writes=[ps_t])
            tr = cb.ring.next()
            kb_.op("act", lambda: nc.scalar.copy(out=tr[:], in_=ps_ap), reads=[ps_t], writes=[tr])
            d = kb_.dma("sp", ox[layer][mrow][j][:, col:col + 128], tr[:], reads=[tr])
            pend.setdefault(j, []).append(d)
            if len(pend[j]) == OCH[j][1] // 128:
                kb_.allgather(ox[layer][mrow][j], og[layer][mrow][j], pend[j], GROUPS)
        return cb

    mk = kb.mark()
    dense_phase(kb, "D0_", ins, x_src=xT, x_dst=xT, blocks=BLK_ALL, layer_a=None, layer_b=0, og=None, hx=hx[0], hg=hg[0], mos=mos)
    phase_end(mk)
    PADW = CTX + 2 + L + 2
    S0 = {"d_qpad": idram(kb, "d_qpad", [128, PADW]), "d_kpad": idram(kb, "d_kpad", [128, PADW]), "d_vpad": idram(kb, "d_vpad", [PADW, 128]),
          "d_gate": idram(kb, "d_gate", [SEQ, 128]), "d_ab": idram(kb, "d_ab", [64, NCK, 4]),
          "s_qT": idram(kb, "s_qT", [128, SEQ]), "s_kT": idram(kb, "s_kT", [128, SEQ]), "s_v": idram(kb, "s_v", [SEQ, 128])}
    mk = kb.mark()
    zt = kb.sb("zeros", [128, 128])
    kb.op("dve", lambda: nc.vector.memset(zt[:], 0.0), writes=[zt])
    for c in (0, CTX + 1, CTX + 2, PADW - 1):
        kb.dma("sp", S0["d_qpad"][:, c:c + 1], zt[:, 0:1], reads=[zt], allow_slow_non_contiguous=True)
        kb.dma("sp", S0["d_kpad"][:, c:c + 1], zt[:, 0:1], reads=[zt], allow_slow_non_contiguous=True)
        kb.dma("sp", S0["d_vpad"][c:c + 1, :], zt[0:1, :], reads=[zt])

    def padcol(ts):
        return 1 + ts if ts < CTX else 3 + ts

    def fm0(f, ts0, n, t, ap):
        if f == 0:
            kb.dma("sp", S0["d_qpad"][:, padcol(ts0):padcol(ts0) + n], ap, reads=[t])
        elif f == 1:
            kb.dma("pool", S0["d_kpad"][:, padcol(ts0):padcol(ts0) + n], ap, reads=[t])
        elif f == 2:
            kb.dma("sp", S0["s_qT"][:, ts0:ts0 + n], ap, reads=[t])
        else:
            kb.dma("pool", S0["s_kT"][:, ts0:ts0 + n], ap, reads=[t])

    def tm0(ts0, m, t):
        kb.dma("sp", S0["d_vpad"][padcol(ts0):padcol(ts0) + m, :], t[:m, 0:128], reads=[t])
        kb.dma("pool", S0["d_gate"][ts0:ts0 + m, :], t[:m, 128:256], reads=[t])
        kb.dma("sp", S0["s_v"][ts0:ts0 + m, :], t[:m, 256:384], reads=[t])
        for cc in range(m // 64):
            kb.dma("pool", S0["d_ab"][:, ts0 // 64 + cc, :], t[cc * 64:(cc + 1) * 64, 384:388], reads=[t])

    fe_phase(kb, "F0_", ins, hg=hg[0], nfm=4, ntm=388, fm_dst=fm0, tm_dst=tm0)
    phase_end(mk)
    kb.cc_base = kb.cnt["cc"]
    mk = kb.mark()
    cb = make_out_cb(0, 0); cb.ring = Ring([kb.sb("otr", [128, 128]) for i in range(2)])
    pre.update(S0)
    _, i2 = build_ab(True, False, kb=kb, pre=pre, out_cb=cb); ins += i2
    phase_end(mk)
    mk = kb.mark()
    cb = make_out_cb(0, 1); cb.ring = Ring([kb.sb("otr", [128, 128]) for i in range(2)])
    _, i2 = build_swa(kb=kb, pre=pre, out_cb=cb); ins += i2
    phase_end(mk)
    cc_o0 = kb.cnt["cc"]
    mk = kb.mark()
    kb._wait("sp", [("cc", cc_o0)]); kb._wait("pool", [("cc", cc_o0)])
    kb.cc_base = kb.cnt["cc"]
    dense_phase(kb, "D1_", ins, x_src=xT, x_dst=x1s, blocks=BLK_ALL, layer_a=0, layer_b=1, og=og[0], hx=hx[1], hg=hg[1], mos=mos)
    phase_end(mk)
    S1 = {"r_qkT": idram(kb, "r_qkT", [128, NCH, 256]), "r_v": idram(kb, "r_v", [NCH * 128, 128]), "r_g": idram(kb, "r_g", [L, 128])}
    hp = idram(kb, "h_hp", [3, 128, L + 2])
    mk = kb.mark()
    zt = kb.sb("zeros", [128, 128])
    kb.op("dve", lambda: nc.vector.memset(zt[:], 0.0), writes=[zt])
    for part in range(3):
        for c in (0, L + 1):
            kb.dma("sp", hp[part, :, c:c + 1], zt[:, 0:1], reads=[zt], allow_slow_non_contiguous=True)

    def fm1(f, ts0, n, t, ap):
        if f < 2:
            ci0 = ts0 // 128
            if n >= 128:
                kb.dma("sp" if f == 0 else "pool", S1["r_qkT"][:, ci0:ci0 + n // 128, f * 128:(f + 1) * 128], ap.rearrange("p (c i) -> p c i", i=128), reads=[t])
            else:
                kb.dma("sp", S1["r_qkT"][:, ci0, f * 128 + ts0 % 128: f * 128 + ts0 % 128 + n], ap, reads=[t])
        elif ts0 >= CTX:
            tl = ts0 - CTX
            kb.dma("sp" if f % 2 == 0 else "pool", hp[f - 2, :, 1 + tl:1 + tl + n], ap, reads=[t])

    def tm1(ts0, m, t):
        kb.dma("sp", S1["r_v"][ts0:ts0 + m, :], t[:m, 0:128], reads=[t])
        if ts0 >= CTX:
            kb.dma("pool", S1["r_g"][ts0 - CTX:ts0 - CTX + m, :], t[:m, 128:256], reads=[t])

    fe_phase(kb, "F1_", ins, hg=hg[1], nfm=5, ntm=256, fm_dst=fm1, tm_dst=tm1)
    phase_end(mk)
    pre.update(S1)
    mk = kb.mark()

    def pad_src(part, c):
        base = hp[part, c, 0:130]
        return bass.AP(base.tensor, base.offset, [[128, 128], [1, 130]])

    def hy_out(kb_, fin_t, fin):
        for j in range(1, 17):
            n0 = 8 * (j - 1)
            d = kb_.dma("sp" if j % 2 else "pool", ox[1][1][j][:, :].rearrange("c (a n) -> a c n", n=128), fin[n0:n0 + 8, :, :], reads=[fin_t])
            kb_.allgather(ox[1][1][j], og[1][1][j], [d], GROUPS)
    _, i2 = build_cd(False, True, kb=kb, pre=pre, hy_out_cb=hy_out, pad_src=pad_src); ins += i2
    phase_end(mk)
    mk = kb.mark()

    def ret_cb(kb_, row0, tile, ident, ps_t, ps_ap):
        make_out_cb_l1(kb_, row0 + CTX, tile, ident, ps_t, ps_ap)
    make_out_cb_l1 = make_out_cb(1, 0); make_out_cb_l1.ring = Ring([kb.sb("otr", [128, 128]) for i in range(3)])
    _, i2 = build_cd(True, False, kb=kb, pre=pre, out_cb=ret_cb); ins += i2
    phase_end(mk)
    cc_o1 = kb.cnt["cc"]
    mk = kb.mark()
    kb._wait("sp", [("cc", cc_o1)]); kb._wait("pool", [("cc", cc_o1)])
    dense_phase(kb, "D2_", ins, x_src=x1s, x_dst=xo, blocks=BLK_LAT, layer_a=1, layer_b=None, og=og[1], hx=None, hg=None, mos=mos)
    phase_end(mk)
    kb.finish("sp")
    kb.close()
    seen = set(); out = []
    for n in ins:
        if n not in seen:
            seen.add(n); out.append(n)
    return kb, out


def _sel_w(W, cols):
    Ws = W[:, cols]
    return np.ascontiguousarray(Ws.reshape(8, 128, len(cols)).transpose(1, 0, 2))


def _dense_inputs(inputs, P, layer_a, layer_b, b, r):
    m = {P + "csT": cs_core(inputs, b)}
    if layer_a is not None:
        m[P + "nffn"] = col8(inputs["norm_ffn_w"][layer_a])
        m[P + "f_in"] = tile_w(inputs["ffn_w_in"][layer_a], KC)
        m[P + "f_out"] = tile_w(inputs["ffn_w_out"][layer_a], 22)
        wo = (inputs["ab_w_out"] if layer_a == 0 else inputs["cd_w_out"])[0]
        rows = np.concatenate([np.concatenate([np.arange(q * 128, (q + 1) * 128), np.arange(512 + q * 128, 512 + (q + 1) * 128)]) for q in range(4)])
        m[P + "w_out"] = tile_w(wo[rows, :], KC)
        sel = np.zeros((128, 4), np.float32); sel[:, r] = 1.0
        m[P + "selv"] = sel
    if layer_b is not None:
        m[P + "modw_b"] = tile_w(inputs["mod_w"][layer_b], KC)
        m[P + "modb_b"] = col8(inputs["mod_b"][layer_b])
        m[P + "nmix"] = col8(inputs["norm_mix_w"][layer_b])
    return m


_FUSED = {}


def kernel(**inputs):
    inputs = {k: np.ascontiguousarray(np.asarray(v), dtype=np.float32) for k, v in inputs.items()}
    x, ctx = inputs["x"], inputs["ctx"]
    if "kb" not in _FUSED:
        _FUSED["kb"], _FUSED["ins"] = build_fused()
    kb, ins = _FUSED["kb"], _FUSED["ins"]
    cA = ab_consts(); cS = swa_consts(); cC = cd_consts()
    shared = {}
    for P, la, lb in (("D0_", None, 0), ("D1_", 0, 1), ("D2_", 1, None)):
        shared[(P, 0)] = None
    maps = []
    dense_cache = {}
    for c in range(8):
        b, h = c // 4, c % 4
        g = h // 2
        m = {"xT": shard_tokT(x, ctx, c)}
        for P, la, lb in (("D0_", None, 0), ("D1_", 0, 1), ("D2_", 1, None)):
            key = (P, b, h)
            dm = _dense_inputs(inputs, P, la, lb, b, h)
            for k_, v_ in dm.items():
                ck = (k_, b if k_.endswith("csT") else -1, h if k_.endswith("selv") else -1)
                if ck not in dense_cache:
                    dense_cache[ck] = v_
                m[k_] = dense_cache[ck]
        W0 = inputs["ab_w_in"][0]
        fm0 = np.concatenate([np.arange(h * 128, (h + 1) * 128), np.arange(512 + h * 128, 512 + (h + 1) * 128),
                              np.arange(2064 + h * 128, 2064 + (h + 1) * 128), np.arange(2064 + 512 + g * 128, 2064 + 512 + (g + 1) * 128)])
        tm0 = np.concatenate([np.arange(1024 + h * 128, 1024 + (h + 1) * 128), np.arange(1536 + h * 128, 1536 + (h + 1) * 128),
                              np.arange(2064 + 768 + g * 128, 2064 + 768 + (g + 1) * 128),
                              np.array([2048 + kind * 8 + d * 4 + h for kind in range(2) for d in range(2)])])
        m["F0_wfm"] = _sel_w(W0, fm0); m["F0_wtm"] = _sel_w(W0, tm0)
        W1 = inputs["cd_w_in"][0]
        fm1 = np.concatenate([np.arange(o + h * 128, o + (h + 1) * 128) for o in (0, 512, 2048, 2560, 3072)])
        tm1 = np.concatenate([np.arange(o + h * 128, o + (h + 1) * 128) for o in (1024, 1536)])
        m["F1_wfm"] = _sel_w(W1, fm1); m["F1_wtm"] = _sel_w(W1, tm1)
        m.update(ab_core_inputs(inputs, None, None, b, h, cA, ins))
        m.update(swa_core_inputs(inputs, None, None, b, h, cS, ins))
        m.update(cd_core_inputs(inputs, None, None, b, h, cC, ins))
        missing = [k for k in ins if k not in m]
        assert not missing, missing
        maps.append({k: m[k] for k in ins})
    res = run_bass_kernel_spmd(kb.nc, maps, core_ids=list(range(8)))
    out = np.zeros((2, L, 1024), np.float32)
    for c, q in enumerate(res.results):
        b, r = c // 4, c % 4
        out[b, r * 4096:(r + 1) * 4096] = q["xo"].T
    return out
```

```python
import numpy as np
import concourse.bass as bass
import concourse.mybir as mybir
from concourse.bass_utils import run_bass_kernel_spmd

F32 = mybir.dt.float32
BF16 = mybir.dt.bfloat16
AF = mybir.ActivationFunctionType
ALU = mybir.AluOpType
AX = mybir.AxisListType


class T:
    __slots__ = ("h", "name", "lw", "rd")

    def __init__(self, h, name):
        self.h = h
        self.name = name
        self.lw = None
        self.rd = {}

    def __getitem__(self, idx):
        return self.h[idx]


class TV:
    def __init__(self, t, fn):
        self.t = t
        self.fn = fn

    def __getitem__(self, idx):
        return self.fn(self.t)[idx]
    lw = property(lambda s: s.t.lw, lambda s, v: setattr(s.t, "lw", v))
    rd = property(lambda s: s.t.rd, lambda s, v: setattr(s.t, "rd", v))
    name = property(lambda s: s.t.name)


class KB:
    NDMA = 6

    def __init__(self):
        self.nc = bass.Bass("TRN2", target_bir_lowering=False)
        nc = self.nc
        self._ctx = []
        self.eng = {"pe": nc.tensor, "dve": nc.vector, "act": nc.scalar, "pool": nc.gpsimd, "sp": nc.sync}
        self.sems = {}
        self.cnt = {}
        for e in ("pe", "dve", "act", "pool"):
            self.sems[e] = self._enter(nc.semaphore("s_" + e))
            self.cnt[e] = 0
        self.dma_rr = {"sp": 0, "pool": 0, "act": 0}
        for q in ("sp", "pool", "act"):
            for i in range(self.NDMA):
                k = "d%s%d" % (q, i)
                self.sems[k] = self._enter(nc.semaphore("s_" + k))
                self.cnt[k] = 0
        self.seen = {e: {} for e in self.eng}
        self.n_ins = 0
        self.n_wait = 0

    def _enter(self, cm):
        v = cm.__enter__()
        self._ctx.append(cm)
        return v

    def close(self):
        for cm in reversed(self._ctx):
            cm.__exit__(None, None, None)
        self._ctx = []

    def sb(self, name, shape, dt=F32):
        self.uid = getattr(self, "uid", 0) + 1
        name = "%s_%d" % (name, self.uid)
        return T(self._enter(self.nc.sbuf_tensor(name, list(shape), dt)), name)

    def ps(self, name, shape, dt=F32):
        self.uid = getattr(self, "uid", 0) + 1
        name = "%s_%d" % (name, self.uid)
        return T(self._enter(self.nc.psum_tensor(name, list(shape), dt)), name)

    def dram(self, name, shape, dt=F32, kind="ExternalInput"):
        return self.nc.dram_tensor(name, list(shape), dt, kind=kind).ap()

    def _wait(self, e, deps):
        need = {}
        for d in deps:
            if d is None:
                continue
            k, c = d
            if c > need.get(k, 0):
                need[k] = c
        seen = self.seen[e]
        for k, c in need.items():
            if seen.get(k, 0) >= c:
                continue
            self.eng[e].wait_ge(self.sems[k], c)
            self.n_wait += 1
            seen[k] = c

    def op(self, e, fn, reads=(), writes=(), pe_acc=False):
        deps = []
        for t in reads:
            deps.append(t.lw)
        for t in writes:
            if not (pe_acc and t.lw is not None and t.lw[0] == "pe"):
                deps.append(t.lw)
            deps.extend(t.rd.items())
        if e == "pe":
            deps = [d for d in deps if d is not None and d[0] != "pe"]
        self._wait(e, deps)
        ins = fn()
        self.cnt[e] += 1
        ins.then_inc(self.sems[e], 1)
        me = (e, self.cnt[e])
        for t in reads:
            t.rd[me[0]] = me[1]
        for t in writes:
            t.lw = me
            t.rd = {}
        self.n_ins += 1
        return ins

    def dma(self, q, out, in_, reads=(), writes=(), **kw):
        i = self.dma_rr[q]
        self.dma_rr[q] = (i + 1) % self.NDMA
        k = "d%s%d" % (q, i)
        deps = [(k, self.cnt[k])] if self.cnt[k] else []
        for t in reads:
            deps.append(t.lw)
        for t in writes:
            deps.append(t.lw)
            deps.extend(t.rd.items())
        self._wait(q, deps)
        ins = self.eng[q].dma_start(out=out, in_=in_, **kw)
        self.cnt[k] += 16
        ins.then_inc(self.sems[k], 16)
        me = (k, self.cnt[k])
        for t in reads:
            t.rd[me[0]] = me[1]
        for t in writes:
            t.lw = me
            t.rd = {}
        self.n_ins += 1
        return me

    def mark(self):
        return len(self._ctx)

    def release(self, mark):
        while len(self._ctx) > mark:
            self._ctx.pop().__exit__(None, None, None)

    def all_counts(self):
        return [(k, c) for k, c in self.cnt.items() if c]

    def barrier(self):
        deps = self.all_counts()
        for e in self.eng:
            self._wait(e, deps)

    def dma_counts(self):
        return [(k, c) for k, c in self.cnt.items() if c and k.startswith("d")]

    def allgather(self, in_ap, out_ap, deps, groups):
        if "cc" not in self.sems:
            self.sems["cc"] = self._enter(self.nc.semaphore("s_cc"))
            self.cnt["cc"] = 0
        self._wait("pool", deps)
        ins = self.nc.gpsimd.collective_compute("AllGather", ALU.bypass, replica_groups=groups, ins=[in_ap], outs=[out_ap])
        self.cnt["cc"] += 1
        ins.then_inc(self.sems["cc"])
        return ("cc", self.cnt["cc"])

    def finish(self, e="sp"):
        deps = [(k, c) for k, c in self.cnt.items() if c]
        self._wait(e, deps)


def interleave(gen_iter, width):
    active = []
    it = iter(gen_iter)
    more = True
    while True:
        while more and len(active) < width:
            try:
                active.append(next(it))
            except StopIteration:
                more = False
        if not active:
            break
        for g in list(active):
            try:
                next(g)
            except StopIteration:
                active.remove(g)


D = 1024
KC = 8
FH = 2816
NTOK = 4160
BLOCKS = [(0, 64, 1)] + [(64 + i * 512, 512, 0) for i in range(8)]
EPS = 1e-6


class Ring:
    def __init__(self, ts):
        self.ts = ts
        self.i = 0

    def next(self):
        t = self.ts[self.i]
        self.i = (self.i + 1) % len(self.ts)
        return t


def tile_w(W, kc):
    K, NCOL = W.shape
    nch = (NCOL + 127) // 128
    Wp = np.zeros((K, nch * 128), np.float32)
    Wp[:, :NCOL] = W
    return np.ascontiguousarray(Wp.reshape(kc, 128, nch, 128).transpose(2, 1, 0, 3).reshape(nch, 128, kc * 128))


def col8(v):
    return np.ascontiguousarray(v.reshape(-1, 128).T)


def build_dense(has_out, has_ffn, has_in, ncols_in, final):
    kb = KB()
    nc = kb.nc
    nin = (ncols_in + 127) // 128 if has_in else 0
    xT = kb.dram("xT", [D, NTOK])
    csT = kb.dram("csT", [128, 16])
    ins = ["xT", "csT"]
    if has_out or has_ffn:
        modw_a = kb.dram("modw_a", [48, 128, KC * 128]); modb_a = kb.dram("modb_a", [128, 48])
        ins += ["modw_a", "modb_a"]
    if has_in:
        modw_b = kb.dram("modw_b", [48, 128, KC * 128]); modb_b = kb.dram("modb_b", [128, 48])
        nmix = kb.dram("nmix", [128, 8])
        w_in = kb.dram("w_in", [nin, 128, KC * 128])
        pT = kb.dram("pT", [nin * 128, NTOK], kind="ExternalOutput")
        ins += ["modw_b", "modb_b", "nmix", "w_in"]
    if has_out:
        oT = kb.dram("oT", [D, NTOK]); w_out = kb.dram("w_out", [8, 128, KC * 128])
        ins += ["oT", "w_out"]
    if has_ffn:
        nffn = kb.dram("nffn", [128, 8])
        f_in = kb.dram("f_in", [44, 128, KC * 128]); f_out = kb.dram("f_out", [8, 128, 22 * 128])
        ins += ["nffn", "f_in", "f_out"]
    xo = kb.dram("xo", [D, NTOK], kind="ExternalOutput")

    wring = Ring([kb.sb("w%d" % i, [128, 22 * 128]) for i in range(4)])
    psr = Ring([kb.ps("ps%d" % i, [128, 512]) for i in range(7)])
    psm = kb.ps("psm", [128, 512])
    xr = Ring([kb.sb("x%d" % i, [128, KC, 512]) for i in range(2)])
    ht = kb.sb("ht", [128, KC, 512])
    sq = kb.sb("sq", [128, KC, 512])
    rr = kb.sb("rr", [128, 512])
    ones = kb.sb("ones", [128, 128])
    cs = kb.sb("cs", [128, 16])
    tmpr = Ring([kb.sb("tmp%d" % i, [128, 512]) for i in range(3)])
    if has_out:
        ot = kb.sb("ot", [128, KC, 512])
    if has_ffn:
        actT = kb.sb("actT", [128, 22, 512])
    wq = ["sp", "pool"]
    wqi = [0]

    def wload(src_ap, width):
        w = wring.next()
        q = wq[wqi[0] % 2]; wqi[0] += 1
        kb.dma(q, w[:, :width], src_ap, writes=[w])
        return w

    kb.op("dve", lambda: nc.vector.memset(ones[:], 1.0 / D), writes=[ones])
    kb.dma("sp", cs[:], csT[:, :], writes=[cs])
    kb.op("act", lambda: nc.scalar.activation(out=cs[:], in_=cs[:], func=AF.Silu), reads=[cs], writes=[cs])

    def mod_compute(modw, modb, name):
        mb = kb.sb(name + "_b", [128, 48])
        mo = kb.sb(name, [128, 48, 2])
        kb.dma("sp", mb[:], modb[:, :], writes=[mb])
        for n in range(48):
            w = wload(modw[n], KC * 128)
            for k in range(KC):
                kb.op("pe", lambda: nc.tensor.matmul(psm[:, n * 2:n * 2 + 2], w[:, k * 128:(k + 1) * 128], cs[:, k * 2:k * 2 + 2],
                                                     start=(k == 0), stop=(k == KC - 1)),
                      reads=[w, cs], writes=[psm], pe_acc=True)
        for s in range(2):
            kb.op("dve", lambda: nc.vector.tensor_tensor(mo[:, :, s], psm[:, s:96:2], mb[:, :], ALU.add),
                  reads=[psm, mb], writes=[mo])
        return mo

    def gs(mo, nw_dram, name, sh, sc):
        nw = kb.sb(name + "_nw", [128, 8])
        G = kb.sb(name + "_G", [128, 8, 2])
        kb.dma("sp", nw[:], nw_dram[:, :], writes=[nw])
        for s in range(2):
            kb.op("dve", lambda: nc.vector.scalar_tensor_tensor(G[:, :, s], mo[:, sc * 8:sc * 8 + 8, s], 1.0, nw[:, :], ALU.add, ALU.mult),
                  reads=[mo, nw], writes=[G])
        return G

    if has_out or has_ffn:
        moA = mod_compute(modw_a, modb_a, "moA")
    if has_ffn:
        G2 = gs(moA, nffn, "g2", 3, 4)
    if has_in:
        moB = mod_compute(modw_b, modb_b, "moB")
        G1 = gs(moB, nmix, "g1", 0, 1)

    def norm_mod(xt, N, G, mo, shift_idx, s):
        for k in range(KC):
            kb.op("act", lambda: nc.scalar.activation(out=sq[:, k, :N], in_=xt[:, k, :N], func=AF.Square), reads=[xt], writes=[sq])
        ps = psr.next()
        for k in range(KC):
            kb.op("pe", lambda: nc.tensor.matmul(ps[:, :N], ones[:, :], sq[:, k, :N], start=(k == 0), stop=(k == KC - 1)),
                  reads=[ones, sq], writes=[ps], pe_acc=True)
        kb.op("dve", lambda: nc.vector.tensor_scalar(rr[:, :N], ps[:, :N], EPS, None, ALU.add), reads=[ps], writes=[rr])
        kb.op("act", lambda: nc.scalar.activation(out=rr[:, :N], in_=rr[:, :N], func=AF.Sqrt), reads=[rr], writes=[rr])
        kb.op("dve", lambda: nc.vector.reciprocal(rr[:, :N], rr[:, :N]), reads=[rr], writes=[rr])
        for k in range(KC):
            kb.op("dve", lambda: nc.vector.scalar_tensor_tensor(ht[:, k, :N], xt[:, k, :N], G[:, k, s:s + 1], rr[:, :N], ALU.mult, ALU.mult),
                  reads=[xt, G, rr], writes=[ht])
            kb.op("act", lambda: nc.scalar.activation(out=ht[:, k, :N], in_=ht[:, k, :N], func=AF.Identity,
                                                      bias=mo[:, shift_idx * 8 + k, s:s + 1], scale=1.0),
                  reads=[ht, mo], writes=[ht])

    xTv = xT.rearrange("(k p) t -> p k t", p=128)
    xov = xo.rearrange("(k p) t -> p k t", p=128)
    if has_out:
        oTv = oT.rearrange("(k p) t -> p k t", p=128)

    for (t0, N, s) in BLOCKS:
        xt = xr.next()
        kb.dma("sp", xt[:, :, :N], xTv[:, :, t0:t0 + N], writes=[xt])
        if has_out:
            kb.dma("pool", ot[:, :, :N], oTv[:, :, t0:t0 + N], writes=[ot])
            for m in range(8):
                w = wload(w_out[m], KC * 128)
                ps = psr.next()
                for k in range(KC):
                    kb.op("pe", lambda: nc.tensor.matmul(ps[:, :N], w[:, k * 128:(k + 1) * 128], ot[:, k, :N], start=(k == 0), stop=(k == KC - 1)),
                          reads=[w, ot], writes=[ps], pe_acc=True)
                kb.op("dve", lambda: nc.vector.scalar_tensor_tensor(xt[:, m, :N], ps[:, :N], moA[:, 16 + m, s:s + 1], xt[:, m, :N], ALU.mult, ALU.add),
                      reads=[ps, moA, xt], writes=[xt])
        if has_ffn:
            norm_mod(xt, N, G2, moA, 3, s)
            for j in range(22):
                wg = wload(f_in[j], KC * 128)
                wu = wload(f_in[22 + j], KC * 128)
                pg = psr.next(); pu = psr.next()
                for k in range(KC):
                    kb.op("pe", lambda: nc.tensor.matmul(pg[:, :N], wg[:, k * 128:(k + 1) * 128], ht[:, k, :N], start=(k == 0), stop=(k == KC - 1)),
                          reads=[wg, ht], writes=[pg], pe_acc=True)
                for k in range(KC):
                    kb.op("pe", lambda: nc.tensor.matmul(pu[:, :N], wu[:, k * 128:(k + 1) * 128], ht[:, k, :N], start=(k == 0), stop=(k == KC - 1)),
                          reads=[wu, ht], writes=[pu], pe_acc=True)
                tg = tmpr.next()
                kb.op("act", lambda: nc.scalar.activation(out=tg[:, :N], in_=pg[:, :N], func=AF.Silu), reads=[pg], writes=[tg])
                kb.op("dve", lambda: nc.vector.tensor_tensor(actT[:, j, :N], tg[:, :N], pu[:, :N], ALU.mult), reads=[tg, pu], writes=[actT])
            for m in range(8):
                w = wload(f_out[m], 22 * 128)
                ps = psr.next()
                for j in range(22):
                    kb.op("pe", lambda: nc.tensor.matmul(ps[:, :N], w[:, j * 128:(j + 1) * 128], actT[:, j, :N], start=(j == 0), stop=(j == 21)),
                          reads=[w, actT], writes=[ps], pe_acc=True)
                kb.op("dve", lambda: nc.vector.scalar_tensor_tensor(xt[:, m, :N], ps[:, :N], moA[:, 40 + m, s:s + 1], xt[:, m, :N], ALU.mult, ALU.add),
                      reads=[ps, moA, xt], writes=[xt])
        kb.dma("sp", xov[:, :, t0:t0 + N], xt[:, :, :N], reads=[xt])
        if has_in:
            norm_mod(xt, N, G1, moB, 0, s)
            for n in range(nin):
                w = wload(w_in[n], KC * 128)
                ps = psr.next()
                for k in range(KC):
                    kb.op("pe", lambda: nc.tensor.matmul(ps[:, :N], w[:, k * 128:(k + 1) * 128], ht[:, k, :N], start=(k == 0), stop=(k == KC - 1)),
                          reads=[w, ht], writes=[ps], pe_acc=True)
                tp = tmpr.next()
                kb.op("act", lambda: nc.scalar.copy(out=tp[:, :N], in_=ps[:, :N]), reads=[ps], writes=[tp])
                kb.dma("sp", pT[n * 128:(n + 1) * 128, t0:t0 + N], tp[:, :N], reads=[tp])
    kb.finish("sp")
    kb.close()
    return kb, ins


def dense_host_common(inputs, layer_a, layer_b):
    m = {}
    if layer_a is not None:
        m["modw_a"] = tile_w(inputs["mod_w"][layer_a], KC)
        m["modb_a"] = col8(inputs["mod_b"][layer_a])
        m["nffn"] = col8(inputs["norm_ffn_w"][layer_a])
        m["f_in"] = tile_w(inputs["ffn_w_in"][layer_a], KC)
        m["f_out"] = tile_w(inputs["ffn_w_out"][layer_a], 22)
        m["w_out"] = tile_w((inputs["ab_w_out"] if layer_a == 0 else inputs["cd_w_out"])[0], KC)
    if layer_b is not None:
        m["modw_b"] = tile_w(inputs["mod_w"][layer_b], KC)
        m["modb_b"] = col8(inputs["mod_b"][layer_b])
        m["nmix"] = col8(inputs["norm_mix_w"][layer_b])
        m["w_in"] = tile_w((inputs["ab_w_in"] if layer_b == 0 else inputs["cd_w_in"])[0], KC)
    return m


def cs_core(inputs, b):
    a = np.stack([inputs["c"][b], inputs["c_ctx"]], axis=-1)
    return np.ascontiguousarray(a.reshape(8, 128, 2).transpose(1, 0, 2).reshape(128, 16))


def shard_tokT(lat, ctx, c):
    b, q = c // 4, c % 4
    return np.ascontiguousarray(np.concatenate([ctx[b, q * 64:(q + 1) * 64], lat[b, q * 4096:(q + 1) * 4096]], axis=0).T)


def unshard_tokT(arrs, F):
    lat = np.zeros((2, 16384, F), np.float32); ctx = np.zeros((2, 256, F), np.float32)
    for c, a in enumerate(arrs):
        b, q = c // 4, c % 4
        ctx[b, q * 64:(q + 1) * 64] = a[:F, :64].T
        lat[b, q * 4096:(q + 1) * 4096] = a[:F, 64:].T
    return lat, ctx


L = 16384
CTX = 256
EPS = 1e-6
NCK = 260
DN_BATCHES = [(0, 4)] + [(4 + 8 * i, 8) for i in range(32)]


def build_ab(do_dn=True, do_swa=True, kb=None, pre=None, out_cb=None):
    own = kb is None
    if own:
        kb = KB()
    nc = kb.nc
    ins = []
    pre = pre or {}

    def din(name, shape):
        if name in pre:
            return pre[name]
        ins.append(name)
        t_ = kb.dram(name, shape)
        if not own:
            pre[name] = t_
        return t_

    ident_d = din("c_ident", [128, 128])
    ident = kb.sb("ident", [128, 128])
    kb.dma("sp", ident[:], ident_d[:, :], writes=[ident])
    ones = kb.sb("ones", [128, 128])
    kb.op("dve", lambda: nc.vector.memset(ones[:], 1.0), writes=[ones])
    PS = [kb.ps("P%d" % i, [128, 512]) for i in range(8)]
    RFD = mybir.dt.float32r

    def RD(ap):
        return ap.bitcast(RFD)

    if do_dn:
        qpad = din("d_qpad", [128, CTX + 2 + L + 2])
        kpad = din("d_kpad", [128, CTX + 2 + L + 2])
        vpad = din("d_vpad", [CTX + 2 + L + 2, 128])
        gate = din("d_gate", [CTX + L, 128])
        abx = din("d_ab", [64, NCK, 4])
        cwqk = din("d_cwqk", [128, 6])
        cwv = din("d_cwv", [64, 3, 128])
        alog = din("d_alog", [64, 2]); dtb = din("d_dtb", [64, 2])
        nw = din("d_nw", [128, 128])
        ctri = din("c_tri", [64, 2, 64])
        cmaskS = din("c_maskS", [64, 2, 64])
        cmaskI = din("c_maskI", [64, 2, 64])
        o_dn = kb.dram("o_dn", [CTX + L, 128], kind="ExternalOutput") if out_cb is None else None
        oscr = kb.dram("d_oscr", [2, CTX + L, 128], kind="Internal")

        tri = kb.sb("tri", [64, 2, 64]); maskS = kb.sb("maskS", [64, 2, 64]); maskI = kb.sb("maskI", [64, 2, 64])
        cw = kb.sb("cw", [128, 6]); cwvs = kb.sb("cwvs", [64, 3, 128]); al = kb.sb("al", [64, 2]); db = kb.sb("db", [64, 2])
        nws = kb.sb("nws", [128, 128])
        for t_, d_ in ((tri, ctri), (maskS, cmaskS), (maskI, cmaskI), (cwvs, cwv)):
            kb.dma("sp", t_[:], d_[:, :, :], writes=[t_])
        for t_, d_ in ((cw, cwqk), (al, alog), (db, dtb), (nws, nw)):
            kb.dma("sp", t_[:], d_[:, :], writes=[t_])
        gb = kb.sb("gb", [64, NCK, 4]); tA = kb.sb("tA", [64, NCK, 2]); tB = kb.sb("tB", [64, NCK, 2])
        kb.dma("sp", gb[:], abx[:, :, :], writes=[gb])
        kb.op("act", lambda: nc.scalar.activation(out=al[:], in_=al[:], func=AF.Exp), reads=[al], writes=[al])
        kb.op("dve", lambda: nc.vector.tensor_scalar(al[:], al[:], -1.0, None, ALU.mult), reads=[al], writes=[al])
        for d in range(2):
            kb.op("dve", lambda: nc.vector.tensor_scalar(tA[:, :, d], gb[:, :, d], db[:, d:d + 1], None, ALU.add), reads=[gb, db], writes=[tA])
        kb.op("act", lambda: nc.scalar.activation(out=tB[:], in_=tA[:], func=AF.Abs), reads=[tA], writes=[tB])
        kb.op("act", lambda: nc.scalar.activation(out=tB[:], in_=tB[:], func=AF.Exp, scale=-1.0), reads=[tB], writes=[tB])
        kb.op("dve", lambda: nc.vector.tensor_scalar(tB[:], tB[:], 1.0, None, ALU.add), reads=[tB], writes=[tB])
        kb.op("act", lambda: nc.scalar.activation(out=tB[:], in_=tB[:], func=AF.Ln), reads=[tB], writes=[tB])
        kb.op("dve", lambda: nc.vector.tensor_scalar(tA[:], tA[:], 0.0, None, ALU.max), reads=[tA], writes=[tA])
        kb.op("dve", lambda: nc.vector.tensor_tensor(tA[:], tA[:], tB[:], ALU.add), reads=[tA, tB], writes=[tA])
        for d in range(2):
            kb.op("dve", lambda: nc.vector.tensor_scalar(gb[:, :, d], tA[:, :, d], al[:, d:d + 1], None, ALU.mult), reads=[tA, al], writes=[gb])
        kb.op("act", lambda: nc.scalar.activation(out=gb[:, :, 2:4], in_=gb[:, :, 2:4], func=AF.Sigmoid), reads=[gb], writes=[gb])

        mk_main = kb.mark()
        rawq = Ring([kb.sb("rawq%d" % i, [128, 514]) for i in range(2)])
        rawk = Ring([kb.sb("rawk%d" % i, [128, 514]) for i in range(2)])
        qTr = Ring([kb.sb("qT%d" % i, [128, 512]) for i in range(2)])
        kTr = Ring([kb.sb("kT%d" % i, [128, 512]) for i in range(2)])
        sqr = Ring([kb.sb("sq%d" % i, [128, 512]) for i in range(2)])
        rsr = Ring([kb.sb("rs%d" % i, [128, 512]) for i in range(2)])
        vrw = Ring([kb.sb("vrw%d" % i, [64, 3, 8, 128]) for i in range(2)])
        vtr = Ring([kb.sb("vt%d" % i, [64, 8, 128]) for i in range(2)])
        vt2 = kb.sb("vt2", [64, 8, 128])
        gLr = Ring([kb.sb("gL%d" % i, [64, 8, 64]) for i in range(2)])
        gcol = kb.sb("gcol", [64, 2, NCK])
        egr = Ring([kb.sb("eg%d" % i, [128, 8, 64]) for i in range(2)])
        dfr = Ring([kb.sb("df%d" % i, [64, 8, 64]) for i in range(2)])
        dSr = Ring([kb.sb("dS%d" % i, [64, 8, 64]) for i in range(2)])
        Ar = Ring([kb.sb("A%d" % i, [64, 8, 64]) for i in range(2)])
        Br = Ring([kb.sb("B%d" % i, [64, 8, 64]) for i in range(2)])
        Ur = Ring([kb.sb("U%d" % i, [64, 8, 64]) for i in range(2)])
        ktm = Ring([kb.sb("ktm%d" % i, [64, 8, 128]) for i in range(2)])
        ecol = kb.sb("ecol", [64, 2, NCK]); ecol2 = kb.sb("ecol2", [64, 2, NCK])
        o_u = Ring([kb.sb("ou%d" % i, [64, 8, 128]) for i in range(2)])
        o_wT = Ring([kb.sb("owT%d" % i, [128, 8, 64]) for i in range(2)])
        o_qd = Ring([kb.sb("oqd%d" % i, [128, 8, 64]) for i in range(2)])
        o_at = Ring([kb.sb("oat%d" % i, [64, 8, 64]) for i in range(2)])
        o_kd = Ring([kb.sb("okd%d" % i, [64, 8, 128]) for i in range(2)])
        o_cd = Ring([kb.sb("ocd%d" % i, [128, 8]) for i in range(2)])
        wtmr = Ring([kb.sb("wtm%d" % i, [64, 8, 128]) for i in range(2)])
        o_MT = Ring([kb.sb("oMT%d" % i, [128, 8, 128]) for i in range(2)])
        o_NN = Ring([kb.sb("oNN%d" % i, [128, 8, 128]) for i in range(2)])
        o_QT = Ring([kb.sb("oQT%d" % i, [128, 8, 64]) for i in range(2)])
        vnr = Ring([kb.sb("vn%d" % i, [64, 128]) for i in range(2)])
        oor = Ring([kb.sb("oo%d" % i, [64, 128]) for i in range(3)])
        Sr = Ring([kb.sb("S%d" % i, [128, 128]) for i in range(2)])

        def tok0(c0):
            return c0 * 64

        def padoff(c0):
            return c0 * 64 if c0 < 4 else c0 * 64 + 2

        for d in range(2):
            kb.op("pe", lambda: nc.tensor.matmul(PS[0][0:64, 0:NCK], tri[:, d, :], gb[:, :, d], start=True, stop=True), reads=[tri, gb], writes=[PS[0]])
            kb.op("dve", lambda: nc.vector.tensor_copy(gcol[:, d, :], PS[0][0:64, 0:NCK]), reads=[PS[0]], writes=[gcol])
        kb.op("act", lambda: nc.scalar.activation(out=ecol[:], in_=gcol[:], func=AF.Exp), reads=[gcol], writes=[ecol])

        def feat_pre(c0, n):
            N = n * 64
            po = padoff(c0)
            outs = []
            for which, (pad_d, ring_raw, ring_o) in enumerate(((qpad, rawq, qTr), (kpad, rawk, kTr))):
                raw = ring_raw.next(); o = ring_o.next(); sq = sqr.next(); rs = rsr.next()
                kb.dma("sp" if which == 0 else "pool", raw[:, :N + 2], pad_d[:, po:po + N + 2], writes=[raw])
                kb.op("dve", lambda: nc.vector.tensor_scalar(o[:, :N], raw[:, 0:N], cw[:, which * 3:which * 3 + 1], None, ALU.mult), reads=[raw, cw], writes=[o])
                kb.op("dve", lambda: nc.vector.scalar_tensor_tensor(o[:, :N], raw[:, 1:N + 1], cw[:, which * 3 + 1:which * 3 + 2], o[:, :N], ALU.mult, ALU.add), reads=[raw, cw, o], writes=[o])
                kb.op("dve", lambda: nc.vector.scalar_tensor_tensor(o[:, :N], raw[:, 2:N + 2], cw[:, which * 3 + 2:which * 3 + 3], o[:, :N], ALU.mult, ALU.add), reads=[raw, cw, o], writes=[o])
                kb.op("act", lambda: nc.scalar.activation(out=o[:, :N], in_=o[:, :N], func=AF.Silu), reads=[o], writes=[o])
                kb.op("act", lambda: nc.scalar.activation(out=sq[:, :N], in_=o[:, :N], func=AF.Square), reads=[o], writes=[sq])
                ps = PS[1]
                kb.op("pe", lambda: nc.tensor.matmul(ps[:, :N], ones[:], sq[:, :N], start=True, stop=True), reads=[ones, sq], writes=[ps])
                kb.op("dve", lambda: nc.vector.tensor_scalar(rs[:, :N], ps[:, :N], EPS, None, ALU.add), reads=[ps], writes=[rs])
                kb.op("act", lambda: nc.scalar.activation(out=rs[:, :N], in_=rs[:, :N], func=AF.Sqrt), reads=[rs], writes=[rs])
                kb.op("dve", lambda: nc.vector.reciprocal(rs[:, :N], rs[:, :N]), reads=[rs], writes=[rs])
                if which == 0:
                    kb.op("dve", lambda: nc.vector.scalar_tensor_tensor(o[:, :N], o[:, :N], 128.0 ** -0.5, rs[:, :N], ALU.mult, ALU.mult), reads=[o, rs], writes=[o])
                else:
                    kb.op("dve", lambda: nc.vector.tensor_tensor(o[:, :N], o[:, :N], rs[:, :N], ALU.mult), reads=[o, rs], writes=[o])
                outs.append(o)
            return outs

        def v_pre(c0, n):
            vr = vrw.next(); vt = vtr.next()
            base = padoff(c0)
            for s in range(3):
                kb.dma("pool", vr[:, s, :n, :], vpad[base + s: base + s + n * 64, :].rearrange("(c j) e -> j c e", j=64), writes=[vr])

            def wv(s):
                return cwvs[:, s, :].rearrange("p (o e) -> p o e", o=1).to_broadcast([64, n, 128])
            kb.op("pool", lambda: nc.gpsimd.tensor_tensor(vt[:, :n, :], vr[:, 0, :n, :], wv(0), ALU.mult), reads=[vr, cwvs], writes=[vt])
            kb.op("pool", lambda: nc.gpsimd.tensor_tensor(vt2[:, :n, :], vr[:, 1, :n, :], wv(1), ALU.mult), reads=[vr, cwvs], writes=[vt2])
            kb.op("pool", lambda: nc.gpsimd.tensor_tensor(vt[:, :n, :], vt[:, :n, :], vt2[:, :n, :], ALU.add), reads=[vt, vt2], writes=[vt])
            kb.op("pool", lambda: nc.gpsimd.tensor_tensor(vt2[:, :n, :], vr[:, 2, :n, :], wv(2), ALU.mult), reads=[vr, cwvs], writes=[vt2])
            kb.op("pool", lambda: nc.gpsimd.tensor_tensor(vt[:, :n, :], vt[:, :n, :], vt2[:, :n, :], ALU.add), reads=[vt, vt2], writes=[vt])
            kb.op("act", lambda: nc.scalar.activation(out=vt[:, :n, :], in_=vt[:, :n, :], func=AF.Silu), reads=[vt], writes=[vt])
            return vt

        def chunk_pre(d, c0, n, qT, kT, vt, res):
            last = 63 if d == 0 else 0
            cs = slice(c0, c0 + n)
            gL = gLr.next()
            kb.op("dve", lambda: nc.vector.tensor_tensor(gL[:, :n, :], tri[:, d, :].rearrange("p (o i) -> p o i", o=1).to_broadcast([64, n, 64]),
                                                         gb[:, cs, d].rearrange("p (c o) -> p c o", o=1).to_broadcast([64, n, 64]), ALU.mult),
                  reads=[tri, gb], writes=[gL])
            pg = PS[2]
            kb.op("pe", lambda: nc.tensor.matmul(pg[:, :n * 64], ones[0:64, :], gL[:, :n, :].rearrange("p c i -> p (c i)"), start=True, stop=True),
                  reads=[ones, gL], writes=[pg])
            pgv = pg[:, :n * 64].rearrange("p (c i) -> p c i", i=64)
            eg = egr.next()
            kb.op("act", lambda: nc.scalar.activation(out=eg[:, :n, :], in_=pgv, func=AF.Exp), reads=[pg], writes=[eg])
            yield
            df = dfr.next(); dS = dSr.next()
            gc_b = gcol[:, d, cs].rearrange("p (c o) -> p c o", o=1).to_broadcast([64, n, 64])
            kb.op("dve", lambda: nc.vector.tensor_tensor(df[:, :n, :], pg[0:64, :n * 64].rearrange("p (c i) -> p c i", i=64), gc_b, ALU.subtract), reads=[pg, gcol], writes=[df])
            kb.op("dve", lambda: nc.vector.tensor_scalar(df[:, :n, :], df[:, :n, :], 0.0, None, ALU.min), reads=[df], writes=[df])
            kb.op("act", lambda: nc.scalar.activation(out=df[:, :n, :], in_=df[:, :n, :], func=AF.Exp), reads=[df], writes=[df])
            mS = maskS[:, d, :].rearrange("p (o i) -> p o i", o=1).to_broadcast([64, n, 64])
            mI = maskI[:, d, :].rearrange("p (o i) -> p o i", o=1).to_broadcast([64, n, 64])
            bc = gb[:, cs, 2 + d].rearrange("p (c o) -> p c o", o=1).to_broadcast([64, n, 64])
            kb.op("dve", lambda: nc.vector.tensor_tensor(dS[:, :n, :], df[:, :n, :], mS, ALU.mult), reads=[df, maskS], writes=[dS])
            kb.op("dve", lambda: nc.vector.tensor_tensor(dS[:, :n, :], dS[:, :n, :], bc, ALU.mult), reads=[dS, gb], writes=[dS])
            kb.op("pool", lambda: nc.gpsimd.tensor_tensor(df[:, :n, :], df[:, :n, :], mI, ALU.mult), reads=[df, maskI], writes=[df])
            yield
            pG = PS[3]; pQ = PS[4]
            for c in range(n):
                kb.op("pe", lambda: nc.tensor.matmul(pG[0:64, c * 64:(c + 1) * 64], kT[:, c * 64:(c + 1) * 64], kT[:, c * 64:(c + 1) * 64], start=True, stop=True),
                      reads=[kT], writes=[pG])
                kb.op("pe", lambda: nc.tensor.matmul(pQ[0:64, c * 64:(c + 1) * 64], kT[:, c * 64:(c + 1) * 64], qT[:, c * 64:(c + 1) * 64], start=True, stop=True),
                      reads=[kT, qT], writes=[pQ])
            B = Br.next(); at = o_at.next()
            kb.op("dve", lambda: nc.vector.tensor_tensor(B[:, :n, :], pG[0:64, :n * 64].rearrange("p (c i) -> p c i", i=64), dS[:, :n, :], ALU.mult), reads=[pG, dS], writes=[B])
            kb.op("dve", lambda: nc.vector.tensor_tensor(RD(at[:, :n, :]), pQ[0:64, :n * 64].rearrange("p (c i) -> p c i", i=64), df[:, :n, :], ALU.mult), reads=[pQ, df], writes=[at])
            yield
            pA = PS[5]
            for c in range(n):
                kb.op("pe", lambda: nc.tensor.matmul(pA[0:64, c * 64:(c + 1) * 64], B[:, c, :], ident[0:64, 0:64], start=True, stop=True), reads=[B, ident], writes=[pA])
            A = Ar.next()
            kb.op("act", lambda: nc.scalar.copy(out=A[:, :n, :], in_=pA[0:64, :n * 64].rearrange("p (c i) -> p c i", i=64)), reads=[pA], writes=[A])
            U = Ur.next()
            idb = ident[0:64, 0:64].rearrange("p (o i) -> p o i", o=1).to_broadcast([64, n, 64])
            kb.op("pool", lambda: nc.gpsimd.tensor_tensor(U[:, :n, :], B[:, :n, :], idb, ALU.add), reads=[B, ident], writes=[U])
            yield
            for lvl in range(1, 6):
                pA2 = PS[5]; pB2 = PS[3]; pU = PS[4]
                for c in range(n):
                    kb.op("pe", lambda: nc.tensor.matmul(pA2[0:64, c * 64:(c + 1) * 64], B[:, c, :], A[:, c, :], start=True, stop=True), reads=[A, B], writes=[pA2])
                if lvl < 5:
                    for c in range(n):
                        kb.op("pe", lambda: nc.tensor.matmul(pB2[0:64, c * 64:(c + 1) * 64], A[:, c, :], B[:, c, :], start=True, stop=True), reads=[A, B], writes=[pB2])
                A2 = Ar.next()
                kb.op("act", lambda: nc.scalar.copy(out=A2[:, :n, :], in_=pA2[0:64, :n * 64].rearrange("p (c i) -> p c i", i=64)), reads=[pA2], writes=[A2])
                if lvl < 5:
                    B2 = Br.next()
                    kb.op("dve", lambda: nc.vector.tensor_copy(B2[:, :n, :], pB2[0:64, :n * 64].rearrange("p (c i) -> p c i", i=64)), reads=[pB2], writes=[B2])
                    B = B2
                A = A2
                yield
                for c in range(n):
                    kb.op("pe", lambda: nc.tensor.matmul(pU[0:64, c * 64:(c + 1) * 64], A[:, c, :], U[:, c, :], start=True, stop=True), reads=[A, U], writes=[pU])
                U2 = Ur.next()
                kb.op("dve", lambda: nc.vector.tensor_tensor(U2[:, :n, :], U[:, :n, :], pU[0:64, :n * 64].rearrange("p (c i) -> p c i", i=64), ALU.add), reads=[U, pU], writes=[U2])
                U = U2
                yield
            bd = gLr.next()
            kb.op("dve", lambda: nc.vector.tensor_tensor(bd[:, :n, :], idb, bc, ALU.mult), reads=[ident, gb], writes=[bd])
            pb = PS[2]
            kb.op("pe", lambda: nc.tensor.matmul(pb[0:64, :n * 64], ones[0:64, 0:64], bd[:, :n, :].rearrange("p c i -> p (c i)"), start=True, stop=True), reads=[ones, bd], writes=[pb])
            kb.op("dve", lambda: nc.vector.tensor_tensor(U[:, :n, :], U[:, :n, :], pb[0:64, :n * 64].rearrange("p (c i) -> p c i", i=64), ALU.mult), reads=[U, pb], writes=[U])
            yield
            kd = o_kd.next(); kw = ktm.next()
            for half in range(0, n, 4):
                pk = PS[6]
                m = min(4, n - half)
                for c in range(half, half + m):
                    kb.op("pe", lambda: nc.tensor.matmul(pk[0:64, (c - half) * 128:(c - half + 1) * 128], kT[:, c * 64:(c + 1) * 64], ident[:], start=True, stop=True),
                          reads=[kT, ident], writes=[pk])
                pkv = pk[0:64, :m * 128].rearrange("p (c e) -> p c e", e=128)
                e1 = ecol[:, d, c0 + half:c0 + half + m].rearrange("p (c o) -> p c o", o=1).to_broadcast([64, m, 128])
                e2 = ecol2[:, d, c0 + half:c0 + half + m].rearrange("p (c o) -> p c o", o=1).to_broadcast([64, m, 128])
                kb.op("dve", lambda: nc.vector.tensor_tensor(kw[:, half:half + m, :], pkv, e1, ALU.mult), reads=[pk, ecol], writes=[kw])
                kb.op("dve", lambda: nc.vector.tensor_tensor(RD(kd[:, half:half + m, :]), pkv, e2, ALU.mult), reads=[pk, ecol2], writes=[kd])
                yield
            u = o_u.next()
            for half in range(0, n, 4):
                pu = PS[7]
                m = min(4, n - half)
                for c in range(half, half + m):
                    kb.op("pe", lambda: nc.tensor.matmul(pu[0:64, (c - half) * 128:(c - half + 1) * 128], U[:, c, :], vt[:, c, :], start=True, stop=True), reads=[U, vt], writes=[pu])
                kb.op("act", lambda: nc.scalar.copy(out=RD(u[:, half:half + m, :]), in_=pu[0:64, :m * 128].rearrange("p (c e) -> p c e", e=128)), reads=[pu], writes=[u])
                yield
            qd = o_qd.next(); cd = o_cd.next()
            kb.op("pool", lambda: nc.gpsimd.tensor_tensor(qd[:, :n, :], qT[:, :n * 64].rearrange("p (c i) -> p c i", i=64), eg[:, :n, :], ALU.mult), reads=[qT, eg], writes=[qd])
            kb.op("act", lambda: nc.scalar.copy(out=cd[:, :n], in_=eg[:, :n, last]), reads=[eg], writes=[cd])
            wt_ = wtmr.next()
            for half in range(0, n, 4):
                pw = PS[6]
                m = min(4, n - half)
                for c in range(half, half + m):
                    kb.op("pe", lambda: nc.tensor.matmul(pw[0:64, (c - half) * 128:(c - half + 1) * 128], U[:, c, :], kw[:, c, :], start=True, stop=True), reads=[U, kw], writes=[pw])
                kb.op("act", lambda: nc.scalar.copy(out=RD(wt_[:, half:half + m, :]), in_=pw[0:64, :m * 128].rearrange("p (c e) -> p c e", e=128)), reads=[pw], writes=[wt_])
                yield
            MT = o_MT.next(); NN = o_NN.next(); QT = o_QT.next()
            for half in range(0, n, 4):
                pm = PS[3]; pn = PS[5]
                m = min(4, n - half)
                for c in range(half, half + m):
                    kb.op("pe", lambda: nc.tensor.matmul(pm[:, (c - half) * 128:(c - half + 1) * 128], RD(wt_[:, c, :]), RD(kd[:, c, :]), start=True, stop=True), reads=[wt_, kd], writes=[pm])
                    kb.op("pe", lambda: nc.tensor.matmul(pn[:, (c - half) * 128:(c - half + 1) * 128], RD(kd[:, c, :]), RD(u[:, c, :]), start=True, stop=True), reads=[kd, u], writes=[pn])
                for c in range(half, half + m):
                    kb.op("dve", lambda: nc.vector.scalar_tensor_tensor(RD(MT[:, c, :]), ident[:], cd[:, c:c + 1], pm[:, (c - half) * 128:(c - half + 1) * 128], ALU.mult, ALU.subtract),
                          reads=[ident, cd, pm], writes=[MT])
                kb.op("act", lambda: nc.scalar.copy(out=NN[:, half:half + m, :], in_=pn[:, :m * 128].rearrange("p (c e) -> p c e", e=128)), reads=[pn], writes=[NN])
                yield
            pq = PS[2]
            for c in range(n):
                kb.op("pe", lambda: nc.tensor.matmul(pq[:, c * 64:(c + 1) * 64], RD(wt_[:, c, :]), RD(at[:, c, :]), start=True, stop=True), reads=[wt_, at], writes=[pq])
            kb.op("dve", lambda: nc.vector.tensor_tensor(QT[:, :n, :], qd[:, :n, :], pq[:, :n * 64].rearrange("p (c i) -> p c i", i=64), ALU.subtract), reads=[qd, pq], writes=[QT])
            res["v"] = (u, MT, NN, QT, at)
            yield

        def chunk_seq(d, c0, n, pre, S, res):
            u, MT, NN, QT, at = pre
            order = range(n) if d == 0 else range(n - 1, -1, -1)
            for c in order:
                p2 = PS[1]
                kb.op("pe", lambda: nc.tensor.matmul(p2[0:64, 0:128], QT[:, c, :], S[:], start=True, stop=False), reads=[QT, S], writes=[p2])
                kb.op("pe", lambda: nc.tensor.matmul(p2[0:64, 0:128], at[:, c, :], u[:, c, :], start=False, stop=True), reads=[at, u], writes=[p2], pe_acc=True)
                p1 = PS[0]
                kb.op("pe", lambda: nc.tensor.matmul(p1[:, 0:128], RD(MT[:, c, :]), RD(S[:]), start=True, stop=True), reads=[MT, S], writes=[p1])
                S2 = Sr.next()
                kb.op("dve", lambda: nc.vector.tensor_tensor(RD(S2[:]), p1[:, 0:128], NN[:, c, :], ALU.add), reads=[p1, NN], writes=[S2])
                oo = oor.next()
                kb.op("act", lambda: nc.scalar.copy(out=oo[:], in_=p2[0:64, 0:128]), reads=[p2], writes=[oo])
                t0 = (c0 + c) * 64
                kb.dma("sp", oscr[d, t0:t0 + 64, :], oo[:], reads=[oo])
                S = S2
                res["S"] = S
                yield
            res["S"] = S

        sel = kb.sb("sel", [64, 2, NCK])
        oh = kb.sb("oh", [64, 2])
        ohd = din("c_onehot", [64, 2])
        kb.dma("sp", oh[:], ohd[:, :], writes=[oh])
        for d in range(2):
            kb.op("dve", lambda: nc.vector.tensor_scalar(sel[:, d, :], gcol[:, d, :], oh[:, d:d + 1], None, ALU.mult), reads=[gcol, oh], writes=[sel])
            kb.op("pe", lambda: nc.tensor.matmul(PS[0][0:64, 0:NCK], ones[0:64, 0:64], sel[:, d, :], start=True, stop=True), reads=[ones, sel], writes=[PS[0]])
            kb.op("dve", lambda: nc.vector.tensor_tensor(ecol2[:, d, :], PS[0][0:64, 0:NCK], gcol[:, d, :], ALU.subtract), reads=[PS[0], gcol], writes=[ecol2])
        kb.op("act", lambda: nc.scalar.activation(out=ecol2[:], in_=ecol2[:], func=AF.Exp), reads=[ecol2], writes=[ecol2])

        for d in range(2):
            batches = DN_BATCHES if d == 0 else [DN_BATCHES[0]] + DN_BATCHES[:0:-1]
            S = Sr.next()
            kb.op("dve", lambda: nc.vector.tensor_scalar(RD(S[:]), ident[:], 0.0, None, ALU.mult), reads=[ident], writes=[S])
            prev = None
            for item in batches + [None]:
                cur = None
                g1 = g2 = None
                r1 = {}; r2 = {"S": S}
                if item is not None:
                    c0, n = item
                    qT, kT = feat_pre(c0, n)
                    vt = v_pre(c0, n)
                    g1 = chunk_pre(d, c0, n, qT, kT, vt, r1)
                if prev is not None:
                    g2 = chunk_seq(d, prev[0], prev[1], prev[2], S, r2)
                tick = 0
                while g1 is not None or g2 is not None:
                    if g1 is not None:
                        try:
                            next(g1)
                        except StopIteration:
                            g1 = None
                    tick += 1
                    if g2 is not None:
                        try:
                            next(g2)
                        except StopIteration:
                            g2 = None
                S = r2["S"]
                if item is not None:
                    cur = (c0, n, r1["v"])
                prev = cur
        dn_done = [(k, c) for k, c in kb.cnt.items() if k.startswith("dsp") and c]
        kb.barrier()
        kb.release(mk_main)
        W_ = 8
        f1r = Ring([kb.sb("f1%d" % i, [128, 2, 128]) for i in range(W_ + 1)])
        f2r = Ring([kb.sb("f2%d" % i, [128, 128]) for i in range(1)])
        fgr = Ring([kb.sb("fg%d" % i, [128, 128]) for i in range(W_ + 1)])
        fjr = Ring([kb.sb("fj%d" % i, [128, 128]) for i in range(W_ + 1)])
        fsr = Ring([kb.sb("fs%d" % i, [128, 2]) for i in range(W_ + 1)])
        kb._wait("pool", dn_done)

        def fin_tile(ti):
            ab_ = f1r.next(); g = fgr.next(); jk = fjr.next(); st = fsr.next()
            rows = slice(ti * 128, (ti + 1) * 128)
            kb.dma("sp", ab_[:], oscr[:, rows, :].rearrange("d p e -> p d e"), writes=[ab_])
            kb.dma("sp", g[:], gate[rows, :], writes=[g])

            a = TV(ab_, lambda t: t[:, 0, :])
            kb.op("dve", lambda: nc.vector.tensor_tensor(ab_[:, 0, :], ab_[:, 0, :], ab_[:, 1, :], ALU.add), reads=[ab_], writes=[ab_])
            yield
            kb.op("act", lambda: nc.scalar.activation(out=jk[:], in_=a[:], func=AF.Square, accum_out=st[:, 0:1]), reads=[a], writes=[jk, st])
            yield
            kb.op("dve", lambda: nc.vector.tensor_scalar(st[:, 1:2], st[:, 0:1], 1.0 / 128, EPS, ALU.mult, ALU.add), reads=[st], writes=[st])
            yield
            kb.op("act", lambda: nc.scalar.activation(out=st[:, 1:2], in_=st[:, 1:2], func=AF.Ln), reads=[st], writes=[st])
            kb.op("act", lambda: nc.scalar.activation(out=st[:, 1:2], in_=st[:, 1:2], func=AF.Exp, scale=-0.5), reads=[st], writes=[st])
            kb.op("act", lambda: nc.scalar.activation(out=jk[:], in_=g[:], func=AF.Exp, scale=-1.0), reads=[g], writes=[jk])
            yield
            kb.op("dve", lambda: nc.vector.tensor_scalar(jk[:], jk[:], 1.0, None, ALU.add), reads=[jk], writes=[jk])
            kb.op("dve", lambda: nc.vector.reciprocal(jk[:], jk[:]), reads=[jk], writes=[jk])
            kb.op("dve", lambda: nc.vector.tensor_tensor(g[:], g[:], jk[:], ALU.mult), reads=[g, jk], writes=[g])
            kb.op("dve", lambda: nc.vector.scalar_tensor_tensor(a[:], a[:], st[:, 1:2], nws[:], ALU.mult, ALU.mult), reads=[a, st, nws], writes=[a])
            kb.op("dve", lambda: nc.vector.tensor_tensor(a[:], a[:], g[:], ALU.mult), reads=[a, g], writes=[a])
            yield
            if out_cb is None:
                kb.dma("sp", o_dn[rows, :], a[:], reads=[a])
            else:
                out_cb(kb, ti * 128, a, ident, PS[4 + ti % 4], PS[4 + ti % 4][:, 0:128])
        interleave((fin_tile(ti) for ti in range((CTX + L) // 128)), W_)

    if own:
        kb.finish("sp")
        kb.close()
    return kb, ins


def ab_consts():
    m = {"c_ident": np.eye(128, dtype=np.float32)}
    i = np.arange(64)
    tri = np.zeros((64, 2, 64), np.float32)
    tri[:, 0, :] = (i[:, None] <= i[None, :]); tri[:, 1, :] = (i[:, None] >= i[None, :])
    m["c_tri"] = tri
    mS = np.zeros((64, 2, 64), np.float32); mI = np.zeros((64, 2, 64), np.float32)
    mS[:, 0, :] = -1.0 * (i[None, :] > i[:, None]); mS[:, 1, :] = -1.0 * (i[None, :] < i[:, None])
    mI[:, 0, :] = (i[None, :] >= i[:, None]); mI[:, 1, :] = (i[None, :] <= i[:, None])
    m["c_maskS"] = mS; m["c_maskI"] = mI
    oh = np.zeros((64, 2), np.float32); oh[63, 0] = 1; oh[0, 1] = 1
    m["c_onehot"] = oh
    return m


def pad_seq(a):
    return np.concatenate([np.zeros((1, a.shape[1]), np.float32), a, np.zeros((1, a.shape[1]), np.float32)], axis=0)


def ab_core_inputs(inputs, pl, pc, b, h, consts, ins):
    m = dict(consts)
    sl = lambda a, o: a[b][:, o + h * 128:o + (h + 1) * 128]
    if "d_qpad" in ins or "d_cwqk" in ins:
        for nm, off in ((("d_qpad", 0), ("d_kpad", 512)) if pl is not None else ()):
            m[nm] = np.concatenate([pad_seq(sl(pc, off)), pad_seq(sl(pl, off))], axis=0).T
        if pl is not None:
            m["d_vpad"] = np.concatenate([pad_seq(sl(pc, 1024)), pad_seq(sl(pl, 1024))], axis=0)
            m["d_gate"] = np.concatenate([sl(pc, 1536), sl(pl, 1536)], axis=0)
            cols = [2048 + kind * 8 + d * 4 + h for kind in range(2) for d in range(2)]
            ab = np.concatenate([pc[b][:, cols], pl[b][:, cols]], axis=0)
            m["d_ab"] = ab.reshape(NCK, 64, 4).transpose(1, 0, 2)
        cwf = inputs["dn_conv_w"][0]
        m["d_cwqk"] = np.concatenate([cwf[:, h * 128:(h + 1) * 128].T, cwf[:, 512 + h * 128:512 + (h + 1) * 128].T], axis=1)
        m["d_cwv"] = np.broadcast_to(cwf[:, 1024 + h * 128:1024 + (h + 1) * 128], (64, 3, 128))
        m["d_alog"] = np.broadcast_to(inputs["dn_a_log"][0][:, h], (64, 2))
        m["d_dtb"] = np.broadcast_to(inputs["dn_dt_bias"][0][:, h], (64, 2))
        m["d_nw"] = np.broadcast_to(inputs["dn_norm_w"][0], (128, 128))
    return {k: np.ascontiguousarray(m[k], dtype=np.float32) for k in ins if k in m}


SCALE = 128.0 ** -0.5


def build_swa(kb=None, pre=None, out_cb=None):
    own = kb is None
    if own:
        kb = KB()
    nc = kb.nc
    ins = []
    pre = pre or {}

    def din(name, shape):
        if name in pre:
            return pre[name]
        ins.append(name)
        t_ = kb.dram(name, shape)
        if not own:
            pre[name] = t_
        return t_

    RF = mybir.dt.float32r

    def RR(ap):
        return ap.bitcast(RF)
    qTd = din("s_qT", [128, CTX + L]); kTd = din("s_kT", [128, CTX + L]); vd = din("s_v", [CTX + L, 128])
    wqk = din("s_wqk", [128, 2]); sinkd = din("s_sink", [128, 1])
    Cd = din("s_C", [128, L]); Sd = din("s_S", [128, L])
    permd = din("s_perm", [128, 128]); identd = din("c_ident", [128, 128]); maskd = din("s_mask", [128, 384])
    o_sw = kb.dram("o_sw", [CTX + L, 128], kind="ExternalOutput") if out_cb is None else None

    BIG1 = kb.sb("BIG1", [128, L]); BIG2 = kb.sb("BIG2", [128, L])
    vv = BIG2[:, :].rearrange("p (b e) -> p b e", e=128)
    PS = [kb.ps("P%d" % i, [128, 512]) for i in range(8)]
    ident = kb.sb("ident", [128, 128]); perm = kb.sb("perm", [128, 128]); mask = kb.sb("mask", [128, 384])
    wq = kb.sb("wq", [128, 2]); sink = kb.sb("sink", [128, 1]); ones = kb.sb("ones", [128, 128])
    kc = kb.sb("kc", [128, 256]); qc = kb.sb("qc", [128, 256]); vc = kb.sb("vc", [128, 2, 128])
    for t_, d_ in ((ident, identd), (perm, permd), (mask, maskd), (wq, wqk), (sink, sinkd)):
        kb.dma("sp", t_[:], d_[:, :], writes=[t_])
    kb.op("dve", lambda: nc.vector.memset(ones[:], 1.0 / 128), writes=[ones])
    vstg = Ring([kb.sb("vstg%d" % i, [128, 8, 128]) for i in range(2)])
    ident_r = kb.sb("ident_r", [128, 128], dt=RF)
    kb.op("act", lambda: nc.scalar.copy(out=ident_r[:], in_=ident[:]), reads=[ident], writes=[ident_r])
    vs_ = vstg.next()
    kb.dma("pool", vs_[:, 0:2, :], vd[0:CTX, :].rearrange("(b p) e -> p b e", p=128), writes=[vs_])
    kb.op("pool", lambda: nc.gpsimd.tensor_copy(RR(vc[:]), vs_[:, 0:2, :]), reads=[vs_], writes=[vc])
    for q16 in range(16):
        vs_ = vstg.next()
        kb.dma("pool", vs_[:], vd[CTX + q16 * 1024: CTX + (q16 + 1) * 1024, :].rearrange("(b p) e -> p b e", p=128), writes=[vs_])
        if q16 % 2:
            kb.op("pool", lambda: nc.gpsimd.tensor_copy(RR(vv[:, q16 * 8:(q16 + 1) * 8, :]), vs_[:]), reads=[vs_], writes=[BIG2])
        else:
            kb.op("act", lambda: nc.scalar.copy(out=RR(vv[:, q16 * 8:(q16 + 1) * 8, :]), in_=vs_[:]), reads=[vs_], writes=[BIG2])

    rawr = Ring([kb.sb("raw%d" % i, [128, 512]) for i in range(2)])
    sqr = Ring([kb.sb("sq%d" % i, [128, 512]) for i in range(2)])
    rsr = Ring([kb.sb("rs%d" % i, [128, 512]) for i in range(2)])
    Cr = Ring([kb.sb("C%d" % i, [128, 512]) for i in range(2)])
    Sr_ = Ring([kb.sb("Sg%d" % i, [128, 512]) for i in range(2)])
    t2r = Ring([kb.sb("t2%d" % i, [128, 512]) for i in range(2)])
    qrr = Ring([kb.sb("qr%d" % i, [128, 512]) for i in range(2)])

    def prep(src_d, col0, N, wcol, out_t, out_ap, rope_t0):
        raw = rawr.next(); sq = sqr.next(); rs = rsr.next()
        kb.dma("sp", raw[:, :N], src_d[:, col0:col0 + N], writes=[raw])
        kb.op("act", lambda: nc.scalar.activation(out=sq[:, :N], in_=raw[:, :N], func=AF.Square), reads=[raw], writes=[sq])
        ps = PS[6]
        kb.op("pe", lambda: nc.tensor.matmul(ps[:, :N], ones[:], sq[:, :N], start=True, stop=True), reads=[ones, sq], writes=[ps])
        kb.op("dve", lambda: nc.vector.tensor_scalar(rs[:, :N], ps[:, :N], EPS, None, ALU.add), reads=[ps], writes=[rs])
        kb.op("act", lambda: nc.scalar.activation(out=rs[:, :N], in_=rs[:, :N], func=AF.Sqrt), reads=[rs], writes=[rs])
        kb.op("dve", lambda: nc.vector.reciprocal(rs[:, :N], rs[:, :N]), reads=[rs], writes=[rs])
        if rope_t0 is None:
            kb.op("dve", lambda: nc.vector.scalar_tensor_tensor(RR(out_ap), raw[:, :N], wq[:, wcol:wcol + 1], rs[:, :N], ALU.mult, ALU.mult), reads=[raw, wq, rs], writes=[out_t])
            return
        kb.op("dve", lambda: nc.vector.scalar_tensor_tensor(raw[:, :N], raw[:, :N], wq[:, wcol:wcol + 1], rs[:, :N], ALU.mult, ALU.mult), reads=[raw, wq, rs], writes=[raw])
        C = Cr.next(); S = Sr_.next(); t2 = t2r.next()
        kb.dma("pool", C[:, :N], Cd[:, rope_t0:rope_t0 + N], writes=[C])
        kb.dma("pool", S[:, :N], Sd[:, rope_t0:rope_t0 + N], writes=[S])
        pp = PS[7]
        kb.op("pe", lambda: nc.tensor.matmul(pp[:, :N], perm[:], raw[:, :N], start=True, stop=True), reads=[perm, raw], writes=[pp])
        kb.op("dve", lambda: nc.vector.tensor_tensor(t2[:, :N], pp[:, :N], S[:, :N], ALU.mult), reads=[pp, S], writes=[t2])
        kb.op("pool", lambda: nc.gpsimd.tensor_tensor(C[:, :N], raw[:, :N], C[:, :N], ALU.mult), reads=[raw, C], writes=[C])
        kb.op("dve", lambda: nc.vector.tensor_tensor(RR(out_ap), C[:, :N], t2[:, :N], ALU.add), reads=[C, t2], writes=[out_t])

    prep(kTd, 0, 256, 1, kc, kc[:, :], None)
    for blk in range(32):
        prep(kTd, CTX + blk * 512, 512, 1, BIG1, BIG1[:, blk * 512:(blk + 1) * 512], blk * 512)
    prep(qTd, 0, 256, 0, qc, qc[:, :], None)

    smr = Ring([kb.sb("sm%d" % i, [128, 640]) for i in range(4)])
    Pr = Ring([kb.sb("Pp%d" % i, [128, 640]) for i in range(4)])
    PTr = Ring([kb.sb("PT%d" % i, [128, 5, 128]) for i in range(4)])
    str_ = Ring([kb.sb("st%d" % i, [128, 8]) for i in range(4)])
    oor = Ring([kb.sb("oo%d" % i, [128, 128]) for i in range(4)])
    par = [0]

    def attend(q_t, q_ap, kloc, out_row0):
        p = par[0]; par[0] ^= 1
        PSl = PS[0 + p]; PSc = PS[2 + p]; PST = PS[4 + p]
        W = 0
        sm = smr.next(); P = Pr.next(); PT = PTr.next(); st = str_.next(); oo = oor.next()
        if kloc is not None:
            k0, k1, jlo, vblocks = kloc
            W = k1 - k0
            kb.op("pe", lambda: nc.tensor.matmul(PSl[:, 0:W], RR(q_ap), RR(BIG1[:, k0:k1]), start=True, stop=True), reads=[q_t, BIG1], writes=[PSl])
            kb.op("dve", lambda: nc.vector.tensor_tensor(sm[:, 0:W], PSl[:, 0:W], mask[:, jlo:jlo + W], ALU.add), reads=[PSl, mask], writes=[sm])
        else:
            vblocks = []
        kb.op("pe", lambda: nc.tensor.matmul(PSc[:, 0:256], RR(q_ap), RR(kc[:, :]), start=True, stop=True), reads=[q_t, kc], writes=[PSc])
        kb.op("act", lambda: nc.scalar.copy(out=sm[:, W:W + 256], in_=PSc[:, 0:256]), reads=[PSc], writes=[sm])
        yield
        WT = W + 256
        kb.op("dve", lambda: nc.vector.reduce_max(st[:, 0:1], sm[:, 0:WT], AX.X), reads=[sm], writes=[st])
        kb.op("dve", lambda: nc.vector.tensor_scalar(st[:, 1:2], st[:, 0:1], SCALE, sink[:, 0:1], ALU.mult, ALU.max), reads=[st, sink], writes=[st])
        kb.op("dve", lambda: nc.vector.tensor_scalar(st[:, 2:3], st[:, 1:2], -1.0, None, ALU.mult), reads=[st], writes=[st])
        yield
        kb.op("act", lambda: nc.scalar.activation(out=RR(P[:, 0:WT]), in_=sm[:, 0:WT], func=AF.Exp, scale=SCALE, bias=st[:, 2:3], accum_out=st[:, 3:4]),
              reads=[sm, st], writes=[P, st])
        kb.op("act", lambda: nc.scalar.activation(out=st[:, 4:5], in_=sink[:, 0:1], func=AF.Exp, scale=1.0, bias=st[:, 2:3]), reads=[sink, st], writes=[st])
        yield
        kb.op("dve", lambda: nc.vector.tensor_tensor(st[:, 5:6], st[:, 3:4], st[:, 4:5], ALU.add), reads=[st], writes=[st])
        kb.op("dve", lambda: nc.vector.reciprocal(st[:, 6:7], st[:, 5:6]), reads=[st], writes=[st])
        nblk = WT // 128
        for i in range(nblk):
            dst = PST[:, i * 128:(i + 1) * 128] if i < 4 else PSc[:, 384:512]
            dst_t = PST if i < 4 else PSc
            kb.op("pe", lambda: nc.tensor.matmul(dst, RR(P[:, i * 128:(i + 1) * 128]), ident_r[:], start=True, stop=True), reads=[P, ident_r], writes=[dst_t])
        n4 = min(4, nblk)
        yield
        kb.op("act", lambda: nc.scalar.copy(out=RR(PT[:, 0:n4, :]), in_=PST[:, 0:n4 * 128].rearrange("p (b q) -> p b q", q=128)), reads=[PST], writes=[PT])
        if nblk > 4:
            kb.op("dve", lambda: nc.vector.tensor_copy(RR(PT[:, 4, :]), PSc[:, 384:512]), reads=[PSc], writes=[PT])
        yield
        vsrc = [(BIG2, vv[:, vb, :]) for vb in vblocks] + [(vc, vc[:, 0, :]), (vc, vc[:, 1, :])]
        PO = PSc
        for i, (vt_, vap) in enumerate(vsrc):
            kb.op("pe", lambda: nc.tensor.matmul(PO[:, 256:384], RR(PT[:, i, :]), RR(vap), start=(i == 0), stop=(i == nblk - 1)), reads=[PT, vt_], writes=[PO], pe_acc=(i > 0))
        yield
        kb.op("dve", lambda: nc.vector.tensor_scalar(oo[:], PO[:, 256:384], st[:, 6:7], None, ALU.mult), reads=[PO, st], writes=[oo])
        if out_cb is None:
            kb.dma("sp", o_sw[out_row0:out_row0 + 128, :], oo[:], reads=[oo])
        else:
            out_cb(kb, out_row0, oo, ident, PSl, PSl[:, 384:512])

    NB = L // 128

    def gens():
        for cb in range(2):
            yield attend(qc, qc[:, cb * 128:(cb + 1) * 128], None, cb * 128)
        for sb_ in range(32):
            qr = qrr.next()

            def pg(qr=qr, sb_=sb_):
                prep(qTd, CTX + sb_ * 512, 512, 0, qr, qr[:, :], sb_ * 512)
                return
                yield
            yield pg()
            for qb in range(4):
                n = sb_ * 4 + qb
                b0 = max(n - 1, 0); b1 = min(n + 1, NB - 1)
                jlo = 0 if n > 0 else 128
                yield attend(qr, qr[:, qb * 128:(qb + 1) * 128], (b0 * 128, (b1 + 1) * 128, jlo, list(range(b0, b1 + 1))), CTX + n * 128)
    interleave(gens(), 2)
    if own:
        kb.finish("sp")
        kb.close()
    return kb, ins


def swa_consts():
    m = {"c_ident": np.eye(128, dtype=np.float32)}
    d = np.arange(128)
    partner = np.where(d % 64 < 32, d + 32, d - 32)
    perm = np.zeros((128, 128), np.float32); perm[partner, d] = 1.0
    m["s_perm"] = perm
    inv = (np.float32(10000.0) ** (-np.arange(32, dtype=np.float32) / np.float32(32))).astype(np.float32)
    t = np.arange(L)
    row = (t // 64).astype(np.float32); col = (t % 64).astype(np.float32)
    ar = (row[:, None] * inv[None, :]).astype(np.float32).astype(np.float64)
    ac = (col[:, None] * inv[None, :]).astype(np.float32).astype(np.float64)
    C = np.concatenate([np.cos(ar), np.cos(ar), np.cos(ac), np.cos(ac)], axis=1).T
    S = np.concatenate([-np.sin(ar), np.sin(ar), -np.sin(ac), np.sin(ac)], axis=1).T
    m["s_C"] = C.astype(np.float32); m["s_S"] = S.astype(np.float32)
    i = np.arange(128)[:, None]; jj = np.arange(384)[None, :]
    valid = (jj >= i) & (jj <= i + 256)
    m["s_mask"] = np.where(valid, 0.0, -1e30).astype(np.float32)
    return m


def swa_core_inputs(inputs, pl, pc, b, h, consts, ins):
    m = dict(consts)
    g = h // 2
    if pl is not None:
        cat = lambda o: np.concatenate([pc[b][:, o:o + 128], pl[b][:, o:o + 128]], axis=0)
        m["s_qT"] = cat(2064 + h * 128).T
        m["s_kT"] = cat(2064 + 512 + g * 128).T
        m["s_v"] = cat(2064 + 768 + g * 128)
    m["s_wqk"] = np.stack([inputs["swa_q_norm_w"][0], inputs["swa_k_norm_w"][0]], axis=1)
    m["s_sink"] = np.full((128, 1), inputs["swa_sink"][0][h], np.float32)
    return {k: np.ascontiguousarray(m[k], dtype=np.float32) for k in ins if k in m}


L = 16384
NCH = 130
EPS = 1e-6


def dft_consts():
    n = np.arange(128)
    k1 = np.arange(256)
    c = {}
    a1 = 2 * np.pi * np.outer(n, k1) / 256.0
    c["F1cat"] = np.concatenate([np.cos(a1), -np.sin(a1)], axis=1)
    at = 2 * np.pi * np.outer(n, k1) / 32768.0
    twr, twi = np.cos(at), -np.sin(at)
    c["TwRR"] = np.concatenate([twr, twr], axis=1)
    c["TwII"] = np.concatenate([twi, twi], axis=1)
    a2 = 2 * np.pi * np.outer(n, n) / 128.0
    f2r, f2i = np.cos(a2), -np.sin(a2)
    c["F2"] = np.concatenate([f2r, f2i, -f2i], axis=1)
    g2r, g2i = np.cos(a2), np.sin(a2)
    c["G2a"] = np.concatenate([g2r, g2i], axis=1)
    c["G2b"] = np.concatenate([-g2i, g2r], axis=1)
    atc = 2 * np.pi * np.outer(k1, n) / 32768.0
    tcr = np.cos(atc).reshape(2, 128, 128).transpose(1, 0, 2)
    tci = np.sin(atc).reshape(2, 128, 128).transpose(1, 0, 2)
    c["TcR"] = tcr.reshape(128, 256)
    c["TcI"] = tci.reshape(128, 256)
    ag = 2 * np.pi * np.outer(k1, n) / 256.0
    g1r = (np.cos(ag) / 32768.0).reshape(2, 128, 128).transpose(1, 0, 2)
    g1i = (-np.sin(ag) / 32768.0).reshape(2, 128, 128).transpose(1, 0, 2)
    c["G1"] = np.concatenate([g1r.reshape(128, 256), g1i.reshape(128, 256)], axis=1)
    return {k: np.ascontiguousarray(v, dtype=np.float32) for k, v in c.items()}


def build_cd(do_ret=True, do_hy=True, kb=None, pre=None, out_cb=None, hy_out_cb=None, pad_src=None):
    own = kb is None
    if own:
        kb = KB()
    nc = kb.nc
    ins = []
    pre = pre or {}

    def din(name, shape):
        if name in pre:
            return pre[name]
        ins.append(name)
        t_ = kb.dram(name, shape)
        if not own:
            pre[name] = t_
        return t_

    BIG1 = kb.sb("BIG1", [128, 16384])
    BIG2 = kb.sb("BIG2", [128, 16384])
    if do_hy:
        PA = [kb.ps("PA", [128, 2, 512]) for i in range(2)]
        PB = [kb.ps("PB", [128, 2, 512]) for i in range(2)]
    else:
        PS8 = [kb.ps("RP%d" % i, [128, 512]) for i in range(8)]
    ones = kb.sb("ones", [128, 128])
    kb.op("dve", lambda: nc.vector.memset(ones[:], 1.0), writes=[ones])

    RF = mybir.dt.float32r

    def RR(ap):
        return ap.bitcast(RF)

    if do_ret:
        qkT = din("r_qkT", [128, NCH, 256])
        CT = din("r_CT", [128, NCH, 256])
        ST = din("r_ST", [128, NCH, 256])
        vtm = din("r_v", [NCH * 128, 128])
        gtm = din("r_g", [L, 128])
        gnw = din("r_gnw", [128, 128])
        lgt = din("r_logit", [128, 2])
        cperm = din("c_perm", [128, 128])
        cident = din("c_ident", [128, 128])
        cposrow = din("c_posrow", [128, 2, 256])
        cmaskT = din("c_maskT", [128, 2, 128])
        o_ret = kb.dram("o_ret", [L, 128], kind="ExternalOutput") if out_cb is None else None

        perm = kb.sb("perm", [128, 128]); ident = kb.sb("ident", [128, 128])
        posrow = kb.sb("posrow", [128, 2, 256]); maskT = kb.sb("maskT", [128, 2, 128])
        lg = kb.sb("lg", [128, 2]); nlg = kb.sb("nlg", [128, 2]); gC = kb.sb("gC", [128, 2]); qks = kb.sb("qks", [128, 2, 256])
        gw = kb.sb("gw", [128, 128])
        kb.dma("sp", perm[:], cperm[:, :], writes=[perm]); kb.dma("sp", ident[:], cident[:, :], writes=[ident])
        kb.dma("sp", posrow[:], cposrow[:, :, :], writes=[posrow]); kb.dma("sp", maskT[:], cmaskT[:, :, :], writes=[maskT])
        kb.dma("sp", lg[:], lgt[:, :], writes=[lg]); kb.dma("sp", gw[:], gnw[:, :], writes=[gw])
        perm_r = kb.sb("perm_r", [128, 128], dt=RF); ident_r = kb.sb("ident_r", [128, 128], dt=RF)
        kb.op("act", lambda: nc.scalar.copy(out=perm_r[:], in_=perm[:]), reads=[perm], writes=[perm_r])
        kb.op("act", lambda: nc.scalar.copy(out=ident_r[:], in_=ident[:]), reads=[ident], writes=[ident_r])
        qkrr = Ring([kb.sb("qkr%d" % i, [128, 256], dt=RF) for i in range(4)])
        vrr = Ring([kb.sb("vr%d" % i, [128, 128], dt=RF) for i in range(4)])
        kb.op("act", lambda: nc.scalar.activation(out=lg[:], in_=lg[:], func=AF.Exp, scale=-1.0), reads=[lg], writes=[lg])
        kb.op("dve", lambda: nc.vector.tensor_scalar(lg[:], lg[:], 1.0, None, ALU.add), reads=[lg], writes=[lg])
        kb.op("act", lambda: nc.scalar.activation(out=lg[:], in_=lg[:], func=AF.Ln), reads=[lg], writes=[lg])
        kb.op("dve", lambda: nc.vector.tensor_scalar(lg[:], lg[:], -1.0, None, ALU.mult), reads=[lg], writes=[lg])
        for d in range(2):
            kb.op("act", lambda: nc.scalar.activation(out=qks[:, d, :], in_=posrow[:, d, :], func=AF.Exp, scale=lg[:, d:d + 1]),
                  reads=[posrow, lg], writes=[qks])
        kb.op("act", lambda: nc.scalar.activation(out=gC[:], in_=lg[:], func=AF.Exp, scale=128.0), reads=[lg], writes=[gC])

        oacc = BIG1
        qkr_ = Ring([kb.sb("qk%d" % i, [128, 256]) for i in range(6)])
        ctr_ = Ring([kb.sb("ct%d" % i, [128, 256]) for i in range(6)])
        str_ = Ring([kb.sb("st%d" % i, [128, 256]) for i in range(6)])
        vr_ = Ring([kb.sb("v%d" % i, [128, 128]) for i in range(6)])
        t1r = Ring([kb.sb("t1%d" % i, [128, 256]) for i in range(4)])
        t2r = Ring([kb.sb("t2%d" % i, [128, 256]) for i in range(4)])
        ktr = Ring([kb.sb("kt%d" % i, [128, 128]) for i in range(4)])
        ptr = Ring([kb.sb("pt%d" % i, [128, 128]) for i in range(4)])
        Sr = Ring([kb.sb("S%d" % i, [128, 128]) for i in range(2)])

        def psl(i):
            if not do_hy:
                return (PS8[i], PS8[i][:, :])
            return (PA[i // 2], PA[i // 2][:, i % 2, :]) if i < 4 else (PB[(i - 4) // 2], PB[(i - 4) // 2][:, i % 2, :])
        psi = [0]

        def nps():
            i = psi[0]; psi[0] = (i + 1) % 8
            return psl(i)

        def ret_dir(d):
            order = [0, 1] + list(range(2, NCH)) if d == 0 else [1, 0] + list(range(NCH - 1, 1, -1))
            Sr = Ring([kb.sb("S%d_%d" % (d, i), [128, 128]) for i in range(2)])
            S = Sr.next()
            kb.op("dve", lambda: nc.vector.tensor_scalar(RR(S[:]), ident[:], 0.0, None, ALU.mult), reads=[ident], writes=[S])
            for ci in order:
                qk = qkr_.next(); ct = ctr_.next(); st = str_.next(); v = vr_.next()
                kb.dma("sp", qk[:], qkT[:, ci, :], writes=[qk])
                kb.dma("pool", ct[:], CT[:, ci, :], writes=[ct])
                kb.dma("pool", st[:], ST[:, ci, :], writes=[st])
                kb.dma("sp", v[:], vtm[ci * 128:(ci + 1) * 128, :], writes=[v])
                qk_r = qkrr.next(); v_r = vrr.next()
                kb.op("pool", lambda: nc.gpsimd.tensor_copy(qk_r[:], qk[:]), reads=[qk], writes=[qk_r])
                kb.op("act", lambda: nc.scalar.copy(out=v_r[:], in_=v[:]), reads=[v], writes=[v_r])
                pT, p1 = nps()
                kb.op("pe", lambda: nc.tensor.matmul(p1[:, :256], perm_r[:], qk_r[:], start=True, stop=True), reads=[perm_r, qk_r], writes=[pT])
                t1 = t1r.next(); t2 = t2r.next()
                kb.op("pool", lambda: nc.gpsimd.tensor_tensor(RR(t1[:]), qk[:], ct[:], ALU.mult), reads=[qk, ct], writes=[t1])
                kb.op("dve", lambda: nc.vector.tensor_tensor(t2[:], p1[:, :256], st[:], ALU.mult), reads=[pT, st], writes=[t2])
                kb.op("dve", lambda: nc.vector.tensor_tensor(RR(t1[:]), t1[:], t2[:], ALU.add), reads=[t1, t2], writes=[t1])
                kb.op("dve", lambda: nc.vector.tensor_tensor(RR(t1[:]), t1[:], qks[:, d, :], ALU.mult), reads=[t1, qks], writes=[t1])
                qT = RR(t1[:, 0:128]); kT = RR(t1[:, 128:256])
                pT2, p2 = nps()
                kb.op("pe", lambda: nc.tensor.matmul(p2[:, :128], kT, ident_r[:], start=True, stop=True), reads=[t1, ident_r], writes=[pT2])
                kt = ktr.next()
                kb.op("act", lambda: nc.scalar.activation(out=RR(kt[:]), in_=p2[:, :128], func=AF.Copy, scale=gC[:, d:d + 1]), reads=[pT2, gC], writes=[kt])
                if ci >= 2:
                    pT3, p3 = nps()
                    kb.op("pe", lambda: nc.tensor.matmul(p3[:, :128], kT, qT, start=True, stop=True), reads=[t1], writes=[pT3])
                    pt = ptr.next()
                    kb.op("dve", lambda: nc.vector.tensor_tensor(RR(pt[:]), p3[:, :128], maskT[:, d, :], ALU.mult), reads=[pT3, maskT], writes=[pt])
                    pT4, p4 = nps()
                    kb.op("pe", lambda: nc.tensor.matmul(p4[:, :128], RR(pt[:]), v_r[:], start=True, stop=False), reads=[pt, v_r], writes=[pT4])
                    kb.op("pe", lambda: nc.tensor.matmul(p4[:, :128], qT, RR(S[:]), start=False, stop=True), reads=[t1, S], writes=[pT4], pe_acc=True)
                    li = ci - 2
                    if first_write[li]:
                        first_write[li] = False
                        kb.op("act", lambda: nc.scalar.copy(out=oacc[:, li * 128:(li + 1) * 128], in_=p4[:, :128]), reads=[pT4], writes=[oacc])
                    else:
                        kb.op("dve", lambda: nc.vector.tensor_tensor(oacc[:, li * 128:(li + 1) * 128], oacc[:, li * 128:(li + 1) * 128], p4[:, :128], ALU.add),
                              reads=[pT4, oacc], writes=[oacc])
                pT5, p5 = nps()
                kb.op("pe", lambda: nc.tensor.matmul(p5[:, :128], RR(kt[:]), v_r[:], start=True, stop=True), reads=[kt, v_r], writes=[pT5])
                S2 = Sr.next()
                kb.op("dve", lambda: nc.vector.scalar_tensor_tensor(RR(S2[:]), S[:], gC[:, d:d + 1], p5[:, :128], ALU.mult, ALU.add),
                      reads=[S, gC, pT5], writes=[S2])
                S = S2
                yield
        first_write = [True] * 128
        gens = [ret_dir(0), ret_dir(1)]
        while gens:
            for g in list(gens):
                try:
                    next(g)
                except StopIteration:
                    gens.remove(g)
        W_ = 4
        gr = Ring([kb.sb("g%d" % i, [128, 128]) for i in range(W_ + 1)])
        cr = Ring([kb.sb("c%d" % i, [128, 128]) for i in range(W_ + 1)])
        jr = Ring([kb.sb("j%d" % i, [128, 128]) for i in range(W_ + 1)])
        yr = Ring([kb.sb("y%d" % i, [128, 128]) for i in range(W_ + 1)])
        s1r = Ring([kb.sb("s1%d" % i, [128, 4]) for i in range(W_ + 1)])

        def gn_tile(li):
            g = gr.next(); cen = cr.next(); y = yr.next(); st = s1r.next(); jk = jr.next()
            o = oacc[:, li * 128:(li + 1) * 128]
            kb.dma("pool", g[:], gtm[li * 128:(li + 1) * 128, :], writes=[g])
            kb.op("dve", lambda: nc.vector.reduce_sum(st[:, 0:1], o, AX.X), reads=[oacc], writes=[st])
            kb.op("dve", lambda: nc.vector.tensor_scalar(st[:, 1:2], st[:, 0:1], -1.0 / 128, None, ALU.mult), reads=[st], writes=[st])
            kb.op("dve", lambda: nc.vector.tensor_scalar(cen[:], o, st[:, 1:2], None, ALU.add), reads=[oacc, st], writes=[cen])
            yield
            kb.op("act", lambda: nc.scalar.activation(out=jk[:], in_=cen[:], func=AF.Square, accum_out=st[:, 2:3]), reads=[cen], writes=[jk, st])
            yield
            kb.op("dve", lambda: nc.vector.tensor_scalar(st[:, 3:4], st[:, 2:3], 1.0 / 128, EPS, ALU.mult, ALU.add), reads=[st], writes=[st])
            yield
            kb.op("act", lambda: nc.scalar.activation(out=st[:, 3:4], in_=st[:, 3:4], func=AF.Ln), reads=[st], writes=[st])
            kb.op("act", lambda: nc.scalar.activation(out=st[:, 3:4], in_=st[:, 3:4], func=AF.Exp, scale=-0.5), reads=[st], writes=[st])
            kb.op("act", lambda: nc.scalar.activation(out=jk[:], in_=g[:], func=AF.Exp, scale=-1.0), reads=[g], writes=[jk])
            yield
            kb.op("dve", lambda: nc.vector.tensor_scalar(jk[:], jk[:], 1.0, None, ALU.add), reads=[jk], writes=[jk])
            kb.op("dve", lambda: nc.vector.reciprocal(jk[:], jk[:]), reads=[jk], writes=[jk])
            kb.op("dve", lambda: nc.vector.tensor_tensor(g[:], g[:], jk[:], ALU.mult), reads=[g, jk], writes=[g])
            kb.op("dve", lambda: nc.vector.scalar_tensor_tensor(y[:], cen[:], st[:, 3:4], gw[:], ALU.mult, ALU.mult), reads=[cen, st, gw], writes=[y])
            kb.op("dve", lambda: nc.vector.tensor_tensor(y[:], y[:], g[:], ALU.mult), reads=[y, g], writes=[y])
            yield
            if out_cb is None:
                kb.dma("sp", o_ret[li * 128:(li + 1) * 128, :], y[:], reads=[y])
            elif do_hy:
                out_cb(kb, li * 128, y, ident, PB[1], PB[1][:, 1, 0:128])
            else:
                out_cb(kb, li * 128, y, ident, PS8[li % 4], PS8[li % 4][:, 0:128])
        interleave((gn_tile(li) for li in range(128)), W_)

    if do_hy:
        F32R = mybir.dt.float32r

        def R_(ap):
            return ap.bitcast(F32R)
        GC = 2
        NG = 128 // GC
        pads = din("h_pads", [3, 128, 128, 130]) if pad_src is None else None
        cw = din("h_cw", [128, 3, 3, 128])
        skp = din("h_skip", [128, 2, 128])
        zT = din("h_zT", [17, L])
        w1 = din("h_w1", [17, 64]); b1 = din("h_b1", [64, 1]); w2 = din("h_w2", [64, 64]); b2 = din("h_b2", [64, 1])
        w3 = din("h_w3", [64, 4, 128])
        dec = din("h_dec", [128, 128, 128])
        cn = {}
        cstg = kb.sb("cstg", [128, 512])
        for nm, w in (("F1cat", 512), ("TwRR", 512), ("TwII", 512), ("F2", 384), ("G2a", 256), ("G2b", 256), ("TcR", 256), ("TcI", 256), ("G1", 512)):
            if nm in ("F1cat", "F2", "G2a", "G2b", "G1"):
                cn[nm] = (din("c_" + nm, [128, w]), kb.sb("k_" + nm, [128, w], dt=F32R))
                kb.dma("sp", cstg[:, :w], cn[nm][0][:, :], writes=[cstg])
                kb.op("act", lambda: nc.scalar.copy(out=cn[nm][1][:], in_=cstg[:, :w]), reads=[cstg], writes=[cn[nm][1]])
            else:
                cn[nm] = (din("c_" + nm, [128, w]), kb.sb("k_" + nm, [128, w]))
                kb.dma("sp", cn[nm][1][:], cn[nm][0][:, :], writes=[cn[nm][1]])
        F1cat, TwRR, TwII, F2, G2a, G2b, TcR, TcI, G1 = [cn[k][1] for k in ("F1cat", "TwRR", "TwII", "F2", "G2a", "G2b", "TcR", "TcI", "G1")]
        Hs = kb.dram("h_Hs", [4, NG, 128, GC * 512], kind="Internal")
        o_hy = kb.dram("o_hy", [128, 128, 128], kind="ExternalOutput") if hy_out_cb is None else None

        def bcg(t, w):
            return t[:, :].rearrange("p (o w) -> p o w", o=1).to_broadcast([128, GC, w])

        M1 = [kb.sb("M1", [128, GC, 512]) for i in range(2)]; M2 = [kb.sb("M2", [128, GC, 512]) for i in range(2)]
        Bt = [kb.sb("Bt", [128, GC, 512]) for i in range(2)]
        GH1 = [kb.sb("GH1", [128, GC, 512]) for i in range(2)]; GH2 = [kb.sb("GH2", [128, GC, 512]) for i in range(2)]

        class V:
            def __init__(self, t, pat, **kw):
                self.t = t; self.pat = pat; self.kw = kw
            def __getitem__(self, idx):
                return self.t[:].rearrange(self.pat, **self.kw)[idx]

        def fwd_fft(s, src_t, lhs_of):
            pa, pb, m1, m2, bt = PA[s], PB[s], M1[s], M2[s], Bt[s]
            for c in range(GC):
                kb.op("pe", lambda: nc.tensor.matmul(pa[:, c, :], R_(lhs_of(c)), F1cat[:], start=True, stop=True), reads=[src_t, F1cat], writes=[pa])
            yield
            kb.op("dve", lambda: nc.vector.tensor_tensor(m1[:], pa[:], bcg(TwRR, 512), ALU.mult), reads=[pa, TwRR], writes=[m1])
            kb.op("dve", lambda: nc.vector.tensor_tensor(m2[:], pa[:], bcg(TwII, 512), ALU.mult), reads=[pa, TwII], writes=[m2])
            yield
            kb.op("dve", lambda: nc.vector.tensor_tensor(R_(bt[:, :, 0:256]), m1[:, :, 0:256], m2[:, :, 256:512], ALU.subtract), reads=[m1, m2], writes=[bt])
            kb.op("pool", lambda: nc.gpsimd.tensor_tensor(R_(bt[:, :, 256:512]), m2[:, :, 0:256], m1[:, :, 256:512], ALU.add), reads=[m1, m2], writes=[bt])
            yield
            for c in range(GC):
                kb.op("pe", lambda: nc.tensor.matmul(pb[:, c, 0:256], F2[:, 0:128], R_(bt[:, c, 0:256]), start=True, stop=False), reads=[F2, bt], writes=[pb])
                kb.op("pe", lambda: nc.tensor.matmul(pb[:, c, 0:256], F2[:, 256:384], R_(bt[:, c, 256:512]), start=False, stop=True), reads=[F2, bt], writes=[pb], pe_acc=True)
                kb.op("pe", lambda: nc.tensor.matmul(pb[:, c, 256:512], F2[:, 128:256], R_(bt[:, c, 0:256]), start=True, stop=False), reads=[F2, bt], writes=[pb], pe_acc=True)
                kb.op("pe", lambda: nc.tensor.matmul(pb[:, c, 256:512], F2[:, 0:128], R_(bt[:, c, 256:512]), start=False, stop=True), reads=[F2, bt], writes=[pb], pe_acc=True)
            yield

        h1T = BIG2
        h2T = BIG1
        taps = BIG2
        w1s = kb.sb("w1s", [17, 64]); w2s = kb.sb("w2s", [64, 64]); w3s = kb.sb("w3s", [64, 512])
        b1s = kb.sb("b1s", [64, 1]); b2s = kb.sb("b2s", [64, 1])
        kb.dma("sp", w1s[:], w1[:, :], writes=[w1s]); kb.dma("sp", w2s[:], w2[:, :], writes=[w2s])
        kb.dma("sp", w3s[:], w3.rearrange("k g c -> k (g c)"), writes=[w3s])
        kb.dma("sp", b1s[:], b1[:, :], writes=[b1s]); kb.dma("sp", b2s[:], b2[:, :], writes=[b2s])
        kb.op("dve", lambda: nc.vector.tensor_scalar(b1s[:], b1s[:], 1.0 / 3, None, ALU.mult), reads=[b1s], writes=[b1s])
        kb.op("dve", lambda: nc.vector.tensor_scalar(b2s[:], b2s[:], 1.0 / 3, None, ALU.mult), reads=[b2s], writes=[b2s])
        ztr = Ring([kb.sb("zt%d" % i, [17, 512]) for i in range(2)])
        sr_ = Ring([M1[0], Bt[0], M1[1], Bt[1]])
        s2r_ = Ring([M2[0], GH1[0], M2[1], GH1[1]])

        def sin3(ps_t, ps_ap, bias, out_t, out_ap):
            s_t = sr_.next(); q_t = s2r_.next()
            s = V(s_t, "p c w -> p (c w)")[0:64, 0:512]; q = V(q_t, "p c w -> p (c w)")[0:64, 0:512]
            kb.op("act", lambda: nc.scalar.activation(out=R_(s), in_=ps_ap, func=AF.Sin, scale=1.0 / 3, bias=bias[:, 0:1]), reads=[ps_t, bias], writes=[s_t])
            kb.op("dve", lambda: nc.vector.tensor_tensor(q, s, s, ALU.mult), reads=[s_t], writes=[q_t])
            kb.op("dve", lambda: nc.vector.tensor_scalar(q, q, -4.0, 3.0, ALU.mult, ALU.add), reads=[q_t], writes=[q_t])
            kb.op("dve", lambda: nc.vector.tensor_tensor(R_(out_ap), q, s, ALU.mult), reads=[q_t, s_t], writes=[out_t])

        for blk in range(32):
            z = ztr.next()
            kb.dma("sp", z[:], zT[:, blk * 512:(blk + 1) * 512], writes=[z])
            pa = PA[blk % 2]
            kb.op("pe", lambda: nc.tensor.matmul(pa[0:64, 0, :], w1s[:], z[:], start=True, stop=True), reads=[w1s, z], writes=[pa])
            sin3(pa, pa[0:64, 0, :], b1s, h1T, h1T[0:64, blk * 512:(blk + 1) * 512])
        for blk in range(32):
            pa = PB[blk % 2]
            kb.op("pe", lambda: nc.tensor.matmul(pa[0:64, 1, :], w2s[:], h1T[0:64, blk * 512:(blk + 1) * 512], start=True, stop=True), reads=[w2s, h1T], writes=[pa])
            sin3(pa, pa[0:64, 1, :], b2s, h2T, h2T[0:64, blk * 512:(blk + 1) * 512])

        rsum = kb.sb("rsum", [128, 128]); rtmp = kb.sb("rtmp", [128, 128])
        rn = kb.sb("rn", [128, 2, 128])
        tapsv = taps[:, :].rearrange("p (n c) -> p n c", c=128)
        for o in range(2):
            for d in range(2):
                gi = o * 2 + d
                for nb in range(16):
                    s_ = nb % 2
                    dc_t = GH1[s_]; dc = V(dc_t, "p c (a n) -> p (c a) n", n=128)
                    kb.dma("pool", dc[:], dec[:, nb * 8:(nb + 1) * 8, :], writes=[dc_t])
                    pa = PA[s_]
                    for q2 in range(2):
                        for j in range(4):
                            n2 = nb * 8 + q2 * 4 + j
                            kb.op("pe", lambda: nc.tensor.matmul(pa[:, q2, j * 128:(j + 1) * 128], h2T[0:64, n2:L:128], w3s[:, gi * 128:(gi + 1) * 128],
                                                                 start=True, stop=True), reads=[h2T, w3s], writes=[pa])
                    kb.op("dve", lambda: nc.vector.tensor_tensor(R_(tapsv[:, nb * 8:(nb + 1) * 8, :]), pa[:].rearrange("p a (j c) -> p (a j) c", c=128), dc[:], ALU.mult),
                          reads=[pa, dc_t], writes=[taps])
                    ab_t = M1[s_]; ab = V(ab_t, "p c (a n) -> p (c a) n", n=128)
                    kb.op("act", lambda: nc.scalar.activation(out=ab[:], in_=tapsv[:, nb * 8:(nb + 1) * 8, :], func=AF.Abs), reads=[taps], writes=[ab_t])
                    if nb == 0:
                        kb.op("dve", lambda: nc.vector.reduce_sum(rsum[:], ab[:].rearrange("p n c -> p c n"), AX.X), reads=[ab_t], writes=[rsum])
                    else:
                        kb.op("dve", lambda: nc.vector.reduce_sum(rtmp[:], ab[:].rearrange("p n c -> p c n"), AX.X), reads=[ab_t], writes=[rtmp])
                        kb.op("dve", lambda: nc.vector.tensor_tensor(rsum[:], rsum[:], rtmp[:], ALU.add), reads=[rsum, rtmp], writes=[rsum])
                pt_ = PB[0]
                kb.op("pe", lambda: nc.tensor.matmul(pt_[:, 0, 0:128], ones[:], rsum[:], start=True, stop=True), reads=[ones, rsum], writes=[pt_])
                if d == 0:
                    kb.op("dve", lambda: nc.vector.tensor_copy(rn[:, o, :], pt_[:, 0, 0:128]), reads=[pt_], writes=[rn])
                else:
                    kb.op("dve", lambda: nc.vector.tensor_tensor(rn[:, o, :], rn[:, o, :], pt_[:, 0, 0:128], ALU.add), reads=[pt_, rn], writes=[rn])
                    kb.op("dve", lambda: nc.vector.tensor_scalar(rn[:, o, :], rn[:, o, :], EPS, None, ALU.add), reads=[rn], writes=[rn])
                    kb.op("dve", lambda: nc.vector.reciprocal(rn[:, o, :], rn[:, o, :]), reads=[rn], writes=[rn])
                    kb.op("dve", lambda: nc.vector.tensor_scalar(R_(tapsv[0:1, 0:1, :]), tapsv[0:1, 0:1, :], 0.0, None, ALU.mult), reads=[taps], writes=[taps])

                def filt_group(gi, cg):
                    s_ = cg % 2
                    yield from fwd_fft(s_, taps, lambda c: tapsv[:, :, cg * GC + c])
                    hs = GH2[s_]
                    kb.op("act", lambda: nc.scalar.copy(out=R_(hs[:]), in_=PB[s_][:]), reads=[PB[s_]], writes=[hs])
                    yield
                    kb.dma("sp", Hs[gi, cg].rearrange("p (c w) -> p c w", w=512), hs[:], reads=[hs])
                interleave((filt_group(gi, cg) for cg in range(NG)), 2)
        hs_done = [(k, c) for k, c in kb.cnt.items() if k.startswith("dsp") and c]

        u = BIG1[:, :].rearrange("p (c n) -> p c n", n=128)
        zz = BIG2[:, :].rearrange("p (c n) -> p c n", n=128)
        cws = kb.sb("cws", [128, 9, 128]); sks = kb.sb("sks", [128, 2, 128])
        kb.dma("sp", cws[:], cw.rearrange("p a k c -> p (a k) c"), writes=[cws]); kb.dma("sp", sks[:], skp[:, :, :], writes=[sks])
        pad_ = [kb.sb("pad", [128, GC, 130]) for i in range(2)]
        cv_ = [kb.sb("cv", [128, GC, 128]) for i in range(2)]
        cv2_ = [kb.sb("cv2", [128, GC, 128]) for i in range(2)]
        tt__ = [kb.sb("tt", [128, GC, 128]) for i in range(2)]

        def wb(part, k, cg):
            return cws[:, part * 3 + k, cg * GC:(cg + 1) * GC].rearrange("p (c o) -> p c o", o=1).to_broadcast([128, GC, 128])

        def sconv(s_, part, cg, out_t, out_ap):
            pd = pad_[s_]; cv2 = cv2_[s_]
            if pad_src is None:
                kb.dma("pool", pd[:], pads[part, :, cg * GC:(cg + 1) * GC, :], writes=[pd])
            else:
                for c in range(GC):
                    kb.dma("pool", pd[:, c, :], pad_src(part, cg * GC + c), writes=[pd])
            kb.op("pool", lambda: nc.gpsimd.tensor_tensor(cv2[:], pd[:, :, 0:128], wb(part, 0, cg), ALU.mult), reads=[pd, cws], writes=[cv2])
            kb.op("dve", lambda: nc.vector.tensor_tensor(R_(out_ap), pd[:, :, 1:129], wb(part, 1, cg), ALU.mult), reads=[pd, cws], writes=[out_t])
            kb.op("dve", lambda: nc.vector.tensor_tensor(R_(out_ap), out_ap, cv2[:], ALU.add), reads=[out_t, cv2], writes=[out_t])
            kb.op("pool", lambda: nc.gpsimd.tensor_tensor(cv2[:], pd[:, :, 2:130], wb(part, 2, cg), ALU.mult), reads=[pd, cws], writes=[cv2])
            kb.op("dve", lambda: nc.vector.tensor_tensor(R_(out_ap), out_ap, cv2[:], ALU.add), reads=[out_t, cv2], writes=[out_t])

        for cg in range(NG):
            sconv(cg % 2, 0, cg, BIG1, u[:, cg * GC:(cg + 1) * GC, :])
        srcs = [(BIG1, u), (BIG2, zz)]

        def data_group(o, cg, src_t, src, dst_t, dst):
            s_ = cg % 2
            pa, pb, m1, m2, bt = PA[s_], PB[s_], M1[s_], M2[s_], Bt[s_]
            hf = GH1[s_]; hb = m2; Hc = hf; Yt = bt; Zp_t = GH2[s_]
            kb.dma("pool", hf[:], Hs[o * 2 + 0, cg].rearrange("p (c w) -> p c w", w=512), writes=[hf])
            kb.dma("pool", hb[:], Hs[o * 2 + 1, cg].rearrange("p (c w) -> p c w", w=512), writes=[hb])
            rb = rn[:, o, cg * GC:(cg + 1) * GC].rearrange("p (c o) -> p c o", o=1).to_broadcast([128, GC, 256])
            kb.op("pool", lambda: nc.gpsimd.tensor_tensor(Hc[:, :, 0:256], hf[:, :, 0:256], hb[:, :, 0:256], ALU.add), reads=[hf, hb], writes=[Hc])
            kb.op("pool", lambda: nc.gpsimd.tensor_tensor(Hc[:, :, 256:512], hf[:, :, 256:512], hb[:, :, 256:512], ALU.subtract), reads=[hf, hb], writes=[Hc])
            yield
            kb.op("pool", lambda: nc.gpsimd.tensor_tensor(Hc[:, :, 0:256], Hc[:, :, 0:256], rb, ALU.mult), reads=[Hc, rn], writes=[Hc])
            kb.op("pool", lambda: nc.gpsimd.tensor_tensor(Hc[:, :, 256:512], Hc[:, :, 256:512], rb, ALU.mult), reads=[Hc, rn], writes=[Hc])
            yield from fwd_fft(s_, src_t, lambda c: src[:, cg * GC + c, :])
            HRR = Hc[:, :, 0:256].rearrange("p c (o w) -> p c o w", o=1).to_broadcast([128, GC, 2, 256])
            HII = Hc[:, :, 256:512].rearrange("p c (o w) -> p c o w", o=1).to_broadcast([128, GC, 2, 256])
            PBv = pb[:].rearrange("p c (o w) -> p c o w", o=2)
            kb.op("dve", lambda: nc.vector.tensor_tensor(m1[:].rearrange("p c (o w) -> p c o w", o=2), PBv, HRR, ALU.mult), reads=[pb, Hc], writes=[m1])
            kb.op("dve", lambda: nc.vector.tensor_tensor(m2[:].rearrange("p c (o w) -> p c o w", o=2), PBv, HII, ALU.mult), reads=[pb, Hc], writes=[m2])
            yield
            kb.op("dve", lambda: nc.vector.tensor_tensor(R_(Yt[:, :, 0:256]), m1[:, :, 0:256], m2[:, :, 256:512], ALU.subtract), reads=[m1, m2], writes=[Yt])
            kb.op("pool", lambda: nc.gpsimd.tensor_tensor(R_(Yt[:, :, 256:512]), m2[:, :, 0:256], m1[:, :, 256:512], ALU.add), reads=[m1, m2], writes=[Yt])
            yield
            for c in range(GC):
                for hf_ in range(2):
                    kb.op("pe", lambda: nc.tensor.matmul(pa[:, c, hf_ * 256:(hf_ + 1) * 256], R_(Yt[:, c, hf_ * 128:(hf_ + 1) * 128]), G2a[:], start=True, stop=False),
                          reads=[Yt, G2a], writes=[pa], pe_acc=(c + hf_ > 0))
                    kb.op("pe", lambda: nc.tensor.matmul(pa[:, c, hf_ * 256:(hf_ + 1) * 256], R_(Yt[:, c, 256 + hf_ * 128:256 + (hf_ + 1) * 128]), G2b[:], start=False, stop=True),
                          reads=[Yt, G2b], writes=[pa], pe_acc=True)
            yield
            ZpV = Zp_t[:].rearrange("p a w -> p (a w)").rearrange("p (h r c n) -> p h r c n", h=2, r=2, c=GC)
            PAv = pa[:].rearrange("p c (h r n) -> p h r c n", h=2, r=2)
            TR = TcR[:, :].rearrange("p (h o n) -> p h o n", h=2, o=1).to_broadcast([128, 2, GC, 128])
            TI = TcI[:, :].rearrange("p (h o n) -> p h o n", h=2, o=1).to_broadcast([128, 2, GC, 128])
            M1v = m1[:].rearrange("p c (h r n) -> p h r c n", h=2, r=2)
            M2v = m2[:].rearrange("p c (h r n) -> p h r c n", h=2, r=2)
            for r in range(2):
                kb.op("dve", lambda: nc.vector.tensor_tensor(M1v[:, :, r], PAv[:, :, r], TR, ALU.mult), reads=[pa, TcR], writes=[m1])
                kb.op("dve", lambda: nc.vector.tensor_tensor(M2v[:, :, r], PAv[:, :, r], TI, ALU.mult), reads=[pa, TcI], writes=[m2])
            yield
            kb.op("dve", lambda: nc.vector.tensor_tensor(R_(ZpV[:, :, 0]), M1v[:, :, 0], M2v[:, :, 1], ALU.subtract), reads=[m1, m2], writes=[Zp_t])
            kb.op("pool", lambda: nc.gpsimd.tensor_tensor(R_(ZpV[:, :, 1]), M2v[:, :, 0], M1v[:, :, 1], ALU.add), reads=[m1, m2], writes=[Zp_t])
            yield
            i = 0
            for r in range(2):
                for hf_ in range(2):
                    kb.op("pe", lambda: nc.tensor.matmul(pb[:, 0, 0:GC * 128], G1[:, r * 256 + hf_ * 128: r * 256 + (hf_ + 1) * 128],
                                                         R_(ZpV[:, hf_, r].rearrange("p c n -> p (c n)")), start=(i == 0), stop=(i == 3)),
                          reads=[G1, Zp_t], writes=[pb], pe_acc=(i > 0))
                    i += 1
            yield
            yv = pb[:, 0, 0:GC * 128].rearrange("p (c n) -> p c n", n=128)
            tt_ = tt__[s_]; cv = cv_[s_]
            sb_ = sks[:, o, cg * GC:(cg + 1) * GC].rearrange("p (c o) -> p c o", o=1).to_broadcast([128, GC, 128])
            kb.op("pool", lambda: nc.gpsimd.tensor_tensor(tt_[:], src[:, cg * GC:(cg + 1) * GC, :], sb_, ALU.mult), reads=[src_t, sks], writes=[tt_])
            kb.op("dve", lambda: nc.vector.tensor_tensor(tt_[:], tt_[:], yv, ALU.add), reads=[tt_, pb], writes=[tt_])
            yield
            sconv(s_, o + 1, cg, cv, cv[:])
            kb.op("dve", lambda: nc.vector.tensor_tensor(R_(dst[:, cg * GC:(cg + 1) * GC, :]), tt_[:], cv[:], ALU.mult), reads=[tt_, cv], writes=[dst_t])
            yield

        kb._wait("pool", hs_done)
        for o in range(2):
            src_t, src = srcs[o % 2]
            dst_t, dst = srcs[(o + 1) % 2]
            interleave((data_group(o, cg, src_t, src, dst_t, dst) for cg in range(NG)), 2)
        fin_t, fin = srcs[0]
        if hy_out_cb is None:
            for q in range(4):
                kb.dma("sp", o_hy[:, q * 32:(q + 1) * 32, :], fin[:, q * 32:(q + 1) * 32, :], reads=[fin_t])
        else:
            hy_out_cb(kb, fin_t, fin)
    if own:
        kb.finish("sp")
        kb.close()
    return kb, ins


def cd_consts():
    c = dft_consts()
    m = {"c_" + k: v for k, v in c.items()}
    idx = np.arange(128)
    perm = np.zeros((128, 128), np.float32); perm[(idx + 64) % 128, idx] = 1.0
    m["c_perm"] = perm
    m["c_ident"] = np.eye(128, dtype=np.float32)
    pr = np.zeros((128, 2, 256), np.float32)
    pr[:, 0, :128] = idx + 1; pr[:, 0, 128:] = -(idx + 1.0)
    pr[:, 1, :128] = 128 - idx; pr[:, 1, 128:] = -(128.0 - idx)
    m["c_posrow"] = pr
    mk = np.zeros((128, 2, 128), np.float32)
    mk[:, 0, :] = (idx[None, :] >= idx[:, None])
    mk[:, 1, :] = (idx[None, :] <= idx[:, None])
    m["c_maskT"] = mk
    inv = (np.float32(10000.0) ** (-np.linspace(0.0, 1.0, 64, dtype=np.float32))).astype(np.float32)
    pos = np.concatenate([np.arange(256, dtype=np.float32), np.arange(L, dtype=np.float32)])
    ang = (pos[:, None] * inv[None, :]).astype(np.float32).astype(np.float64)
    cos = np.cos(ang).T; sin = np.sin(ang).T
    Cf = np.concatenate([cos, cos], axis=0); Sf = np.concatenate([-sin, sin], axis=0)
    ks = 128.0 ** -0.5
    CT = np.stack([Cf.reshape(128, NCH, 128), Cf.reshape(128, NCH, 128) * ks], axis=2).reshape(128, NCH, 256)
    ST = np.stack([Sf.reshape(128, NCH, 128), Sf.reshape(128, NCH, 128) * ks], axis=2).reshape(128, NCH, 256)
    m["r_CT"] = CT.astype(np.float32); m["r_ST"] = ST.astype(np.float32)
    p = np.arange(L, dtype=np.float32)
    t = (p / np.float32(L - 1)).astype(np.float32)
    bands = np.linspace(1e-4, 7, 8, dtype=np.float32)
    phase = (np.float32(2.0 * np.pi / L) * p[:, None] * bands[None, :]).astype(np.float32).astype(np.float64)
    z = np.concatenate([t[:, None].astype(np.float64), np.cos(phase), -np.sin(phase)], axis=-1)
    m["h_zT"] = np.ascontiguousarray(z.T, dtype=np.float32)
    return m


def cd_dec(h):
    p = np.arange(L, dtype=np.float32)
    t = (p / np.float32(L - 1)).astype(np.float32)
    rates = np.abs(np.linspace(np.log(1e-2) / 1.5, np.log(1e-2) / 0.3, 512, dtype=np.float32))[h * 128:(h + 1) * 128]
    d = np.exp(-(t[:, None] * rates[None, :]).astype(np.float32).astype(np.float64))
    return np.ascontiguousarray(d.reshape(128, 128, 128), dtype=np.float32)


def cd_core_inputs(inputs, pl, pc, b, h, consts, ins):
    m = dict(consts)
    sl = lambda a, o: a[b][:, o + h * 128:o + (h + 1) * 128]
    if pl is not None:
        q = np.concatenate([sl(pc, 0), sl(pl, 0)], axis=0)
        k = np.concatenate([sl(pc, 512), sl(pl, 512)], axis=0)
        qk = np.stack([q.T.reshape(128, NCH, 128), k.T.reshape(128, NCH, 128)], axis=2).reshape(128, NCH, 256)
        m["r_qkT"] = np.ascontiguousarray(qk)
        m["r_v"] = np.ascontiguousarray(np.concatenate([sl(pc, 1024), sl(pl, 1024)], axis=0))
        m["r_g"] = np.ascontiguousarray(sl(pl, 1536))
    m["r_gnw"] = np.ascontiguousarray(np.broadcast_to(inputs["ret_gn_w"][0][h * 128:(h + 1) * 128], (128, 128)))
    m["r_logit"] = np.ascontiguousarray(np.broadcast_to(inputs["ret_decay_logit"][0][:, h], (128, 2)))
    pads = np.zeros((3, 128, 128, 130), np.float32)
    for part in (range(3) if pl is not None else ()):
        a = pl[b][:, 2048 + part * 512 + h * 128: 2048 + part * 512 + (h + 1) * 128]
        ap = np.zeros((L + 2, 128), np.float32); ap[1:L + 1] = a
        i0 = (np.arange(128)[:, None] * 128 + np.arange(130)[None, :])
        pads[part] = ap[i0].transpose(0, 2, 1)
    if pl is not None:
        m["h_pads"] = pads
    cwf = inputs["hy_conv_w"][0]
    cw = np.stack([cwf[:, part * 512 + h * 128: part * 512 + (h + 1) * 128] for part in range(3)], axis=0)
    m["h_cw"] = np.ascontiguousarray(np.broadcast_to(cw, (128, 3, 3, 128)))
    m["h_skip"] = np.ascontiguousarray(np.broadcast_to(inputs["hy_bias"][0][:, h * 128:(h + 1) * 128], (128, 2, 128)))
    m["h_w1"] = inputs["hy_f_w1"][0]; m["h_b1"] = inputs["hy_f_b1"][0].reshape(64, 1)
    m["h_w2"] = inputs["hy_f_w2"][0]; m["h_b2"] = inputs["hy_f_b2"][0].reshape(64, 1)
    w3 = inputs["hy_f_w3"][0].reshape(64, 2, 2, 512)[:, :, :, h * 128:(h + 1) * 128].reshape(64, 4, 128)
    m["h_w3"] = np.ascontiguousarray(w3)
    m["h_dec"] = cd_dec(h)
    return {k: np.ascontiguousarray(m[k], dtype=np.float32) for k in ins if k in m}


def cd_assemble(results, do_ret=True, do_hy=True):
    o = np.zeros((2, L, 1024), np.float32)
    for c, r in enumerate(results):
        b, h = c // 4, c % 4
        if do_ret:
            o[b, :, h * 128:(h + 1) * 128] = r["o_ret"]
        if do_hy:
            o[b, :, 512 + h * 128: 512 + (h + 1) * 128] = r["o_hy"].transpose(0, 2, 1).reshape(L, 128)
    return o


GROUPS = [[0, 1, 2, 3], [4, 5, 6, 7]]
F32R = mybir.dt.float32r


def R_(ap):
    return ap.bitcast(F32R)

HCH = [(0, 64)] + [(64 + 256 * i, 256) for i in range(16)]
OCH = [(0, 256)] + [(256 + 1024 * i, 1024) for i in range(16)]
BLK_ALL = [(0, 64, 1)] + [(64 + i * 512, 512, 0) for i in range(8)]
BLK_LAT = [(64 + i * 512, 512, 0) for i in range(8)]
SEQ = CTX + L


def idram(kb, name, shape):
    return kb.nc.dram_tensor(name, list(shape), F32, kind="Internal").ap()


def dense_phase(kb, P, inputs_list, *, x_src, x_dst, blocks, layer_a, layer_b, og, hx, hg, mos=None):
    nc = kb.nc
    has_a = layer_a is not None
    has_b = layer_b is not None

    def din(name, shape):
        nm = P + name
        inputs_list.append(nm)
        return kb.dram(nm, shape)

    csT = din("csT", [128, 16])
    if has_a:
        w_out = din("w_out", [8, 128, KC * 128])
        nffn = din("nffn", [128, 8])
        f_in = din("f_in", [44, 128, KC * 128]); f_out = din("f_out", [8, 128, 22 * 128])
        selv = din("selv", [128, 4])
    if has_b:
        modw_b = din("modw_b", [48, 128, KC * 128]); modb_b = din("modb_b", [128, 48])
        nmix = din("nmix", [128, 8])

    WT = 11 * 128
    wring = Ring([kb.sb("w", [128, WT]) for i in range(6)])
    wrr = Ring([kb.sb("wr", [128, WT], dt=F32R) for i in range(6)])
    psr = Ring([kb.ps("ps", [128, 512]) for i in range(7)])
    psm = kb.ps("psm", [128, 512])
    xr = Ring([kb.sb("x", [128, KC, 512]) for i in range(1 if has_a else 2)])
    ht = kb.sb("ht", [128, KC, 512])
    sq = kb.sb("sq", [128, KC, 512]) if not has_a else None
    rr = kb.sb("rr", [128, 512])
    ones = kb.sb("ones", [128, 128])
    cs = kb.sb("cs", [128, 16])
    tmpr = Ring([kb.sb("tmp", [128, 512]) for i in range(3)])
    if has_a:
        candr = Ring([kb.sb("cand", [128, KC, 512]) for i in range(2)])
        sq = candr.ts[0]
        candr.i = 1
        actT = kb.sb("actT", [128, 22, 512])

        ot_v = kb.sb("ot", [128, KC, 512])
        sel = kb.sb("sel", [128, 4])
        kb.dma("sp", sel[:], selv[:, :], writes=[sel])
    wqi = [0]

    def wload(src_ap, width, rounded=True):
        w = wring.next()
        q = ("sp", "pool")[wqi[0] % 2]; wqi[0] += 1
        kb.dma(q, w[:, :width], src_ap, writes=[w])
        if not rounded:
            return w
        wr = wrr.next()
        if wqi[0] % 2:
            kb.op("act", lambda: nc.scalar.copy(out=wr[:, :width], in_=w[:, :width]), reads=[w], writes=[wr])
        else:
            kb.op("pool", lambda: nc.gpsimd.tensor_copy(wr[:, :width], w[:, :width]), reads=[w], writes=[wr])
        return wr

    kb.op("dve", lambda: nc.vector.memset(ones[:], 1.0 / D), writes=[ones])
    kb.dma("sp", cs[:], csT[:, :], writes=[cs])
    kb.op("act", lambda: nc.scalar.activation(out=cs[:], in_=cs[:], func=AF.Silu), reads=[cs], writes=[cs])

    def mod_load(layer, name):
        mo = kb.sb(name, [128, 48, 2])
        kb.dma("sp", mo[:], mos[layer][:, :].rearrange("p (n s) -> p n s", s=2), writes=[mo])
        return mo

    def mod_compute(modw, modb, name, layer):
        mb = kb.sb(name + "_b", [128, 48])
        mo = kb.sb(name, [128, 48, 2])
        kb.dma("sp", mb[:], modb[:, :], writes=[mb])
        for n in range(48):
            w = wload(modw[n], KC * 128, rounded=False)
            for k in range(KC):
                kb.op("pe", lambda: nc.tensor.matmul(psm[:, n * 2:n * 2 + 2], w[:, k * 128:(k + 1) * 128], cs[:, k * 2:k * 2 + 2],
                                                     start=(k == 0), stop=(k == KC - 1)),
                      reads=[w, cs], writes=[psm], pe_acc=True)
        for s in range(2):
            kb.op("dve", lambda: nc.vector.tensor_tensor(mo[:, :, s], psm[:, s:96:2], mb[:, :], ALU.add),
                  reads=[psm, mb], writes=[mo])
        kb.dma("sp", mos[layer][:, :].rearrange("p (n s) -> p n s", s=2), mo[:], reads=[mo])
        return mo

    def gs(mo, nw_dram, name, sc):
        nw = kb.sb(name + "_nw", [128, 8])
        G = kb.sb(name + "_G", [128, 8, 2])
        kb.dma("sp", nw[:], nw_dram[:, :], writes=[nw])
        for s in range(2):
            kb.op("dve", lambda: nc.vector.scalar_tensor_tensor(G[:, :, s], mo[:, sc * 8:sc * 8 + 8, s], 1.0, nw[:, :], ALU.add, ALU.mult),
                  reads=[mo, nw], writes=[G])
        return G

    if has_a:
        moA = mod_load(layer_a, "moA")
        G2 = gs(moA, nffn, "g2", 4)
    if has_b:
        moB = mod_compute(modw_b, modb_b, "moB", layer_b)
        G1 = gs(moB, nmix, "g1", 1)

    def norm_mod(xt, N, G, mo, shift_idx, s):
        for k in range(KC):
            kb.op("act", lambda: nc.scalar.activation(out=sq[:, k, :N], in_=xt[:, k, :N], func=AF.Square), reads=[xt], writes=[sq])
        ps = psr.next()
        for k in range(KC):
            kb.op("pe", lambda: nc.tensor.matmul(ps[:, :N], ones[:, :], sq[:, k, :N], start=(k == 0), stop=(k == KC - 1)),
                  reads=[ones, sq], writes=[ps], pe_acc=True)
        kb.op("dve", lambda: nc.vector.tensor_scalar(rr[:, :N], ps[:, :N], EPS, None, ALU.add), reads=[ps], writes=[rr])
        kb.op("act", lambda: nc.scalar.activation(out=rr[:, :N], in_=rr[:, :N], func=AF.Sqrt), reads=[rr], writes=[rr])
        kb.op("dve", lambda: nc.vector.reciprocal(rr[:, :N], rr[:, :N]), reads=[rr], writes=[rr])
        for k in range(KC):
            kb.op("dve", lambda: nc.vector.scalar_tensor_tensor(sq[:, k, :N], xt[:, k, :N], G[:, k, s:s + 1], rr[:, :N], ALU.mult, ALU.mult),
                  reads=[xt, G, rr], writes=[sq])
            kb.op("act", lambda: nc.scalar.activation(out=R_(ht[:, k, :N]), in_=sq[:, k, :N], func=AF.Identity,
                                                      bias=mo[:, shift_idx * 8 + k, s:s + 1], scale=1.0),
                  reads=[sq, mo], writes=[ht])

    xsv = x_src.rearrange("(k p) t -> p k t", p=128)
    xdv = x_dst.rearrange("(k p) t -> p k t", p=128)
    dcol0 = blocks[0][0] if x_dst.shape[1] != NTOK else 0

    for bi, (t0, N, s) in enumerate(blocks):
        xt = xr.next()
        kb.dma("sp", xt[:, :, :N], xsv[:, :, t0:t0 + N], writes=[xt])
        if has_a:
            for rq in range(4):
                cd = candr.next()
                for m_ in range(2):
                    if s == 1:
                        src = og[m_][0][:, rq * 64:(rq + 1) * 64]
                    else:
                        i = (t0 - 64) // 512
                        j = 1 + rq * 4 + i // 2
                        c0 = (i % 2) * 512
                        src = og[m_][j][:, c0:c0 + N]
                    kb.dma("pool" if m_ else "sp", cd[:, m_:KC:2, :N], src.rearrange("(k p) t -> p k t", p=128), writes=[cd])
                if rq == 0:
                    kb.op("dve", lambda: nc.vector.tensor_scalar(R_(ot_v[:, :, :N]), cd[:, :, :N], sel[:, 0:1], None, ALU.mult), reads=[cd, sel], writes=[ot_v])
                else:
                    kb.op("dve", lambda: nc.vector.scalar_tensor_tensor(R_(ot_v[:, :, :N]), cd[:, :, :N], sel[:, rq:rq + 1], ot_v[:, :, :N], ALU.mult, ALU.add),
                          reads=[cd, sel, ot_v], writes=[ot_v])
            for m in range(8):
                w = wload(w_out[m], KC * 128)
                ps = psr.next()
                for k in range(KC):
                    kb.op("pe", lambda: nc.tensor.matmul(ps[:, :N], w[:, k * 128:(k + 1) * 128], R_(ot_v[:, k, :N]), start=(k == 0), stop=(k == KC - 1)),
                          reads=[w, ot_v], writes=[ps], pe_acc=True)
                kb.op("dve", lambda: nc.vector.scalar_tensor_tensor(xt[:, m, :N], ps[:, :N], moA[:, 16 + m, s:s + 1], xt[:, m, :N], ALU.mult, ALU.add),
                      reads=[ps, moA, xt], writes=[xt])
            norm_mod(xt, N, G2, moA, 3, s)
            for j in range(22):
                wg = wload(f_in[j], KC * 128)
                wu = wload(f_in[22 + j], KC * 128)
                pg = psr.next(); pu = psr.next()
                for k in range(KC):
                    kb.op("pe", lambda: nc.tensor.matmul(pg[:, :N], wg[:, k * 128:(k + 1) * 128], R_(ht[:, k, :N]), start=(k == 0), stop=(k == KC - 1)),
                          reads=[wg, ht], writes=[pg], pe_acc=True)
                for k in range(KC):
                    kb.op("pe", lambda: nc.tensor.matmul(pu[:, :N], wu[:, k * 128:(k + 1) * 128], R_(ht[:, k, :N]), start=(k == 0), stop=(k == KC - 1)),
                          reads=[wu, ht], writes=[pu], pe_acc=True)
                tg = tmpr.next()
                kb.op("act", lambda: nc.scalar.activation(out=tg[:, :N], in_=pg[:, :N], func=AF.Silu), reads=[pg], writes=[tg])
                kb.op("dve", lambda: nc.vector.tensor_tensor(R_(actT[:, j, :N]), tg[:, :N], pu[:, :N], ALU.mult), reads=[tg, pu], writes=[actT])
            for m in range(8):
                wa = wload(f_out[m][:, 0:WT], WT)
                wb = wload(f_out[m][:, WT:2 * WT], WT)
                ps = psr.next()
                for j in range(22):
                    w = wa if j < 11 else wb
                    jj = j % 11
                    kb.op("pe", lambda: nc.tensor.matmul(ps[:, :N], w[:, jj * 128:(jj + 1) * 128], R_(actT[:, j, :N]), start=(j == 0), stop=(j == 21)),
                          reads=[w, actT], writes=[ps], pe_acc=True)
                kb.op("dve", lambda: nc.vector.scalar_tensor_tensor(xt[:, m, :N], ps[:, :N], moA[:, 40 + m, s:s + 1], xt[:, m, :N], ALU.mult, ALU.add),
                      reads=[ps, moA, xt], writes=[xt])
            kb.dma("sp", xdv[:, :, t0 - dcol0:t0 - dcol0 + N], xt[:, :, :N], reads=[xt])
        if has_b:
            norm_mod(xt, N, G1, moB, 0, s)
            for ji, (c0, ncol) in enumerate(HCH):
                if c0 < t0 or c0 >= t0 + N:
                    continue
                d = kb.dma("sp", hx[ji].rearrange("p (k t) -> p k t", k=KC), ht[:, :, c0 - t0:c0 - t0 + ncol], reads=[ht])
                kb.allgather(hx[ji], hg[ji], [d], GROUPS)


def fe_phase(kb, P, inputs_list, *, hg, nfm, ntm, fm_dst, tm_dst):
    nc = kb.nc
    nmW = P + "wfm"; nmT = P + "wtm"
    inputs_list += [nmW, nmT]
    wfm_d = kb.dram(nmW, [128, KC, nfm * 128]); wtm_d = kb.dram(nmT, [128, KC, ntm])
    wfm = kb.sb("wfm", [128, KC, nfm * 128], dt=F32R); wtm = kb.sb("wtm", [128, KC, ntm], dt=F32R)
    wtm32 = kb.sb("wtm32", [128, KC, ntm])
    for k in range(KC):
        stg = kb.sb("wstg", [128, nfm * 128]) if k == 0 else stg
        kb.dma("sp", stg[:], wfm_d[:, k, :], writes=[stg])
        kb.op("act", lambda: nc.scalar.copy(out=wfm[:, k, :], in_=stg[:]), reads=[stg], writes=[wfm])
    kb.dma("pool", wtm32[:], wtm_d[:, :, :], writes=[wtm32])
    kb.op("dve", lambda: nc.vector.tensor_copy(wtm[:], wtm32[:]), reads=[wtm32], writes=[wtm])
    WD = 3
    hur = Ring([kb.sb("hu", [128, KC, 256], dt=F32R) for i in range(WD + 1)])
    hu32r = Ring([kb.sb("hu32", [128, KC, 256]) for i in range(WD + 1)])
    psr = Ring([kb.ps("ps", [128, 512]) for i in range(8)])
    fmr = Ring([kb.sb("fmo", [128, 256]) for i in range(6)])
    tmr = Ring([kb.sb("tmo", [128, ntm]) for i in range(4)])

    def unit(ji, c0, ncol, rq):
        kb._wait("sp", [("cc", kb.cc_base + ji + 1)]); kb._wait("pool", [("cc", kb.cc_base + ji + 1)])
        ts0 = rq * 64 if ji == 0 else CTX + rq * 4096 + (c0 - 64)
        hu = hur.next(); h32 = hu32r.next()
        kb.dma("sp" if rq % 2 == 0 else "pool", h32[:, :, :ncol], hg[ji][rq * 128:(rq + 1) * 128, :].rearrange("p (k t) -> p k t", k=KC), writes=[h32])
        if rq % 2 == 0:
            kb.op("pool", lambda: nc.gpsimd.tensor_copy(hu[:, :, :ncol], h32[:, :, :ncol]), reads=[h32], writes=[hu])
        else:
            kb.op("dve", lambda: nc.vector.tensor_copy(hu[:, :, :ncol], h32[:, :, :ncol]), reads=[h32], writes=[hu])
        yield
        for f in range(nfm):
            ps = psr.next()
            for k in range(KC):
                kb.op("pe", lambda: nc.tensor.matmul(ps[:, :ncol], wfm[:, k, f * 128:(f + 1) * 128], hu[:, k, :ncol], start=(k == 0), stop=(k == KC - 1)),
                      reads=[wfm, hu], writes=[ps], pe_acc=True)
            yield
            fo = fmr.next()
            kb.op("act" if f % 2 == 0 else "dve",
                  (lambda: nc.scalar.copy(out=fo[:, :ncol], in_=ps[:, :ncol])) if f % 2 == 0 else (lambda: nc.vector.tensor_copy(fo[:, :ncol], ps[:, :ncol])),
                  reads=[ps], writes=[fo])
            fm_dst(f, ts0, ncol, fo, fo[:, :ncol])
        for sub in range(0, ncol, 128):
            m = min(128, ncol - sub)
            ps = psr.next()
            for k in range(KC):
                kb.op("pe", lambda: nc.tensor.matmul(ps[:m, :ntm], hu[:, k, sub:sub + m] if m == 128 else h32[:, k, sub:sub + m], wtm[:, k, :] if m == 128 else wtm32[:, k, :], start=(k == 0), stop=(k == KC - 1)),
                      reads=[wtm, hu, wtm32, h32], writes=[ps], pe_acc=True)
            yield
            to = tmr.next()
            kb.op("act", lambda: nc.scalar.copy(out=to[:m, :], in_=ps[:m, :ntm]), reads=[ps], writes=[to])
            tm_dst(ts0 + sub, m, to)
    interleave((unit(ji, c0, ncol, rq) for ji, (c0, ncol) in enumerate(HCH) for rq in range(4)), WD)


def build_fused():
    kb = KB()
    nc = kb.nc
    ins = []
    pre = {}
    kb.cc_base = 0
    xT = kb.dram("xT", [D, NTOK]); ins.append("xT")
    xo = kb.dram("xo", [D, 4096], kind="ExternalOutput")
    x1s = idram(kb, "x1s", [D, NTOK])
    hx = [[idram(kb, "hx%d_%d" % (l, j), [128, KC * n]) for j, (c0, n) in enumerate(HCH)] for l in range(2)]
    hg = [[idram(kb, "hg%d_%d" % (l, j), [4 * 128, KC * n]) for j, (c0, n) in enumerate(HCH)] for l in range(2)]
    ox = [[[idram(kb, "ox%d_%d_%d" % (l, m, j), [128, n]) for j, (c0, n) in enumerate(OCH)] for m in range(2)] for l in range(2)]
    og = [[[idram(kb, "og%d_%d_%d" % (l, m, j), [4 * 128, n]) for j, (c0, n) in enumerate(OCH)] for m in range(2)] for l in range(2)]
    zt = None
    mos = [idram(kb, "mos%d" % l, [128, 96]) for l in range(2)]

    def phase_end(mark):
        kb.barrier()
        kb.release(mark)

    def och_of(ts):
        if ts < CTX:
            return 0, ts
        tl = ts - CTX
        return 1 + tl // 1024, tl % 1024

    def make_out_cb(layer, mrow):
        pend = {}

        def cb(kb_, row0, tile, ident, ps_t, ps_ap):
            j, col = och_of(row0)
            kb_.op("pe", lambda: nc.tensor.matmul(ps_ap, tile[:], ident[:], start=True, stop=True), reads=[tile, ident], writes=[ps_t])
            tr = cb.ring.next()
            kb_.op("act", lambda: nc.scalar.copy(out=tr[:], in_=ps_ap), reads=[ps_t], writes=[tr])
            d = kb_.dma("sp", ox[layer][mrow][j][:, col:col + 128], tr[:], reads=[tr])
            pend.setdefault(j, []).append(d)
            if len(pend[j]) == OCH[j][1] // 128:
                kb_.allgather(ox[layer][mrow][j], og[layer][mrow][j], pend[j], GROUPS)
        return cb

    mk = kb.mark()
    dense_phase(kb, "D0_", ins, x_src=xT, x_dst=xT, blocks=BLK_ALL, layer_a=None, layer_b=0, og=None, hx=hx[0], hg=hg[0], mos=mos)
    phase_end(mk)
    PADW = CTX + 2 + L + 2
    S0 = {"d_qpad": idram(kb, "d_qpad", [128, PADW]), "d_kpad": idram(kb, "d_kpad", [128, PADW]), "d_vpad": idram(kb, "d_vpad", [PADW, 128]),
          "d_gate": idram(kb, "d_gate", [SEQ, 128]), "d_ab": idram(kb, "d_ab", [64, NCK, 4]),
          "s_qT": idram(kb, "s_qT", [128, SEQ]), "s_kT": idram(kb, "s_kT", [128, SEQ]), "s_v": idram(kb, "s_v", [SEQ, 128])}
    mk = kb.mark()
    zt = kb.sb("zeros", [128, 128])
    kb.op("dve", lambda: nc.vector.memset(zt[:], 0.0), writes=[zt])
    for c in (0, CTX + 1, CTX + 2, PADW - 1):
        kb.dma("sp", S0["d_qpad"][:, c:c + 1], zt[:, 0:1], reads=[zt], allow_slow_non_contiguous=True)
        kb.dma("sp", S0["d_kpad"][:, c:c + 1], zt[:, 0:1], reads=[zt], allow_slow_non_contiguous=True)
        kb.dma("sp", S0["d_vpad"][c:c + 1, :], zt[0:1, :], reads=[zt])

    def padcol(ts):
        return 1 + ts if ts < CTX else 3 + ts

    def fm0(f, ts0, n, t, ap):
        if f == 0:
            kb.dma("sp", S0["d_qpad"][:, padcol(ts0):padcol(ts0) + n], ap, reads=[t])
        elif f == 1:
            kb.dma("pool", S0["d_kpad"][:, padcol(ts0):padcol(ts0) + n], ap, reads=[t])
        elif f == 2:
            kb.dma("sp", S0["s_qT"][:, ts0:ts0 + n], ap, reads=[t])
        else:
            kb.dma("pool", S0["s_kT"][:, ts0:ts0 + n], ap, reads=[t])

    def tm0(ts0, m, t):
        kb.dma("sp", S0["d_vpad"][padcol(ts0):padcol(ts0) + m, :], t[:m, 0:128], reads=[t])
        kb.dma("pool", S0["d_gate"][ts0:ts0 + m, :], t[:m, 128:256], reads=[t])
        kb.dma("sp", S0["s_v"][ts0:ts0 + m, :], t[:m, 256:384], reads=[t])
        for cc in range(m // 64):
            kb.dma("pool", S0["d_ab"][:, ts0 // 64 + cc, :], t[cc * 64:(cc + 1) * 64, 384:388], reads=[t])

    fe_phase(kb, "F0_", ins, hg=hg[0], nfm=4, ntm=388, fm_dst=fm0, tm_dst=tm0)
    phase_end(mk)
    kb.cc_base = kb.cnt["cc"]
    mk = kb.mark()
    cb = make_out_cb(0, 0); cb.ring = Ring([kb.sb("otr", [128, 128]) for i in range(2)])
    pre.update(S0)
    _, i2 = build_ab(True, False, kb=kb, pre=pre, out_cb=cb); ins += i2
    phase_end(mk)
    mk = kb.mark()
    cb = make_out_cb(0, 1); cb.ring = Ring([kb.sb("otr", [128, 128]) for i in range(2)])
    _, i2 = build_swa(kb=kb, pre=pre, out_cb=cb); ins += i2
    phase_end(mk)
    cc_o0 = kb.cnt["cc"]
    mk = kb.mark()
    kb._wait("sp", [("cc", cc_o0)]); kb._wait("pool", [("cc", cc_o0)])
    kb.cc_base = kb.cnt["cc"]
    dense_phase(kb, "D1_", ins, x_src=xT, x_dst=x1s, blocks=BLK_ALL, layer_a=0, layer_b=1, og=og[0], hx=hx[1], hg=hg[1], mos=mos)
    phase_end(mk)
    S1 = {"r_qkT": idram(kb, "r_qkT", [128, NCH, 256]), "r_v": idram(kb, "r_v", [NCH * 128, 128]), "r_g": idram(kb, "r_g", [L, 128])}
    hp = idram(kb, "h_hp", [3, 128, L + 2])
    mk = kb.mark()
    zt = kb.sb("zeros", [128, 128])
    kb.op("dve", lambda: nc.vector.memset(zt[:], 0.0), writes=[zt])
    for part in range(3):
        for c in (0, L + 1):
            kb.dma("sp", hp[part, :, c:c + 1], zt[:, 0:1], reads=[zt], allow_slow_non_contiguous=True)

    def fm1(f, ts0, n, t, ap):
        if f < 2:
            ci0 = ts0 // 128
            if n >= 128:
                kb.dma("sp" if f == 0 else "pool", S1["r_qkT"][:, ci0:ci0 + n // 128, f * 128:(f + 1) * 128], ap.rearrange("p (c i) -> p c i", i=128), reads=[t])
            else:
                kb.dma("sp", S1["r_qkT"][:, ci0, f * 128 + ts0 % 128: f * 128 + ts0 % 128 + n], ap, reads=[t])
        elif ts0 >= CTX:
            tl = ts0 - CTX
            kb.dma("sp" if f % 2 == 0 else "pool", hp[f - 2, :, 1 + tl:1 + tl + n], ap, reads=[t])

    def tm1(ts0, m, t):
        kb.dma("sp", S1["r_v"][ts0:ts0 + m, :], t[:m, 0:128], reads=[t])
        if ts0 >= CTX:
            kb.dma("pool", S1["r_g"][ts0 - CTX:ts0 - CTX + m, :], t[:m, 128:256], reads=[t])

    fe_phase(kb, "F1_", ins, hg=hg[1], nfm=5, ntm=256, fm_dst=fm1, tm_dst=tm1)
    phase_end(mk)
    pre.update(S1)
    mk = kb.mark()

    def pad_src(part, c):
        base = hp[part, c, 0:130]
        return bass.AP(base.tensor, base.offset, [[128, 128], [1, 130]])

    def hy_out(kb_, fin_t, fin):
        for j in range(1, 17):
            n0 = 8 * (j - 1)
            d = kb_.dma("sp" if j % 2 else "pool", ox[1][1][j][:, :].rearrange("c (a n) -> a c n", n=128), fin[n0:n0 + 8, :, :], reads=[fin_t])
            kb_.allgather(ox[1][1][j], og[1][1][j], [d], GROUPS)
    _, i2 = build_cd(False, True, kb=kb, pre=pre, hy_out_cb=hy_out, pad_src=pad_src); ins += i2
    phase_end(mk)
    mk = kb.mark()

    def ret_cb(kb_, row0, tile, ident, ps_t, ps_ap):
        make_out_cb_l1(kb_, row0 + CTX, tile, ident, ps_t, ps_ap)
    make_out_cb_l1 = make_out_cb(1, 0); make_out_cb_l1.ring = Ring([kb.sb("otr", [128, 128]) for i in range(3)])
    _, i2 = build_cd(True, False, kb=kb, pre=pre, out_cb=ret_cb); ins += i2
    phase_end(mk)
    cc_o1 = kb.cnt["cc"]
    mk = kb.mark()
    kb._wait("sp", [("cc", cc_o1)]); kb._wait("pool", [("cc", cc_o1)])
    dense_phase(kb, "D2_", ins, x_src=x1s, x_dst=xo, blocks=BLK_LAT, layer_a=1, layer_b=None, og=og[1], hx=None, hg=None, mos=mos)
    phase_end(mk)
    kb.finish("sp")
    kb.close()
    seen = set(); out = []
    for n in ins:
        if n not in seen:
            seen.add(n); out.append(n)
    return kb, out


def _sel_w(W, cols):
    Ws = W[:, cols]
    return np.ascontiguousarray(Ws.reshape(8, 128, len(cols)).transpose(1, 0, 2))


def _dense_inputs(inputs, P, layer_a, layer_b, b, r):
    m = {P + "csT": cs_core(inputs, b)}
    if layer_a is not None:
        m[P + "nffn"] = col8(inputs["norm_ffn_w"][layer_a])
        m[P + "f_in"] = tile_w(inputs["ffn_w_in"][layer_a], KC)
        m[P + "f_out"] = tile_w(inputs["ffn_w_out"][layer_a], 22)
        wo = (inputs["ab_w_out"] if layer_a == 0 else inputs["cd_w_out"])[0]
        rows = np.concatenate([np.concatenate([np.arange(q * 128, (q + 1) * 128), np.arange(512 + q * 128, 512 + (q + 1) * 128)]) for q in range(4)])
        m[P + "w_out"] = tile_w(wo[rows, :], KC)
        sel = np.zeros((128, 4), np.float32); sel[:, r] = 1.0
        m[P + "selv"] = sel
    if layer_b is not None:
        m[P + "modw_b"] = tile_w(inputs["mod_w"][layer_b], KC)
        m[P + "modb_b"] = col8(inputs["mod_b"][layer_b])
        m[P + "nmix"] = col8(inputs["norm_mix_w"][layer_b])
    return m


_FUSED = {}


def kernel(**inputs):
    inputs = {k: np.ascontiguousarray(np.asarray(v), dtype=np.float32) for k, v in inputs.items()}
    x, ctx = inputs["x"], inputs["ctx"]
    if "kb" not in _FUSED:
        _FUSED["kb"], _FUSED["ins"] = build_fused()
    kb, ins = _FUSED["kb"], _FUSED["ins"]
    cA = ab_consts(); cS = swa_consts(); cC = cd_consts()
    shared = {}
    for P, la, lb in (("D0_", None, 0), ("D1_", 0, 1), ("D2_", 1, None)):
        shared[(P, 0)] = None
    maps = []
    dense_cache = {}
    for c in range(8):
        b, h = c // 4, c % 4
        g = h // 2
        m = {"xT": shard_tokT(x, ctx, c)}
        for P, la, lb in (("D0_", None, 0), ("D1_", 0, 1), ("D2_", 1, None)):
            key = (P, b, h)
            dm = _dense_inputs(inputs, P, la, lb, b, h)
            for k_, v_ in dm.items():
                ck = (k_, b if k_.endswith("csT") else -1, h if k_.endswith("selv") else -1)
                if ck not in dense_cache:
                    dense_cache[ck] = v_
                m[k_] = dense_cache[ck]
        W0 = inputs["ab_w_in"][0]
        fm0 = np.concatenate([np.arange(h * 128, (h + 1) * 128), np.arange(512 + h * 128, 512 + (h + 1) * 128),
                              np.arange(2064 + h * 128, 2064 + (h + 1) * 128), np.arange(2064 + 512 + g * 128, 2064 + 512 + (g + 1) * 128)])
        tm0 = np.concatenate([np.arange(1024 + h * 128, 1024 + (h + 1) * 128), np.arange(1536 + h * 128, 1536 + (h + 1) * 128),
                              np.arange(2064 + 768 + g * 128, 2064 + 768 + (g + 1) * 128),
                              np.array([2048 + kind * 8 + d * 4 + h for kind in range(2) for d in range(2)])])
        m["F0_wfm"] = _sel_w(W0, fm0); m["F0_wtm"] = _sel_w(W0, tm0)
        W1 = inputs["cd_w_in"][0]
        fm1 = np.concatenate([np.arange(o + h * 128, o + (h + 1) * 128) for o in (0, 512, 2048, 2560, 3072)])
        tm1 = np.concatenate([np.arange(o + h * 128, o + (h + 1) * 128) for o in (1024, 1536)])
        m["F1_wfm"] = _sel_w(W1, fm1); m["F1_wtm"] = _sel_w(W1, tm1)
        m.update(ab_core_inputs(inputs, None, None, b, h, cA, ins))
        m.update(swa_core_inputs(inputs, None, None, b, h, cS, ins))
        m.update(cd_core_inputs(inputs, None, None, b, h, cC, ins))
        missing = [k for k in ins if k not in m]
        assert not missing, missing
        maps.append({k: m[k] for k in ins})
    res = run_bass_kernel_spmd(kb.nc, maps, core_ids=list(range(8)))
    out = np.zeros((2, L, 1024), np.float32)
    for c, q in enumerate(res.results):
        b, r = c // 4, c % 4
        out[b, r * 4096:(r + 1) * 4096] = q["xo"].T
    return out
```
